# Optimizing a Trainium2 kernel written in Bass

```python
import math
import jax, jax.numpy as jnp
from jax import lax
import numpy as np

D_MODEL = 1024
BATCH = 4
SEQ = 4096
DEPTH = 1
DEC_BATCH = 32
DEC_SEQ = 1
PAST_LEN = 8192
PAGE_SIZE = 128

SSM_W = D_MODEL // 2
SSM_GROUP = 16
SSM_GROUPS = SSM_W // SSM_GROUP
SSM_STATE = 64
HEAD_DIM = 64
N_HEADS = (D_MODEL - SSM_W) // HEAD_DIM
KV_HEADS = 2
Q_PER_KV = N_HEADS // KV_HEADS
NSA_W = N_HEADS * HEAD_DIM
MIX_W = SSM_W + NSA_W
KV_W = KV_HEADS * HEAD_DIM
CMP_BLOCK = 32
CMP_STRIDE = 16
CMP_HID = 2 * HEAD_DIM
SLC_BLOCK = 64
TOP_N = 16
N_LOCAL_BLOCKS = 2
WINDOW = 512
Q_BLOCK = 128
ROPE_THETA = 500000.0
ROPE_DIM = HEAD_DIM // 4
RMS_EPS = 1e-6
NEG_INF = -1e30
IN_W = 2 * SSM_W + 2 * NSA_W + 6 * KV_W + 3 * N_HEADS
IN_SPLITS = (SSM_W, 2 * SSM_W, 2 * SSM_W + NSA_W, 2 * SSM_W + 2 * NSA_W, 2 * SSM_W + 2 * NSA_W + 6 * KV_W)

kernel_name = 'hymba_s5_nsa_decode_step'


def rms_norm(x, w):
    xf = x.astype(jnp.float32)
    y = xf * lax.rsqrt(jnp.mean(xf * xf, axis=-1, keepdims=True) + RMS_EPS)
    return (y * w.astype(jnp.float32)).astype(x.dtype)


def rope(x, pos):
    half = ROPE_DIM // 2
    inv = ROPE_THETA ** (-jnp.arange(half, dtype=jnp.float32) / half)
    ang = pos.astype(jnp.float32)[:, None] * inv
    cos = jnp.cos(ang)[:, None, :]
    sin = jnp.sin(ang)[:, None, :]
    xf = x.astype(jnp.float32)
    x1, x2, rest = xf[..., :half], xf[..., half:ROPE_DIM], xf[..., ROPE_DIM:]
    out = jnp.concatenate([x1 * cos - x2 * sin, x2 * cos + x1 * sin, rest], axis=-1)
    return out.astype(x.dtype)


def mixer_inputs(x, pos, norm_w, w_in, gate_b, q_norm_w, k_norm_w):
    B, S, _ = x.shape
    h = rms_norm(x, norm_w)
    z = jnp.einsum('bsd,de->bse', h, w_in)
    u, g_ssm, q, g_nsa, kv, gl = jnp.split(z, IN_SPLITS, axis=-1)
    q = rope(rms_norm(q.reshape(B, S, N_HEADS, HEAD_DIM), q_norm_w), pos)
    q = q.reshape(B, S, KV_HEADS, Q_PER_KV, HEAD_DIM)
    kv = kv.reshape(B, S, 6, KV_HEADS, HEAD_DIM)
    ks = rms_norm(kv[:, :, 0::2], k_norm_w[:, None, :])
    ks = rope(ks.reshape(B, S, 3 * KV_HEADS, HEAD_DIM), pos).reshape(B, S, 3, KV_HEADS, HEAD_DIM)
    vs = kv[:, :, 1::2]
    kv_rows = jnp.stack([ks[:, :, 0], vs[:, :, 0], ks[:, :, 1], vs[:, :, 1]], axis=2)
    win_rows = jnp.stack([ks[:, :, 2], vs[:, :, 2]], axis=2)
    gates = jax.nn.sigmoid(gl.reshape(B, S, N_HEADS, 3) + gate_b).reshape(B, S, KV_HEADS, Q_PER_KV, 3)
    return u, g_ssm, q, g_nsa, kv_rows, win_rows, gates


def s5_scan(u, h0_re, h0_im, lam_re, lam_im, log_step, b_re, b_im, c_re, c_im, d_skip):
    Bsz, S, _ = u.shape
    f32 = jnp.float32
    uf = u.astype(f32).reshape(Bsz, S, SSM_GROUPS, SSM_GROUP)
    dt = jnp.exp(log_step.astype(f32))[:, None]
    lr, li = lam_re.astype(f32), lam_im.astype(f32)
    mag, ang = jnp.exp(lr * dt), li * dt
    a_re, a_im = mag * jnp.cos(ang), mag * jnp.sin(ang)
    den = lr * lr + li * li
    nr, ni = a_re - 1.0, a_im
    f_re, f_im = (nr * lr + ni * li) / den, (ni * lr - nr * li) / den
    br, bi = b_re.astype(f32), b_im.astype(f32)
    bb_re = f_re[..., None] * br - f_im[..., None] * bi
    bb_im = f_re[..., None] * bi + f_im[..., None] * br
    x_re = jnp.einsum('bsgc,gnc->bsgn', uf, bb_re)
    x_im = jnp.einsum('bsgc,gnc->bsgn', uf, bb_im)
    x_re = x_re.at[:, 0].add(a_re * h0_re - a_im * h0_im)
    x_im = x_im.at[:, 0].add(a_re * h0_im + a_im * h0_re)
    A_re = jnp.broadcast_to(a_re, x_re.shape)
    A_im = jnp.broadcast_to(a_im, x_im.shape)

    def combine(l, r):
        ar1, ai1, xr1, xi1 = l
        ar2, ai2, xr2, xi2 = r
        return (ar2 * ar1 - ai2 * ai1, ar2 * ai1 + ai2 * ar1,
                ar2 * xr1 - ai2 * xi1 + xr2, ar2 * xi1 + ai2 * xr1 + xi2)

    _, _, h_re, h_im = lax.associative_scan(combine, (A_re, A_im, x_re, x_im), axis=1)
    y = (jnp.einsum('gcn,bsgn->bsgc', c_re.astype(f32), h_re)
         - jnp.einsum('gcn,bsgn->bsgc', c_im.astype(f32), h_im)
         + d_skip.astype(f32).reshape(SSM_GROUPS, SSM_GROUP) * uf)
    return y.reshape(Bsz, S, SSM_W).astype(u.dtype), h_re[:, -1], h_im[:, -1]


def compress(rows, pe, w1, b1, w2):
    B, L, G, HD = rows.shape
    ch = rows.reshape(B, L // CMP_STRIDE, CMP_STRIDE, G, HD)
    pe = pe.reshape(2, CMP_STRIDE, 1, HD)
    w1 = w1.reshape(2, CMP_STRIDE, HD, CMP_HID)
    pa = jnp.einsum('bnjgd,jdh->bngh', ch + pe[0], w1[0])
    pb = jnp.einsum('bnjgd,jdh->bngh', ch + pe[1], w1[1])
    hid = jax.nn.gelu(pa[:, :-1] + pb[:, 1:] + b1)
    return jnp.einsum('bngh,hd->bngd', hid, w2)


def nsa_context(kv_rows, cmp_pe, cmp_w1, cmp_b1, cmp_w2):
    B, L = kv_rows.shape[:2]
    Lp = -(-L // SLC_BLOCK) * SLC_BLOCK
    kv = jnp.pad(kv_rows, ((0, 0), (0, Lp - L), (0, 0), (0, 0), (0, 0)))
    kc = compress(kv[:, :, 0], cmp_pe[0], cmp_w1[0], cmp_b1[0], cmp_w2[0])
    vc = compress(kv[:, :, 1], cmp_pe[1], cmp_w1[1], cmp_b1[1], cmp_w2[1])
    sl = kv[:, :, 2:4].reshape(B, Lp // SLC_BLOCK, SLC_BLOCK, 2, KV_HEADS, HEAD_DIM)
    sl = sl.transpose(3, 0, 4, 1, 2, 5)
    return kc, vc, sl[0], sl[1]


def nsa_query_block(q, qpos, kc, vc, ksb, vsb, kw, vw, kwpos, gates):
    f32 = jnp.float32
    B, Q = q.shape[:2]
    C, NB = kc.shape[1], ksb.shape[2]
    scale = HEAD_DIM ** -0.5
    c_start = jnp.arange(C) * CMP_STRIDE
    cmask = (c_start + CMP_BLOCK - 1)[None, :] <= qpos[:, None]
    s = jnp.einsum('bqgrd,bcgd->bqgrc', q, kc, preferred_element_type=f32) * scale
    p_cmp = jax.nn.softmax(jnp.where(cmask[None, :, None, None, :], s, NEG_INF), axis=-1)
    p_cmp = jnp.where(jnp.any(cmask, axis=-1)[None, :, None, None, None], p_cmp, 0.0)
    o_cmp = jnp.einsum('bqgrc,bcgd->bqgrd', p_cmp.astype(vc.dtype), vc)
    blk = jnp.arange(NB)
    overlap = ((c_start[:, None] < (blk[None, :] + 1) * SLC_BLOCK)
               & (c_start[:, None] + CMP_BLOCK > blk[None, :] * SLC_BLOCK)).astype(f32)
    imp = jnp.einsum('bqgrc,cn->bqgn', p_cmp, overlap)
    q_blk = qpos // SLC_BLOCK
    causal = blk[None, :] <= q_blk[:, None]
    forced = (blk[None, :] == 0) | (blk[None, :] >= q_blk[:, None] - (N_LOCAL_BLOCKS - 1))
    imp = jnp.where(causal[None, :, None, :], jnp.where(forced[None, :, None, :], jnp.inf, imp), -jnp.inf)
    _, idx = lax.top_k(imp, min(TOP_N, NB))
    bi = jnp.arange(B)[:, None, None, None]
    gi = jnp.arange(KV_HEADS)[None, None, :, None]
    kg = ksb[bi, gi, idx]
    vg = vsb[bi, gi, idx]
    kpos = idx[..., None] * SLC_BLOCK + jnp.arange(SLC_BLOCK)
    smask = kpos <= qpos[None, :, None, None, None]
    s = jnp.einsum('bqgrd,bqgkjd->bqgrkj', q, kg, preferred_element_type=f32) * scale
    s = jnp.where(smask[:, :, :, None], s, NEG_INF)
    p = jax.nn.softmax(s.reshape(s.shape[:4] + (-1,)), axis=-1).reshape(s.shape)
    o_slc = jnp.einsum('bqgrkj,bqgkjd->bqgrd', p.astype(vg.dtype), vg)
    wmask = ((kwpos[None, :] <= qpos[:, None]) & (kwpos[None, :] > qpos[:, None] - WINDOW)
             & (kwpos[None, :] >= 0))
    s = jnp.einsum('bqgrd,bwgd->bqgrw', q, kw, preferred_element_type=f32) * scale
    p = jax.nn.softmax(jnp.where(wmask[None, :, None, None, :], s, NEG_INF), axis=-1)
    o_win = jnp.einsum('bqgrw,bwgd->bqgrd', p.astype(vw.dtype), vw)
    return gates[..., 0:1] * o_cmp + gates[..., 1:2] * o_slc + gates[..., 2:3] * o_win


def nsa_prompt(q, kv_rows, win_rows, gates, cmp_pe, cmp_w1, cmp_b1, cmp_w2):
    B, S = q.shape[:2]
    kc, vc, ksb, vsb = nsa_context(kv_rows, cmp_pe, cmp_w1, cmp_b1, cmp_w2)
    wpad = jnp.pad(win_rows, ((0, 0), (WINDOW, 0), (0, 0), (0, 0), (0, 0)))
    nqb = S // Q_BLOCK
    qb = q.reshape(B, nqb, Q_BLOCK, KV_HEADS, Q_PER_KV, HEAD_DIM).swapaxes(0, 1)
    gb = gates.reshape(B, nqb, Q_BLOCK, KV_HEADS, Q_PER_KV, 3).swapaxes(0, 1)

    def block(args):
        q_i, g_i, s0 = args
        qpos = s0 + jnp.arange(Q_BLOCK)
        w = lax.dynamic_slice_in_dim(wpad, s0, WINDOW + Q_BLOCK, axis=1)
        kwpos = s0 - WINDOW + jnp.arange(WINDOW + Q_BLOCK)
        return nsa_query_block(q_i, qpos, kc, vc, ksb, vsb, w[:, :, 0], w[:, :, 1], kwpos, g_i)

    o = lax.map(block, (qb, gb, jnp.arange(nqb) * Q_BLOCK))
    return o.swapaxes(0, 1).reshape(B, S, NSA_W)


def mixer_output(x, y_ssm, g_ssm, o_nsa, g_nsa, w_glu, w_out):
    a, b = jnp.split(jnp.einsum('bsc,ce->bse', y_ssm, w_glu), 2, axis=-1)
    ssm_out = a * jax.nn.sigmoid(b) * jax.nn.silu(g_ssm)
    nsa_out = o_nsa * jax.nn.silu(g_nsa)
    mix = jnp.concatenate([ssm_out, nsa_out], axis=-1)
    return x + jnp.einsum('bsc,cd->bsd', mix, w_out)


def setup_inputs(seed: int = 0) -> dict:
    key = jax.random.key(seed)
    ks = jax.random.split(key, 32)
    f32 = jnp.float32
    n_pages = PAST_LEN // PAGE_SIZE
    n_used = DEC_BATCH * n_pages
    n_phys = n_used + max(1, n_used // 4)
    win_buf = min(WINDOW, PAST_LEN)

    def nrm(k, shape, s=1.0):
        return s * jax.random.normal(k, shape, f32)

    n_idx = jnp.arange(SSM_STATE, dtype=f32)
    return {
        'x_prompt': nrm(ks[0], (BATCH, SEQ, D_MODEL)),
        'x_sample': nrm(ks[1], (DEC_BATCH, DEC_SEQ, D_MODEL)),
        'cache_kv': nrm(ks[2], (DEPTH, n_phys, PAGE_SIZE, 4, KV_HEADS, HEAD_DIM)),
        'cache_win': nrm(ks[3], (DEPTH, DEC_BATCH, win_buf, 2, KV_HEADS, HEAD_DIM)),
        'state_ssm_re': nrm(ks[4], (DEPTH, DEC_BATCH, SSM_GROUPS, SSM_STATE), 0.5),
        'state_ssm_im': nrm(ks[5], (DEPTH, DEC_BATCH, SSM_GROUPS, SSM_STATE), 0.5),
        'page_table': jax.random.permutation(ks[6], n_phys)[:n_used].reshape(DEC_BATCH, n_pages).astype(jnp.int32),
        'norm_w': 1.0 + nrm(ks[7], (DEPTH, D_MODEL), 0.01),
        'w_in': nrm(ks[8], (DEPTH, D_MODEL, IN_W), D_MODEL ** -0.5),
        'gate_b': nrm(ks[9], (DEPTH, N_HEADS, 3), 0.1),
        'q_norm_w': 1.0 + nrm(ks[10], (DEPTH, HEAD_DIM), 0.01),
        'k_norm_w': 1.0 + nrm(ks[11], (DEPTH, 3, HEAD_DIM), 0.01),
        'cmp_pe': nrm(ks[12], (DEPTH, 2, CMP_BLOCK, HEAD_DIM), 0.1),
        'cmp_w1': nrm(ks[13], (DEPTH, 2, CMP_BLOCK * HEAD_DIM, CMP_HID), (CMP_BLOCK * HEAD_DIM) ** -0.5),
        'cmp_b1': nrm(ks[14], (DEPTH, 2, CMP_HID), 0.01),
        'cmp_w2': nrm(ks[15], (DEPTH, 2, CMP_HID, HEAD_DIM), CMP_HID ** -0.5),
        'ssm_lam_re': -0.5 + nrm(ks[16], (DEPTH, SSM_GROUPS, SSM_STATE), 0.01),
        'ssm_lam_im': math.pi * n_idx + nrm(ks[17], (DEPTH, SSM_GROUPS, SSM_STATE), 0.01),
        'ssm_log_step': jax.random.uniform(ks[18], (DEPTH, SSM_GROUPS), f32, math.log(1e-3), math.log(1e-1)),
        'ssm_b_re': nrm(ks[19], (DEPTH, SSM_GROUPS, SSM_STATE, SSM_GROUP), (2 * SSM_GROUP) ** -0.5),
        'ssm_b_im': nrm(ks[20], (DEPTH, SSM_GROUPS, SSM_STATE, SSM_GROUP), (2 * SSM_GROUP) ** -0.5),
        'ssm_c_re': nrm(ks[21], (DEPTH, SSM_GROUPS, SSM_GROUP, SSM_STATE), SSM_STATE ** -0.5),
        'ssm_c_im': nrm(ks[22], (DEPTH, SSM_GROUPS, SSM_GROUP, SSM_STATE), SSM_STATE ** -0.5),
        'ssm_d': nrm(ks[23], (DEPTH, SSM_W), 1.0),
        'w_glu': nrm(ks[24], (DEPTH, SSM_W, 2 * SSM_W), SSM_W ** -0.5),
        'w_out': nrm(ks[25], (DEPTH, MIX_W, D_MODEL), MIX_W ** -0.5),
    }


def reference(x_prompt, x_sample, cache_kv, cache_win, state_ssm_re, state_ssm_im, page_table,
              norm_w, w_in, gate_b, q_norm_w, k_norm_w, cmp_pe, cmp_w1, cmp_b1, cmp_w2,
              ssm_lam_re, ssm_lam_im, ssm_log_step, ssm_b_re, ssm_b_im, ssm_c_re, ssm_c_im, ssm_d,
              w_glu, w_out):
    b_p, s_p = x_prompt.shape[:2]
    b_s, s_s = x_sample.shape[:2]
    past_len = page_table.shape[1] * cache_kv.shape[2]
    win_buf = cache_win.shape[2]
    pos_p = jnp.arange(s_p)
    pos_s = past_len + jnp.arange(s_s)
    kwpos_s = past_len - win_buf + jnp.arange(win_buf + s_s)
    h_p, h_s = x_prompt, x_sample
    kv_p_l, kv_s_l, win_p_l, win_s_l = [], [], [], []
    sre_p_l, sim_p_l, sre_s_l, sim_s_l = [], [], [], []
    for l in range(DEPTH):
        ssm_p = (ssm_lam_re[l], ssm_lam_im[l], ssm_log_step[l], ssm_b_re[l], ssm_b_im[l],
                 ssm_c_re[l], ssm_c_im[l], ssm_d[l])
        cmp_p = (cmp_pe[l], cmp_w1[l], cmp_b1[l], cmp_w2[l])
        proj = (norm_w[l], w_in[l], gate_b[l], q_norm_w[l], k_norm_w[l])
        u, g_ssm, q, g_nsa, kv_rows, win_rows, gates = mixer_inputs(h_p, pos_p, *proj)
        zeros = jnp.zeros((b_p, SSM_GROUPS, SSM_STATE), jnp.float32)
        y_ssm, hr_p, hi_p = s5_scan(u, zeros, zeros, *ssm_p)
        o_nsa = nsa_prompt(q, kv_rows, win_rows, gates, *cmp_p)
        h_p = mixer_output(h_p, y_ssm, g_ssm, o_nsa, g_nsa, w_glu[l], w_out[l])
        kv_p_l.append(kv_rows)
        win_p_l.append(win_rows[:, s_p - min(WINDOW, s_p):])
        sre_p_l.append(hr_p)
        sim_p_l.append(hi_p)
        u, g_ssm, q, g_nsa, kv_rows, win_rows, gates = mixer_inputs(h_s, pos_s, *proj)
        y_ssm, hr_s, hi_s = s5_scan(u, state_ssm_re[l], state_ssm_im[l], *ssm_p)
        past = cache_kv[l][page_table].reshape(b_s, past_len, 4, KV_HEADS, HEAD_DIM)
        kc, vc, ksb, vsb = nsa_context(jnp.concatenate([past, kv_rows], axis=1), *cmp_p)
        wrows = jnp.concatenate([cache_win[l], win_rows], axis=1)
        o_nsa = nsa_query_block(q, pos_s, kc, vc, ksb, vsb, wrows[:, :, 0], wrows[:, :, 1], kwpos_s, gates)
        h_s = mixer_output(h_s, y_ssm, g_ssm, o_nsa.reshape(b_s, s_s, NSA_W), g_nsa, w_glu[l], w_out[l])
        kv_s_l.append(kv_rows)
        win_s_l.append(wrows[:, wrows.shape[1] - min(WINDOW, wrows.shape[1]):])
        sre_s_l.append(hr_s)
        sim_s_l.append(hi_s)
    return (h_p, h_s, jnp.stack(kv_p_l), jnp.stack(kv_s_l), jnp.stack(win_p_l), jnp.stack(win_s_l),
            jnp.stack(sre_p_l), jnp.stack(sim_p_l), jnp.stack(sre_s_l), jnp.stack(sim_s_l))
```

```python
import numpy as np
from contextlib import ExitStack
import concourse.bass as bass
import concourse.mybir as mybir
from concourse.bass_utils import run_bass_kernel_spmd

F32 = mybir.dt.float32
BF16 = mybir.dt.bfloat16
I32 = mybir.dt.int32
ALU = mybir.AluOpType
AF = mybir.ActivationFunctionType
AX = mybir.AxisListType

D_MODEL = 1024
IN_W = 2840
N_CORES = 8
SEQ = 4096
HALF = 2048
RMS_EPS = 1e-6


class Buf:
    __slots__ = ("t", "w", "r", "name")

    def __init__(self, t, name=""):
        self.t = t
        self.w = None
        self.r = {}
        self.name = name

    def __getitem__(self, k):
        return self.t[k]


class MK:
    CAP = 30000

    def __init__(self, nc, es, n_dma_sems=40):
        self.nc, self.es = nc, es
        self.es_sem = es
        self.q = {"pe": nc.tensor, "act": nc.scalar, "dve": nc.vector,
                  "pool": nc.gpsimd, "sp": nc.sync}
        self.cur = {}
        self.waited = {k: {} for k in self.q}
        self.nsem = 0
        self.dma_pool = [[self.new_sem(), 0] for _ in range(n_dma_sems)]
        self.dma_rr = 0
        self.n_inst = 0

    def new_sem(self):
        s = self.es_sem.enter_context(self.nc.semaphore(f"s{self.nsem}"))
        self.nsem += 1
        return s

    def sb(self, name, shape, dt=F32):
        return Buf(self.es.enter_context(self.nc.sbuf_tensor("S_" + name, list(shape), dt)), name)

    def ps(self, name, shape, dt=F32):
        return Buf(self.es.enter_context(self.nc.psum_tensor("P_" + name, list(shape), dt)), name)

    def _wait(self, q, tok):
        sem, val, src = tok
        if src == "pe" and q == "pe":
            return
        w = self.waited[q]
        if w.get(id(sem), 0) >= val:
            return
        self.q[q].wait_ge(sem, val)
        w[id(sem)] = val
        self.n_inst += 1

    def _deps(self, q, reads, writes):
        for b in reads:
            if b.w is not None:
                self._wait(q, b.w)
        for b in writes:
            if b.w is not None:
                self._wait(q, b.w)
            for t in b.r.values():
                self._wait(q, t)

    def _mark(self, tok, reads, writes):
        for b in reads:
            b.r[id(tok[0])] = tok
        for b in writes:
            b.w = tok
            b.r = {}

    def op(self, q, fn, reads=(), writes=()):
        self._deps(q, reads, writes)
        inst = fn(self.q[q])
        c = self.cur.get(q)
        if c is None or c[1] >= self.CAP:
            c = [self.new_sem(), 0]
            self.cur[q] = c
        c[1] += 1
        inst.then_inc(c[0], 1)
        tok = (c[0], c[1], q)
        self._mark(tok, reads, writes)
        self.n_inst += 1
        return tok

    def dma(self, q, out, in_, reads=(), writes=(), **kw):
        self._deps(q, reads, writes)
        slot = self.dma_pool[self.dma_rr]
        self.dma_rr = (self.dma_rr + 1) % len(self.dma_pool)
        if slot[1] > 0:
            self._wait(q, (slot[0], slot[1], "dma"))
        inst = self.q[q].dma_start(out=out, in_=in_, **kw)
        slot[1] += 16
        inst.then_inc(slot[0], 16)
        tok = (slot[0], slot[1], "dma")
        self._mark(tok, reads, writes)
        self.n_inst += 1
        return tok

    def barrier(self):
        toks = [(c[0], c[1], q) for q, c in self.cur.items()]
        toks += [(sl[0], sl[1], "dma") for sl in self.dma_pool if sl[1] > 0]
        for q in self.q:
            for t in toks:
                sem, val, src = t
                w = self.waited[q]
                if w.get(id(sem), 0) >= val:
                    continue
                self.q[q].wait_ge(sem, val)
                w[id(sem)] = val
                self.n_inst += 1

    def finish(self, q="sp"):
        for slot in self.dma_pool:
            if slot[1] > 0:
                self._wait(q, (slot[0], slot[1], "dma"))


COLS_B = [(512, 512, 0), (1024, 512, 512), (1536, 512, 1024), (2816, 24, 1536)]
NEG = 30000.0
NSA_STOP = 99
NSA_LVL = 99
NSA_SUB = 99


def nsa_and_mixer(L):
    mk = L["mk"]; nc = L["nc"]
    ident_b, ident_f, eps_t = L["ident_b"], L["ident_f"], L["eps_t"]
    nw, qw, tv = L["nw"], L["qw"], L["tv"]
    KT, kcT, vcT, Vs, Vw, yT = L["KT"], L["kcT"], L["vcT"], L["Vs"], L["Vw"], L["yT"]
    w_in, x_all, cs_all = L["w_in"], L["x_all"], L["cs_all"]

    wbB = mk.sb("wbB", [128, 8 * 1560], BF16)
    wglu = mk.sb("wglu", [128, 4 * 1024], BF16)
    wout = mk.sb("wout", [128, 8 * 1024], BF16)
    gb = mk.sb("gb", [128, 24])
    mk.dma("sp", gb[:], L["gate_b"].partition_broadcast(128), writes=[gb])
    fadd = mk.sb("fadd", [128, 16 * 64])
    mk.dma("sp", fadd[:], L["fadd_in"][:, :], writes=[fadd])
    cthr = mk.sb("cthr", [128, 2])
    mk.dma("sp", cthr[:], L["cthr_in"][:, :], writes=[cthr])
    zeros_b = mk.sb("zeros_b", [128, 128], BF16)
    mk.op("dve", lambda e: e.memset(zeros_b[:], 0.0), [], [zeros_b])
    Eall = mk.sb("Eall", [128, 32 * 128], BF16)
    mk.op("pool", lambda e: e.memset(Eall[64:128, :], 0.0), [], [Eall])
    caus4 = mk.sb("caus4", [128, 512], BF16)
    winlo4 = mk.sb("winlo4", [128, 512], BF16)
    pfxrow = mk.sb("pfxrow", [128, 512], BF16)
    kcTc = mk.sb("kcTc", [128, 2 * 256], BF16)
    vcA = mk.sb("vcA", [128, 2 * 2 * 136], BF16)
    b1c = mk.sb("b1c", [128, 2])
    mk.dma("sp", b1c[:], L["cmp_b1"].rearrange("k h -> h k"), writes=[b1c])

    esT = ExitStack()
    mk.es = esT
    stg = [mk.sb(f"stgB{i}", [128, 1560], F32) for i in range(2)]
    for dt_ in range(8):
        st = stg[dt_ % 2]
        for (c0, cw, z0) in COLS_B:
            mk.dma("sp", st[:, z0:z0 + cw], w_in[dt_ * 128:(dt_ + 1) * 128, c0:c0 + cw], writes=[st])
        mk.op("dve", lambda e, st=st, dt_=dt_: e.tensor_scalar(
            wbB[:, dt_ * 1560:(dt_ + 1) * 1560], st[:], nw[:, dt_:dt_ + 1], None, ALU.mult), [st, nw], [wbB])
    for G in range(4):
        st = stg[G % 2]
        mk.dma("sp", st[:, 0:1024], L["w_glu"][G * 128:(G + 1) * 128, :], writes=[st])
        mk.op("dve", lambda e, st=st, G=G: e.tensor_copy(wglu[:, G * 1024:(G + 1) * 1024], st[:, 0:1024]), [st], [wglu])
    for kt in range(8):
        st = stg[kt % 2]
        mk.dma("sp", st[:, 0:1024], L["w_out"][kt * 128:(kt + 1) * 128, :], writes=[st])
        mk.op("dve", lambda e, st=st, kt=kt: e.tensor_copy(wout[:, kt * 1024:(kt + 1) * 1024], st[:, 0:1024]), [st], [wout])
    for hh in range(4):
        st = stg[hh % 2]
        mk.dma("sp", st[0:64, 0:1024], L["eall_in"][:, hh * 1024:(hh + 1) * 1024], writes=[st])
        mk.op("dve", lambda e, st=st, hh=hh: e.tensor_copy(Eall[0:64, hh * 1024:(hh + 1) * 1024], st[0:64, 0:1024]), [st], [Eall])
    for (src, dst) in ((L["caus_in"], caus4), (L["winlo_in"], winlo4)):
        st = stg[0]
        mk.dma("sp", st[:, 0:128], src[:, :], writes=[st])
        mk.op("dve", lambda e, st=st, dst=dst: e.tensor_copy(
            dst[:, :].rearrange("p (r q) -> p r q", r=4), st[:, 0:128].unsqueeze(1).to_broadcast([128, 4, 128])), [st], [dst])
    st = stg[1]
    mk.dma("sp", st[:, 0:512], L["pfx_in"].rearrange("a n -> (a n)").partition_broadcast(128), writes=[st])
    mk.op("dve", lambda e, st=st: e.tensor_copy(pfxrow[:], st[:, 0:512]), [st], [pfxrow])
    st = stg[0]
    mk.dma("sp", st[:, 0:128].rearrange("p (ct n) -> p ct n", ct=2), L["ov_in"].rearrange("(ct p) n -> p ct n", p=128), writes=[st])
    vcA4 = vcA[:, :].rearrange("p (g ct w) -> p g ct w", g=2, ct=2)
    mk.op("dve", lambda e: e.memset(vcA[:], 1.0), [], [vcA])
    for g in range(2):
        mk.op("dve", lambda e, g=g, st=st: e.tensor_copy(
            vcA4[:, g, :, 65:129], st[:, 0:128].rearrange("p (ct n) -> p ct n", ct=2)), [st], [vcA])

    pC = [mk.ps(f"pC{i}", [128, 512], F32) for i in range(2)]
    W1 = mk.sb("W1c", [128, 32 * 128], BF16)
    w2b = mk.sb("w2b", [128, 128], BF16)
    peT = mk.sb("peT", [128, 32])
    kA = mk.sb("kA", [128, SEQ], BF16)
    kB = mk.sb("kB", [128, SEQ], BF16)
    hidT = mk.sb("hidT", [128, 256], BF16)
    w1s = stg
    mk.op("dve", lambda e: e.memset(kcTc[:], 0.0), [], [kcTc])
    mk.op("dve", lambda e: e.memset(hidT[:], 0.0), [], [hidT])
    for kind, src in ((0, kcT), (1, vcT)):
        for half in range(2):
            mk.dma("sp", peT[half * 64:(half + 1) * 64, :], L["cmp_pe"][kind].rearrange("j d -> d j"), writes=[peT])
        for jq in range(4):
            ws = w1s[jq % 2]
            for half in range(2):
                mk.dma("sp", ws[half * 64:(half + 1) * 64, 0:1024].rearrange("p (j h) -> p j h", j=8),
                       L["cmp_w1"][kind].rearrange("(j d) h -> d j h", d=64)[:, jq * 8:(jq + 1) * 8, :], writes=[ws])
            mk.op("dve", lambda e, ws=ws, jq=jq: e.tensor_copy(W1[:, jq * 1024:(jq + 1) * 1024], ws[:, 0:1024]), [ws], [W1])
        st = stg[1]
        mk.dma("sp", st[:, 0:64], L["cmp_w2"][kind], writes=[st])
        mk.op("dve", lambda e, st=st: e.tensor_copy(
            w2b[:, :].rearrange("p (r d) -> p r d", r=2), st[:, 0:64].unsqueeze(1).to_broadcast([128, 2, 64])), [st], [w2b])
        sv = src[:, :].rearrange("p (i j) -> p i j", j=16)
        mk.op("dve", lambda e, sv=sv: e.tensor_tensor(
            kA[:, :].rearrange("p (i j) -> p i j", j=16), sv, peT[:, 0:16].unsqueeze(1).to_broadcast([128, 256, 16]), ALU.add),
            [src, peT], [kA])
        mk.op("pool", lambda e, sv=sv: e.tensor_tensor(
            kB[:, :].rearrange("p (i j) -> p i j", j=16), sv, peT[:, 16:32].unsqueeze(1).to_broadcast([128, 256, 16]), ALU.add),
            [src, peT], [kB])
        for g in range(2):
            ph = pC[0]
            lo, hi = g * 64, g * 64 + 64
            for j in range(32):
                if j < 16:
                    rhs = kA[lo:hi, :].rearrange("p (i j) -> p i j", j=16)[:, 0:255, j]
                else:
                    rhs = kB[lo:hi, :].rearrange("p (i j) -> p i j", j=16)[:, 1:256, j - 16]
                mk.op("pe", lambda e, ph=ph, j=j, rhs=rhs, lo=lo, hi=hi: e.matmul(
                    ph[:, 0:255], W1[lo:hi, j * 128:(j + 1) * 128], rhs, start=(j == 0), stop=(j == 31)),
                    [W1, kA, kB], [ph])
            mk.op("act", lambda e, ph=ph, kind=kind: e.activation(
                hidT[:, 0:255], ph[:, 0:255], AF.Gelu_apprx_tanh, bias=b1c[:, kind:kind + 1]), [ph, b1c], [hidT])
            po = pC[1]
            if kind == 0:
                mk.op("pe", lambda e, po=po: e.matmul(po[:, 0:256], w2b[:, :], hidT[:, :], start=True, stop=True), [w2b, hidT], [po])
                mk.op("dve", lambda e, po=po, g=g: e.tensor_copy(kcTc[:, g * 256: g * 256 + 255], po[:, 0:255]), [po], [kcTc])
            else:
                for ct in range(2):
                    mk.op("pe", lambda e, po=po, ct=ct: e.matmul(
                        po[:, ct * 64:(ct + 1) * 64], hidT[:, ct * 128:(ct + 1) * 128], w2b[:, 0:64], start=True, stop=True),
                        [hidT, w2b], [po])
                mk.op("dve", lambda e, po=po, g=g: e.tensor_copy(
                    vcA4[:, g, :, 0:64], po[:, 0:128].rearrange("p (ct d) -> p ct d", ct=2)), [po], [vcA])
    mk.barrier()
    esT.close()
    mk.es = L["esPr"]
    if NSA_STOP <= 1:
        return

    xb = mk.sb("xb_B", [128, D_MODEL]); xsq = mk.sb("xsq_B", [128, D_MODEL])
    ssum = mk.sb("ssum_B", [128, 1]); rstd = mk.sb("rstd_B", [128, 1])
    xn = mk.sb("xn_B", [128, D_MODEL], BF16); xnT = mk.sb("xnT_B", [128, 1024], BF16)
    zB = mk.sb("zB", [128, 1560]); cst = mk.sb("cs_B", [128, 16])
    sq = mk.sb("sq_B", [128, 512]); hss = mk.sb("hss_B", [128, 8]); hrs = mk.sb("hrs_B", [128, 8])
    r1 = mk.sb("r1_B", [128, 64]); r2 = mk.sb("r2_B", [128, 64]); r3 = mk.sb("r3_B", [128, 64])
    qb = mk.sb("qb_B", [128, 512], BF16)
    QT = mk.sb("QT_B", [128, 2 * 512], BF16)
    mk.op("pool", lambda e: e.memset(QT[:], 0.0), [], [QT])

    gates = mk.sb("gates_B", [128, 24])
    Pt = [mk.sb(f"Pt{i}", [128, 512], BF16) for i in range(2)]
    mcm = mk.sb("mcm", [128, 128], BF16)
    ocA = mk.sb("ocA", [128, 4 * 129]); osl = mk.sb("osl", [128, 4 * 65]); owi = mk.sb("owi", [128, 4 * 65])
    rd = mk.sb("rd_B", [128, 4]); imp = mk.sb("imp_B", [128, 64]); impt = mk.sb("impt_B", [128, 4 * 64])
    m8 = mk.sb("m8_B", [128, 16]); wk = mk.sb("wk_B", [128, 64]); thr = mk.sb("thr_B", [128, 1])
    nm = mk.sb("nm_B", [128, 64]); nmT4 = mk.sb("nmT4", [128, 512], BF16)
    mk.op("pool", lambda e: e.memset(nmT4[:], 0.0), [], [nmT4])
    mix = mk.sb("mix_B", [128, 1024]); mixb = xn; mixT = xnT
    sg = xsq; yo = zB; ab = yo
    otmp = impt
    pZ = [mk.ps(f"pZ_B{i}", [128, 512], F32) for i in range(2)]
    pSc = [mk.ps(f"pSc{i}", [128, 512], F32) for i in range(2)]
    pO = [mk.ps(f"pO{i}", [128, 512], F32) for i in range(2)]
    pM = mk.ps("pM_B", [128, 512], F32)
    pT = mk.ps("pT_B", [128, 512], BF16)
    KT3 = KT[:, :].rearrange("p (j t) -> p j t", j=4)
    sc_i = [0]

    def scores(g, kt_ap, extra, P):
        ps = pSc[sc_i[0] % 2]
        sc_i[0] += 1
        mk.op("pe", lambda e: e.matmul(ps[:, :], kt_ap, QT[:, g * 512:(g + 1) * 512],
                                       start=True, stop=(len(extra) == 0)), [KT, kcTc, QT], [ps])
        for ei, (lh, rh, deps) in enumerate(extra):
            mk.op("pe", lambda e, lh=lh, rh=rh, ei=ei: e.matmul(
                ps[:, :], lh, rh, start=False, stop=(ei == len(extra) - 1)), deps, [ps])
        mk.op("act", lambda e: e.activation(P[:], ps[:], AF.Exp), [ps], [P])

    oT = sq

    def pv_finish(po, ob):
        mk.op("act", lambda e: e.copy(oT[0:65, :], po[0:65, :]), [po], [oT])
        for sl in range(4):
            mk.op("pe", lambda e, sl=sl: e.transpose(pM[:, sl * 128: sl * 128 + 65], oT[0:65, sl * 128:(sl + 1) * 128], ident_f[0:65, 0:65]),
                  [oT, ident_f], [pM])
        mk.op("act", lambda e: e.copy(ob[:, :].rearrange("p (s w) -> p s w", s=4),
                                      pM[:, :].rearrange("p (s w) -> p s w", s=4)[:, :, 0:65]), [pM], [ob])

    for i in range(16 if NSA_STOP > 2 else 1):
        T = 16 + i
        r0 = T * 128
        mk.dma("sp", xb[:], x_all[r0:r0 + 128, :], writes=[xb])
        mk.dma("sp", cst[:], cs_all[r0:r0 + 128, :], writes=[cst])
        mk.op("act", lambda e: e.activation(xsq[:], xb[:], AF.Square, accum_out=ssum[:]), [xb], [xsq, ssum])
        mk.op("act", lambda e: e.activation(rstd[:], ssum[:], AF.Sqrt, bias=eps_t[:, 0:1], scale=1.0 / D_MODEL), [ssum, eps_t], [rstd])
        mk.op("dve", lambda e: e.reciprocal(rstd[:], rstd[:]), [rstd], [rstd])
        mk.op("dve", lambda e: e.tensor_scalar(xn[:], xb[:], rstd[:, 0:1], None, ALU.mult), [xb, rstd], [xn])
        for g4 in range(2):
            for j in range(4):
                dt_ = g4 * 4 + j
                mk.op("pe", lambda e, j=j, dt_=dt_: e.transpose(pT[:, j * 128:(j + 1) * 128], xn[:, dt_ * 128:(dt_ + 1) * 128], ident_b[:]),
                      [xn, ident_b], [pT])
            mk.op("act", lambda e, g4=g4: e.copy(xnT[:, g4 * 512:(g4 + 1) * 512], pT[:]), [pT], [xnT])
        for ci, (c0, cw, z0) in enumerate(COLS_B):
            p = pZ[ci % 2]
            for dt_ in range(8):
                mk.op("pe", lambda e, p=p, dt_=dt_, z0=z0, cw=cw: e.matmul(
                    p[:, 0:cw], xnT[:, dt_ * 128:(dt_ + 1) * 128], wbB[:, dt_ * 1560 + z0: dt_ * 1560 + z0 + cw],
                    start=(dt_ == 0), stop=(dt_ == 7)), [xnT, wbB], [p])
            mk.op("act" if ci % 2 == 0 else "dve",
                  (lambda e, p=p, z0=z0, cw=cw: e.copy(zB[:, z0:z0 + cw], p[:, 0:cw])) if ci % 2 == 0 else
                  (lambda e, p=p, z0=z0, cw=cw: e.tensor_copy(zB[:, z0:z0 + cw], p[:, 0:cw])), [p], [zB])
        qv = zB[:, 512:1024].rearrange("p (h d) -> p h d", d=64)
        mk.op("act", lambda e: e.activation(sq[:, :].rearrange("p (h d) -> p h d", d=64), qv, AF.Square), [zB], [sq])
        mk.op("dve", lambda e: e.tensor_reduce(hss[:, :], sq[:, :].rearrange("p (h d) -> p h d", d=64), AX.X, ALU.add), [sq], [hss])
        mk.op("act", lambda e: e.activation(hrs[:], hss[:], AF.Sqrt, bias=eps_t[:, 0:1], scale=1.0 / 64), [hss, eps_t], [hrs])
        mk.op("dve", lambda e: e.reciprocal(hrs[:], hrs[:]), [hrs], [hrs])
        mk.op("dve", lambda e: e.tensor_tensor(qv, qv, hrs[:, :].unsqueeze(2).to_broadcast([128, 8, 64]), ALU.mult), [zB, hrs], [zB])
        mk.op("dve", lambda e: e.scalar_tensor_tensor(qv, qv, 0.125, qw[:, :].unsqueeze(1).to_broadcast([128, 8, 64]), ALU.mult, ALU.mult),
              [zB, qw], [zB])
        cosq = cst[:, 0:8].unsqueeze(1).to_broadcast([128, 8, 8])
        sinq = cst[:, 8:16].unsqueeze(1).to_broadcast([128, 8, 8])
        a1 = r1[:, :].rearrange("p (h d) -> p h d", d=8); a2 = r2[:, :].rearrange("p (h d) -> p h d", d=8)
        a3 = r3[:, :].rearrange("p (h d) -> p h d", d=8)
        x1 = qv[..., 0:8]; x2 = qv[..., 8:16]
        mk.op("dve", lambda e: e.tensor_tensor(a1, x2, sinq, ALU.mult), [zB, cst], [r1])
        mk.op("dve", lambda e: e.tensor_tensor(a2, x1, sinq, ALU.mult), [zB, cst], [r2])
        mk.op("dve", lambda e: e.tensor_tensor(a3, x1, cosq, ALU.mult), [zB, cst], [r3])
        mk.op("dve", lambda e: e.tensor_tensor(x1, a3, a1, ALU.subtract), [r3, r1], [zB])
        mk.op("dve", lambda e: e.tensor_tensor(a3, x2, cosq, ALU.mult), [zB, cst], [r3])
        mk.op("dve", lambda e: e.tensor_tensor(x2, a3, a2, ALU.add), [r3, r2], [zB])
        mk.op("dve", lambda e: e.tensor_copy(qb[:], zB[:, 512:1024]), [zB], [qb])
        for j in range(4):
            mk.op("pe", lambda e, j=j: e.transpose(pT[:, j * 128:(j + 1) * 128], qb[:, j * 128:(j + 1) * 128], ident_b[:]), [qb, ident_b], [pT])
        QTv = QT[:, :].rearrange("p (g hh pp q) -> p g hh pp q", g=2, hh=2, pp=2)
        pTv = pT[:, :].rearrange("p (g pp q) -> p g pp q", g=2, pp=2)
        mk.op("act", lambda e: e.copy(QTv[0:64, :, 0, :, :], pTv[0:64, :, :, :]), [pT], [QT])
        mk.op("act", lambda e: e.copy(QTv[64:128, :, 1, :, :], pTv[64:128, :, :, :]), [pT], [QT])
        mk.op("dve", lambda e: e.tensor_tensor(gates[:], zB[:, 1536:1560], gb[:], ALU.add), [zB, gb], [gates])
        mk.op("act", lambda e: e.activation(gates[:], gates[:], AF.Sigmoid), [gates], [gates])
        qrow = tv[:, r0:r0 + 128]

        if NSA_LVL < 1:
            continue
        for g in range(2):
            gview = gates[:, g * 12:(g + 1) * 12].rearrange("p (pp hh t) -> p hh pp t", pp=2, hh=2)
            mixv = mix[:, 512 + g * 256: 512 + (g + 1) * 256].rearrange("p (pp hh d) -> p hh pp d", pp=2, hh=2)
            poA, poB = pO[0], pO[1]
            for ct in range(2):
                P = Pt[ct % 2]
                scores(g, kcTc[:, g * 256 + ct * 128: g * 256 + (ct + 1) * 128], [], P)
                mk.op("dve", lambda e, ct=ct: e.tensor_scalar(mcm[:], qrow, cthr[:, ct:ct + 1], None, ALU.is_ge), [tv, cthr], [mcm])
                mk.op("dve", lambda e, P=P: e.tensor_tensor(
                    P[:, :].rearrange("p (s q) -> p s q", s=4), P[:, :].rearrange("p (s q) -> p s q", s=4),
                    mcm[:, :].unsqueeze(1).to_broadcast([128, 4, 128]), ALU.mult), [P, mcm], [P])
                mk.op("pe", lambda e, P=P, ct=ct: e.matmul(poA[0:65, :], vcA4[:, g, ct, 0:65], P[:, :], start=(ct == 0), stop=(ct == 1)),
                      [P, vcA], [poA])
                mk.op("pe", lambda e, P=P, ct=ct: e.matmul(poB[0:64, :], vcA4[:, g, ct, 65:129], P[:, :], start=(ct == 0), stop=(ct == 1)),
                      [P, vcA], [poB])
            oc3w = ocA[:, :].rearrange("p (s w) -> p s w", s=4)
            mk.op("act", lambda e: e.copy(oT[0:65, :], poA[0:65, :]), [poA], [oT])
            for sl in range(4):
                mk.op("pe", lambda e, sl=sl: e.transpose(pM[:, sl * 128: sl * 128 + 65], oT[0:65, sl * 128:(sl + 1) * 128], ident_f[0:65, 0:65]),
                      [oT, ident_f], [pM])
            mk.op("act", lambda e: e.copy(oc3w[:, :, 0:65], pM[:, :].rearrange("p (s w) -> p s w", s=4)[:, :, 0:65]), [pM], [ocA])
            mk.op("act", lambda e: e.copy(oT[0:64, :], poB[0:64, :]), [poB], [oT])
            for sl in range(4):
                mk.op("pe", lambda e, sl=sl: e.transpose(pM[:, sl * 128: sl * 128 + 64], oT[0:64, sl * 128:(sl + 1) * 128], ident_f[0:64, 0:64]),
                      [oT, ident_f], [pM])
            mk.op("act", lambda e: e.copy(oc3w[:, :, 65:129], pM[:, :].rearrange("p (s w) -> p s w", s=4)[:, :, 0:64]), [pM], [ocA])
            oc3 = ocA[:, :].rearrange("p (s w) -> p s w", s=4)
            mk.op("dve", lambda e: e.tensor_scalar(rd[:], oc3[:, :, 64], 1e-30, None, ALU.max), [ocA], [rd])
            mk.op("dve", lambda e: e.reciprocal(rd[:], rd[:]), [rd], [rd])
            mk.op("dve", lambda e: e.tensor_tensor(
                impt[:, :].rearrange("p (s n) -> p s n", s=4), oc3[:, :, 65:129], rd[:, :].unsqueeze(2).to_broadcast([128, 4, 64]), ALU.mult),
                [ocA, rd], [impt])
            mk.op("dve", lambda e: e.tensor_reduce(imp[:], impt[:, :].rearrange("p (s n) -> p n s", s=4), AX.X, ALU.add), [impt], [imp])
            mk.op("dve", lambda e: e.tensor_tensor(
                rd[:, :].rearrange("p (hh pp) -> p hh pp", hh=2), rd[:, :].rearrange("p (hh pp) -> p hh pp", hh=2), gview[:, :, :, 0], ALU.mult),
                [rd, gates], [rd])
            mk.op("dve", lambda e: e.tensor_tensor(
                mixv, oc3[:, :, 0:64].rearrange("p (hh pp) d -> p hh pp d", hh=2),
                rd[:, :].rearrange("p (hh pp) -> p hh pp", hh=2).unsqueeze(3).to_broadcast([128, 2, 2, 64]), ALU.mult), [ocA, rd], [mix])
            if NSA_LVL < 2:
                continue
            mk.op("dve", lambda e, i=i: e.tensor_tensor(imp[:], imp[:], fadd[:, i * 64:(i + 1) * 64], ALU.add), [imp, fadd], [imp])
            mk.op("dve", lambda e: e.max(out=m8[:, 0:8], in_=imp[:]), [imp], [m8])
            mk.op("dve", lambda e: e.match_replace(out=wk[:], in_to_replace=m8[:, 0:8], in_values=imp[:], imm_value=-1e9), [imp, m8], [wk])
            mk.op("dve", lambda e: e.max(out=m8[:, 8:16], in_=wk[:]), [wk], [m8])
            mk.op("dve", lambda e: e.tensor_scalar(thr[:], m8[:, 15:16], -5000.0, None, ALU.max), [m8], [thr])
            mk.op("dve", lambda e: e.tensor_scalar(nm[:], imp[:], thr[:, 0:1], 1.0, ALU.is_ge, ALU.subtract), [imp, thr], [nm])
            mk.op("pe", lambda e: e.transpose(pM[0:64, 0:128], nm[:], ident_f[:]), [nm, ident_f], [pM])
            mk.op("dve", lambda e: e.tensor_copy(
                nmT4[0:64, :].rearrange("p (s q) -> p s q", s=4), pM[0:64, 0:128].unsqueeze(1).to_broadcast([64, 4, 128])), [pM], [nmT4])
            if NSA_LVL < 3:
                continue
            po = pO[0]
            prev = None
            for tk in range(T + 1):
                extra = [(Eall[:, tk * 128:(tk + 1) * 128], nmT4[:, :], [Eall, nmT4])]
                if tk == T:
                    extra.append((ident_b[:, :], caus4[:, :], [ident_b, caus4]))
                P = Pt[tk % 2]
                scores(g, KT3[:, g, tk * 128:(tk + 1) * 128], extra, P)
                if prev is not None:
                    prev()
                vt = Vs[:, tk * 144 + g * 72: tk * 144 + g * 72 + 65]
                prev = (lambda P=P, vt=vt, tk=tk, po=po: mk.op("pe", lambda e: e.matmul(
                    po[0:65, :], vt, P[:, :], start=(tk == 0), stop=(tk == T)), [P, Vs], [po]))
            prev()
            pv_finish(po, osl)
            if NSA_LVL < 4:
                continue
            po = pO[1]
            prev = None
            tks = list(range(T - 4, T + 1))
            for tk in tks:
                extra = []
                if tk == T - 4:
                    extra.append((ident_b[:, :], winlo4[:, :], [ident_b, winlo4]))
                if tk == T:
                    extra.append((ident_b[:, :], caus4[:, :], [ident_b, caus4]))
                if tk < 16:
                    extra.append((ident_b[:, :], pfxrow[:, :], [ident_b, pfxrow]))
                P = Pt[tk % 2]
                scores(g, KT3[:, 2 + g, tk * 128:(tk + 1) * 128], extra, P)
                if prev is not None:
                    prev()
                vt = Vw[:, tk * 144 + g * 72: tk * 144 + g * 72 + 65]
                prev = (lambda P=P, vt=vt, tk=tk, po=po: mk.op("pe", lambda e: e.matmul(
                    po[0:65, :], vt, P[:, :], start=(tk == tks[0]), stop=(tk == T)), [P, Vw], [po]))
            prev()
            pv_finish(po, owi)
            if NSA_LVL < 5:
                continue
            for bi, ob in ((1, osl), (2, owi)):
                o3 = ob[:, :].rearrange("p (s w) -> p s w", s=4)
                mk.op("dve", lambda e, o3=o3: e.tensor_scalar(rd[:], o3[:, :, 64], 1e-30, None, ALU.max), [ob], [rd])
                mk.op("dve", lambda e: e.reciprocal(rd[:], rd[:]), [rd], [rd])
                mk.op("dve", lambda e, bi=bi: e.tensor_tensor(
                    rd[:, :].rearrange("p (hh pp) -> p hh pp", hh=2), rd[:, :].rearrange("p (hh pp) -> p hh pp", hh=2), gview[:, :, :, bi], ALU.mult),
                    [rd, gates], [rd])
                ot = otmp[:, :].rearrange("p (hh pp d) -> p hh pp d", hh=2, pp=2)
                mk.op("dve", lambda e, o3=o3, ot=ot: e.tensor_tensor(
                    ot, o3[:, :, 0:64].rearrange("p (hh pp) d -> p hh pp d", hh=2),
                    rd[:, :].rearrange("p (hh pp) -> p hh pp", hh=2).unsqueeze(3).to_broadcast([128, 2, 2, 64]), ALU.mult), [ob, rd], [otmp])
                mk.op("dve", lambda e, ot=ot: e.tensor_tensor(mixv, mixv, ot, ALU.add), [mix, otmp], [mix])

        if NSA_LVL < 6:
            continue
        mk.op("act", lambda e: e.activation(sg[:, 0:512], zB[:, 0:512], AF.Silu), [zB], [sg])
        mk.op("act", lambda e: e.activation(sg[:, 512:1024], zB[:, 1024:1536], AF.Silu), [zB], [sg])
        for nh in range(2):
            p = pZ[nh]
            for G in range(4):
                mk.op("pe", lambda e, p=p, G=G, nh=nh, i=i: e.matmul(
                    p[:, :], yT[:, G * HALF + i * 128: G * HALF + (i + 1) * 128],
                    wglu[:, G * 1024 + nh * 512: G * 1024 + (nh + 1) * 512], start=(G == 0), stop=(G == 3)), [yT, wglu], [p])
        mk.op("act", lambda e: e.activation(ab[:, 512:1024], pZ[1][:, :], AF.Sigmoid), [pZ[1]], [ab])
        mk.op("dve", lambda e: e.tensor_tensor(ab[:, 0:512], pZ[0][:, :], ab[:, 512:1024], ALU.mult), [pZ[0], ab], [ab])
        mk.op("dve", lambda e: e.tensor_tensor(mix[:, 0:512], ab[:, 0:512], sg[:, 0:512], ALU.mult), [ab, sg], [mix])
        mk.op("dve", lambda e: e.tensor_tensor(mix[:, 512:1024], mix[:, 512:1024], sg[:, 512:1024], ALU.mult), [mix, sg], [mix])
        mk.op("dve", lambda e: e.tensor_copy(mixb[:], mix[:]), [mix], [mixb])
        for g4 in range(2):
            for j in range(4):
                kt = g4 * 4 + j
                mk.op("pe", lambda e, j=j, kt=kt: e.transpose(pT[:, j * 128:(j + 1) * 128], mixb[:, kt * 128:(kt + 1) * 128], ident_b[:]),
                      [mixb, ident_b], [pT])
            mk.op("act", lambda e, g4=g4: e.copy(mixT[:, g4 * 512:(g4 + 1) * 512], pT[:]), [pT], [mixT])
        for nh in range(2):
            p = pZ[nh]
            for kt in range(8):
                mk.op("pe", lambda e, p=p, kt=kt, nh=nh: e.matmul(
                    p[:, :], mixT[:, kt * 128:(kt + 1) * 128], wout[:, kt * 1024 + nh * 512: kt * 1024 + (nh + 1) * 512],
                    start=(kt == 0), stop=(kt == 7)), [mixT, wout], [p])
            mk.op("dve", lambda e, p=p, nh=nh: e.tensor_tensor(yo[:, nh * 512:(nh + 1) * 512], p[:, :], xb[:, nh * 512:(nh + 1) * 512], ALU.add),
                  [p, xb], [yo])
        mk.dma("sp", L["y_o"][i * 128:(i + 1) * 128, :], yo[:, 0:1024], reads=[yo])


SMP_LVL = 99
SMP_SUB = 99
NPG = 64
LK = NPG * 128 + 128


def sample_phase(L):
    mk = L["mk"]; nc = L["nc"]
    mk.es = L["esSm"]
    ident_b, ident_f, eps_t = L["ident_b"], L["ident_f"], L["eps_t"]
    nw, qw, snew, yTs = L["nw"], L["qw"], L["snew"], L["yTs"]
    w_in = L["w_in"]
    cache = L["cache_rows"]; cwin = L["cache_win_in"]; ptab = L["ptab_in"]

    wglu = mk.sb("s_wglu", [128, 4 * 1024], BF16)
    wout = mk.sb("s_wout", [128, 8 * 1024], BF16)
    gb = mk.sb("s_gb", [128, 24])
    mk.dma("sp", gb[:], L["gate_b"].partition_broadcast(128), writes=[gb])
    W1 = [mk.sb(f"s_W1{k}", [128, 32 * 128], BF16) for k in range(2)]
    w2b = [mk.sb(f"s_w2b{k}", [128, 128], BF16) for k in range(2)]
    peT = [mk.sb(f"s_peT{k}", [128, 32], BF16) for k in range(2)]
    b1c = mk.sb("s_b1c", [128, 2])
    mk.dma("sp", b1c[:], L["cmp_b1"].rearrange("k h -> h k"), writes=[b1c])
    pcol = mk.sb("s_pcol", [128, 1])
    mk.dma("sp", pcol[:], L["pcol_in"][:, :], writes=[pcol])
    cval = mk.sb("s_cval", [128, 4])
    mk.dma("sp", cval[:], L["cval_in"][:, :], writes=[cval])
    wcol = mk.sb("s_wcol", [128, 5])
    mk.dma("sp", wcol[:], L["wcol_in"][:, :], writes=[wcol])
    newcol = mk.sb("s_newcol", [128, 1])
    mk.dma("sp", newcol[:], L["newcol_in"][:, :], writes=[newcol])
    fadd_s = mk.sb("s_fadd", [1, 130])
    mk.dma("sp", fadd_s[:], L["fadds_in"][:, :], writes=[fadd_s])
    sel2 = mk.sb("s_sel2", [1, 256])
    mk.dma("sp", sel2[:], L["sel2_in"][:, :], writes=[sel2])
    vcA = mk.sb("s_vcA", [128, 4 * 2 * 200], BF16)
    vcA4 = vcA[:, :].rearrange("p (ct g w) -> p ct g w", ct=4, g=2)
    mk.op("dve", lambda e: e.memset(vcA[:], 1.0), [], [vcA])
    pT = mk.ps("s_pT", [128, 1024], BF16)
    pZ = [mk.ps(f"s_pZ{i}", [128, 512], F32) for i in range(2)]
    pSc = [mk.ps(f"s_pSc{i}", [128, 512], F32) for i in range(2)]
    pAcc = [mk.ps(f"s_pAcc{i}", [128, 512], F32) for i in range(2)]
    pM = mk.ps("s_pM", [128, 512], F32)
    bcol = mk.sb("s_bcol", [128, 2])

    xb = mk.sb("s_xb", [128, D_MODEL]); xsq = mk.sb("s_xsq", [128, D_MODEL])
    ssum = mk.sb("s_ssum", [128, 1]); rstd = mk.sb("s_rstd", [128, 1])
    xn = mk.sb("s_xn", [128, D_MODEL], BF16); xnT = mk.sb("s_xnT", [128, 1024], BF16)
    zB = mk.sb("s_zB", [128, 1560]); cst = mk.sb("s_cs", [128, 16])
    sq = mk.sb("s_sq", [128, 512]); hss = mk.sb("s_hss", [128, 8]); hrs = mk.sb("s_hrs", [128, 8])
    r1 = mk.sb("s_r1", [128, 64]); r2 = mk.sb("s_r2", [128, 64]); r3 = mk.sb("s_r3", [128, 64])
    qb = mk.sb("s_qb", [128, 512], BF16)
    QT = mk.sb("s_QT", [128, 4 * 128], BF16)
    gates = mk.sb("s_gates", [128, 24])
    snb = mk.sb("s_snb", [128, 512], BF16)
    knT = mk.sb("s_knT", [128, 512], BF16)
    vnb = mk.sb("s_vnb", [128, 256], BF16)
    esT = ExitStack()
    mk.es = esT
    wbB = mk.sb("s_wbB", [128, 8 * 1560], BF16)
    stg = [mk.sb(f"s_stg{i}", [128, 1560], F32) for i in range(2)]
    for dt_ in range(8):
        st = stg[dt_ % 2]
        for (c0, cw, z0) in COLS_B:
            mk.dma("sp", st[:, z0:z0 + cw], w_in[dt_ * 128:(dt_ + 1) * 128, c0:c0 + cw], writes=[st])
        mk.op("dve", lambda e, st=st, dt_=dt_: e.tensor_scalar(
            wbB[:, dt_ * 1560:(dt_ + 1) * 1560], st[:], nw[:, dt_:dt_ + 1], None, ALU.mult), [st, nw], [wbB])
    for G in range(4):
        st = stg[G % 2]
        mk.dma("sp", st[:, 0:1024], L["w_glu"][G * 128:(G + 1) * 128, :], writes=[st])
        mk.op("dve", lambda e, st=st, G=G: e.tensor_copy(wglu[:, G * 1024:(G + 1) * 1024], st[:, 0:1024]), [st], [wglu])
    for kt in range(8):
        st = stg[kt % 2]
        mk.dma("sp", st[:, 0:1024], L["w_out"][kt * 128:(kt + 1) * 128, :], writes=[st])
        mk.op("dve", lambda e, st=st, kt=kt: e.tensor_copy(wout[:, kt * 1024:(kt + 1) * 1024], st[:, 0:1024]), [st], [wout])
    for ct in range(4):
        st = stg[ct % 2]
        mk.dma("sp", st[:, 0:129], L["ovs_in"][ct * 128:(ct + 1) * 128, :], writes=[st])
        for g in range(2):
            mk.op("dve", lambda e, st=st, ct=ct, g=g: e.tensor_copy(vcA4[:, ct, g, 65:194], st[:, 0:129]), [st], [vcA])
    for kind in range(2):
        for jq in range(4):
            ws = stg[jq % 2]
            for half in range(2):
                mk.dma("sp", ws[half * 64:(half + 1) * 64, 0:1024].rearrange("p (j h) -> p j h", j=8),
                       L["cmp_w1"][kind].rearrange("(j d) h -> d j h", d=64)[:, jq * 8:(jq + 1) * 8, :], writes=[ws])
            mk.op("dve", lambda e, ws=ws, jq=jq, kind=kind: e.tensor_copy(W1[kind][:, jq * 1024:(jq + 1) * 1024], ws[:, 0:1024]), [ws], [W1[kind]])
        st = stg[0]
        mk.dma("sp", st[:, 0:64], L["cmp_w2"][kind], writes=[st])
        mk.op("dve", lambda e, st=st, kind=kind: e.tensor_copy(
            w2b[kind][:, :].rearrange("p (r d) -> p r d", r=2), st[:, 0:64].unsqueeze(1).to_broadcast([128, 2, 64])), [st], [w2b[kind]])
        st = stg[1]
        for half in range(2):
            mk.dma("sp", st[half * 64:(half + 1) * 64, 0:32], L["cmp_pe"][kind].rearrange("j d -> d j"), writes=[st])
        mk.op("dve", lambda e, st=st, kind=kind: e.tensor_copy(peT[kind][:], st[:, 0:32]), [st], [peT[kind]])
    for kind in range(2):
        for j in range(32):
            mk.op("pe", lambda e, kind=kind, j=j: e.matmul(
                pM[:, kind:kind + 1], W1[kind][0:64, j * 128:(j + 1) * 128], peT[kind][0:64, j:j + 1],
                start=(j == 0), stop=(j == 31)), [W1[kind], peT[kind]], [pM])
        mk.op("dve", lambda e, kind=kind: e.tensor_tensor(bcol[:, kind:kind + 1], pM[:, kind:kind + 1], b1c[:, kind:kind + 1], ALU.add),
              [pM, b1c], [bcol])

    mk.dma("sp", xb[:], L["x_smp"][:, :], writes=[xb])
    mk.dma("sp", cst[:], L["cs_smp"][:, :], writes=[cst])
    mk.op("act", lambda e: e.activation(xsq[:], xb[:], AF.Square, accum_out=ssum[:]), [xb], [xsq, ssum])
    mk.op("act", lambda e: e.activation(rstd[:], ssum[:], AF.Sqrt, bias=eps_t[:, 0:1], scale=1.0 / D_MODEL), [ssum, eps_t], [rstd])
    mk.op("dve", lambda e: e.reciprocal(rstd[:], rstd[:]), [rstd], [rstd])
    mk.op("dve", lambda e: e.tensor_scalar(xn[:], xb[:], rstd[:, 0:1], None, ALU.mult), [xb, rstd], [xn])
    for g4 in range(2):
        for j in range(4):
            dt_ = g4 * 4 + j
            mk.op("pe", lambda e, j=j, dt_=dt_: e.transpose(pT[:, j * 128:(j + 1) * 128], xn[:, dt_ * 128:(dt_ + 1) * 128], ident_b[:]),
                  [xn, ident_b], [pT])
        mk.op("act", lambda e, g4=g4: e.copy(xnT[:, g4 * 512:(g4 + 1) * 512], pT[:, 0:512]), [pT], [xnT])
    for ci, (c0, cw, z0) in enumerate(COLS_B):
        p = pZ[ci % 2]
        for dt_ in range(8):
            mk.op("pe", lambda e, p=p, dt_=dt_, z0=z0, cw=cw: e.matmul(
                p[:, 0:cw], xnT[:, dt_ * 128:(dt_ + 1) * 128], wbB[:, dt_ * 1560 + z0: dt_ * 1560 + z0 + cw],
                start=(dt_ == 0), stop=(dt_ == 7)), [xnT, wbB], [p])
        mk.op("dve", lambda e, p=p, z0=z0, cw=cw: e.tensor_copy(zB[:, z0:z0 + cw], p[:, 0:cw]), [p], [zB])
    qv = zB[:, 512:1024].rearrange("p (h d) -> p h d", d=64)
    mk.op("act", lambda e: e.activation(sq[:, :].rearrange("p (h d) -> p h d", d=64), qv, AF.Square), [zB], [sq])
    mk.op("dve", lambda e: e.tensor_reduce(hss[:, :], sq[:, :].rearrange("p (h d) -> p h d", d=64), AX.X, ALU.add), [sq], [hss])
    mk.op("act", lambda e: e.activation(hrs[:], hss[:], AF.Sqrt, bias=eps_t[:, 0:1], scale=1.0 / 64), [hss, eps_t], [hrs])
    mk.op("dve", lambda e: e.reciprocal(hrs[:], hrs[:]), [hrs], [hrs])
    mk.op("dve", lambda e: e.tensor_tensor(qv, qv, hrs[:, :].unsqueeze(2).to_broadcast([128, 8, 64]), ALU.mult), [zB, hrs], [zB])
    mk.op("dve", lambda e: e.scalar_tensor_tensor(qv, qv, 0.125, qw[:, :].unsqueeze(1).to_broadcast([128, 8, 64]), ALU.mult, ALU.mult),
          [zB, qw], [zB])
    cosq = cst[:, 0:8].unsqueeze(1).to_broadcast([128, 8, 8])
    sinq = cst[:, 8:16].unsqueeze(1).to_broadcast([128, 8, 8])
    a1 = r1[:, :].rearrange("p (h d) -> p h d", d=8); a2 = r2[:, :].rearrange("p (h d) -> p h d", d=8)
    a3 = r3[:, :].rearrange("p (h d) -> p h d", d=8)
    x1 = qv[..., 0:8]; x2 = qv[..., 8:16]
    mk.op("dve", lambda e: e.tensor_tensor(a1, x2, sinq, ALU.mult), [zB, cst], [r1])
    mk.op("dve", lambda e: e.tensor_tensor(a2, x1, sinq, ALU.mult), [zB, cst], [r2])
    mk.op("dve", lambda e: e.tensor_tensor(a3, x1, cosq, ALU.mult), [zB, cst], [r3])
    mk.op("dve", lambda e: e.tensor_tensor(x1, a3, a1, ALU.subtract), [r3, r1], [zB])
    mk.op("dve", lambda e: e.tensor_tensor(a3, x2, cosq, ALU.mult), [zB, cst], [r3])
    mk.op("dve", lambda e: e.tensor_tensor(x2, a3, a2, ALU.add), [r3, r2], [zB])
    mk.op("dve", lambda e: e.tensor_copy(qb[:], zB[:, 512:1024]), [zB], [qb])
    for j in range(4):
        mk.op("pe", lambda e, j=j: e.transpose(pT[:, j * 128:(j + 1) * 128], qb[:, j * 128:(j + 1) * 128], ident_b[:]), [qb, ident_b], [pT])
    mk.op("act", lambda e: e.copy(QT[:], pT[:, 0:512]), [pT], [QT])
    QT3 = QT[:, :].rearrange("p (pr q) -> p pr q", pr=4)
    mk.op("dve", lambda e: e.tensor_tensor(gates[:], zB[:, 1536:1560], gb[:], ALU.add), [zB, gb], [gates])
    mk.op("act", lambda e: e.activation(gates[:], gates[:], AF.Sigmoid), [gates], [gates])
    mk.op("dve", lambda e: e.tensor_copy(
        snb[:, 0:256].rearrange("p (g r d) -> p g r d", g=2, r=2),
        snew[:, 256:384].rearrange("p (g d) -> p g d", g=2).unsqueeze(2).to_broadcast([128, 2, 2, 64])), [snew], [snb])
    mk.op("dve", lambda e: e.tensor_copy(
        snb[:, 256:512].rearrange("p (g r d) -> p g r d", g=2, r=2),
        snew[:, 512:640].rearrange("p (g d) -> p g d", g=2).unsqueeze(2).to_broadcast([128, 2, 2, 64])), [snew], [snb])
    for j in range(4):
        mk.op("pe", lambda e, j=j: e.transpose(pT[:, j * 128:(j + 1) * 128], snb[:, j * 128:(j + 1) * 128], ident_b[:]), [snb, ident_b], [pT])
    mk.op("act", lambda e: e.copy(knT[:], pT[:, 0:512]), [pT], [knT])
    mk.op("dve", lambda e: e.tensor_copy(vnb[:, 0:128], snew[:, 384:512]), [snew], [vnb])
    mk.op("dve", lambda e: e.tensor_copy(vnb[:, 128:256], snew[:, 640:768]), [snew], [vnb])

    mk.barrier()
    esT.close()
    mk.es = L["esSm"]
    CT = mk.sb("s_CT", [128, 2 * 8192], BF16)
    KTs = mk.sb("s_KTs", [128, 2 * LK], BF16)
    Vs = mk.sb("s_Vs", [128, 65 * 144], BF16)
    KTw = mk.sb("s_KTw", [128, 2 * 640], BF16)
    Vw = mk.sb("s_Vw", [128, 5 * 144], BF16)
    mk.op("pool", lambda e: e.memset(Vs[:], 1.0), [], [Vs])
    mk.op("pool", lambda e: e.memset(Vw[:], 1.0), [], [Vw])
    mk.op("pool", lambda e: e.memset(KTs[:], 0.0), [], [KTs])
    mk.op("pool", lambda e: e.memset(KTw[:], 0.0), [], [KTw])
    pg = [mk.sb(f"s_pg{i}", [128, 512]) for i in range(2)]
    pgb = [mk.sb(f"s_pgb{i}", [128, 512], BF16) for i in range(2)]
    ptf = mk.sb("s_ptf", [128, 64]); pti = mk.sb("s_pti", [128, 64], I32); idx = mk.sb("s_idx", [128, 64], I32)
    hidT = mk.sb("s_hidT", [128, 512], BF16)
    mk.op("dve", lambda e: e.memset(hidT[:], 0.0), [], [hidT])
    kcTc = mk.sb("s_kcTc", [128, 2 * 512], BF16)
    P4 = [mk.sb(f"s_P4{i}", [128, 4], BF16) for i in range(2)]
    ocs = mk.sb("s_ocs", [4, 200]); osl = mk.sb("s_osl", [4, 72]); owi = mk.sb("s_owi", [4, 72])
    rdn = mk.sb("s_rdn", [4, 1]); obr = mk.sb("s_obr", [4, 3 * 64])
    impr = mk.sb("s_impr", [1, 130]); m8 = mk.sb("s_m8", [1, 16]); wk = mk.sb("s_wk", [1, 130]); thr = mk.sb("s_thr", [1, 1])
    nmr = mk.sb("s_nmr", [1, 130]); mcol = mk.sb("s_mcol", [128, 65])
    onsa = mk.sb("s_onsa", [128, 3 * 512])
    mk.op("dve", lambda e: e.memset(onsa[:], 0.0), [], [onsa])
    wpg = [mk.sb(f"s_wpg{i}", [128, 256]) for i in range(2)]
    CT3 = CT[:, :].rearrange("p (a t) -> p a t", a=2)
    KTs3 = KTs[:, :].rearrange("p (g t) -> p g t", g=2)
    KTw3 = KTw[:, :].rearrange("p (g t) -> p g t", g=2)

    def gather(q, dst_buf, dst_ap, idx_ap):
        mk._deps(q, [idx], [dst_buf])
        slot = mk.dma_pool[mk.dma_rr]
        mk.dma_rr = (mk.dma_rr + 1) % len(mk.dma_pool)
        if slot[1] > 0:
            mk._wait(q, (slot[0], slot[1], "dma"))
        inst = nc.gpsimd.indirect_dma_start(out=dst_ap, out_offset=None, in_=cache[:, :],
                                            in_offset=bass.IndirectOffsetOnAxis(ap=idx_ap, axis=0))
        slot[1] += 16
        inst.then_inc(slot[0], 16)
        tok = (slot[0], slot[1], "dma")
        mk._mark(tok, [idx], [dst_buf])
        mk.n_inst += 1

    for si in range(4 if SMP_LVL > -3 else 0):
        mk.dma("sp", pti[:], ptab[si, :].partition_broadcast(128), writes=[pti])
        mk.op("dve", lambda e: e.tensor_copy(ptf[:], pti[:]), [pti], [ptf])
        mk.op("dve", lambda e: e.tensor_scalar(ptf[:], ptf[:], 128.0, pcol[:, 0:1], ALU.mult, ALU.add), [ptf, pcol], [ptf])
        mk.op("dve", lambda e: e.tensor_copy(idx[:], ptf[:]), [ptf], [idx])
        if SMP_LVL < -1:
            continue
        for j in range(NPG):
            g_ = pg[j % 2]; b_ = pgb[j % 2]
            gather("pool", g_, g_[:, :], idx[:, j:j + 1])
            if SMP_SUB < 1:
                continue
            mk.op("dve", lambda e, g_=g_, b_=b_: e.tensor_copy(b_[:, 0:256], g_[:, 0:256]), [g_], [b_])
            mk.op("pool", lambda e, g_=g_, b_=b_: e.tensor_copy(
                b_[:, 256:512].rearrange("p (g r d) -> p g r d", g=2, r=2),
                g_[:, 256:384].rearrange("p (g d) -> p g d", g=2).unsqueeze(2).to_broadcast([128, 2, 2, 64])), [g_], [b_])
            mk.op("pool", lambda e, g_=g_, j=j: e.tensor_copy(
                Vs[:, j * 144:(j + 1) * 144].rearrange("p (g d) -> p g d", g=2)[:, :, 0:64],
                g_[:, 384:512].rearrange("p (g d) -> p g d", g=2)), [g_], [Vs])
            if SMP_SUB < 2:
                continue
            for q4 in range(4):
                mk.op("pe", lambda e, q4=q4, b_=b_: e.transpose(pT[:, q4 * 128:(q4 + 1) * 128], b_[:, q4 * 128:(q4 + 1) * 128], ident_b[:]),
                      [b_, ident_b], [pT])
            mk.op("act", lambda e, j=j: e.copy(CT3[:, :, j * 128:(j + 1) * 128], pT[:, 0:256].rearrange("p (a t) -> p a t", a=2)), [pT], [CT])
            mk.op("act", lambda e, j=j: e.copy(KTs3[:, :, j * 128:(j + 1) * 128], pT[:, 256:512].rearrange("p (a t) -> p a t", a=2)),
                  [pT], [KTs])
        if SMP_LVL < 0:
            continue
        for g in range(2):
            mk.op("dve", lambda e, g=g, si=si: e.tensor_copy(KTs3[:, g, 8192:8193], knT[:, g * 128 + si: g * 128 + si + 1]), [knT], [KTs])
            mk.op("dve", lambda e, g=g, si=si: e.tensor_copy(KTw3[:, g, 512:513], knT[:, (2 + g) * 128 + si: (2 + g) * 128 + si + 1]), [knT], [KTw])
        mk.dma("sp", Vs[0:1, 64 * 144: 65 * 144].rearrange("p (g d) -> p g d", g=2)[:, :, 0:64],
               vnb[si:si + 1, 0:128].rearrange("p (g d) -> p g d", g=2), reads=[vnb], writes=[Vs])
        mk.dma("sp", Vw[0:1, 4 * 144: 5 * 144].rearrange("p (g d) -> p g d", g=2)[:, :, 0:64],
               vnb[si:si + 1, 128:256].rearrange("p (g d) -> p g d", g=2), reads=[vnb], writes=[Vw])
        for wt in range(4):
            wp = wpg[wt % 2]; b_ = pgb[wt % 2]
            mk.dma("sp", wp[:], cwin[si, wt * 128:(wt + 1) * 128, :], writes=[wp])
            mk.op("pool", lambda e, wp=wp, b_=b_: e.tensor_copy(
                b_[:, 0:256].rearrange("p (g r d) -> p g r d", g=2, r=2),
                wp[:, 0:128].rearrange("p (g d) -> p g d", g=2).unsqueeze(2).to_broadcast([128, 2, 2, 64])), [wp], [b_])
            mk.op("pool", lambda e, wp=wp, wt=wt: e.tensor_copy(
                Vw[:, wt * 144:(wt + 1) * 144].rearrange("p (g d) -> p g d", g=2)[:, :, 0:64],
                wp[:, 128:256].rearrange("p (g d) -> p g d", g=2)), [wp], [Vw])
            for q4 in range(2):
                mk.op("pe", lambda e, q4=q4, b_=b_: e.transpose(pT[:, q4 * 128:(q4 + 1) * 128], b_[:, q4 * 128:(q4 + 1) * 128], ident_b[:]),
                      [b_, ident_b], [pT])
            mk.op("act", lambda e, wt=wt: e.copy(KTw3[:, :, wt * 128:(wt + 1) * 128], pT[:, 0:256].rearrange("p (a t) -> p a t", a=2)),
                  [pT], [KTw])
        mk.dma("sp", L["wins_o"][si, 0:511, :], cwin[si, 1:512, :])
        mk.dma("sp", L["wins_o"][si, 511:512, :], snew[si:si + 1, 512:768], reads=[snew])
        if SMP_LVL < 1:
            continue
        for kind in range(2):
            for g in range(2):
                lo, hi = g * 64, g * 64 + 64
                ph = pZ[0]
                src = CT3[lo:hi, kind, :].rearrange("p (i j) -> p i j", j=16)
                for j in range(32):
                    rhs = src[:, 0:511, j] if j < 16 else src[:, 1:512, j - 16]
                    mk.op("pe", lambda e, j=j, rhs=rhs, lo=lo, hi=hi, kind=kind: e.matmul(
                        ph[:, 0:511], W1[kind][lo:hi, j * 128:(j + 1) * 128], rhs, start=(j == 0), stop=(j == 31)),
                        [W1[kind], CT], [ph])
                mk.op("act", lambda e, kind=kind: e.activation(hidT[:, 0:511], ph[:, 0:511], AF.Gelu_apprx_tanh, bias=bcol[:, kind:kind + 1]),
                      [ph, bcol], [hidT])
                po = pZ[1]
                if kind == 0:
                    mk.op("pe", lambda e: e.matmul(po[:, 0:512], w2b[0][:, :], hidT[:, :], start=True, stop=True), [w2b[0], hidT], [po])
                    mk.op("dve", lambda e, g=g: e.tensor_copy(kcTc[:, g * 512:(g + 1) * 512], po[:, 0:512]), [po], [kcTc])
                else:
                    for ct in range(4):
                        mk.op("pe", lambda e, ct=ct: e.matmul(po[:, ct * 64:(ct + 1) * 64], hidT[:, ct * 128:(ct + 1) * 128], w2b[1][:, 0:64],
                                                              start=True, stop=True), [hidT, w2b[1]], [po])
                    mk.op("dve", lambda e, g=g: e.tensor_copy(vcA4[:, :, g, 0:64], po[:, 0:256].rearrange("p (ct d) -> p ct d", ct=4)), [po], [vcA])
        if SMP_LVL < 2:
            continue
        for g in range(2):
            def sc(kt_fn, bias_ap, Pd, extra_reads):
                for hh in range(2):
                    ps = pSc[hh]
                    mk.op("pe", lambda e, ps=ps, hh=hh: e.matmul(ps[:, 0:2], kt_fn(hh), QT3[hh * 64:(hh + 1) * 64, 2 * g:2 * g + 2, si],
                                                                 start=True, stop=True), [QT] + extra_reads, [ps])
                    if bias_ap is None:
                        mk.op("act", lambda e, ps=ps, hh=hh: e.activation(Pd[:, hh * 2:hh * 2 + 2], ps[:, 0:2], AF.Exp), [ps], [Pd])
                    else:
                        mk.op("act", lambda e, ps=ps, hh=hh: e.activation(Pd[:, hh * 2:hh * 2 + 2], ps[:, 0:2], AF.Exp, bias=bias_ap),
                              [ps, mcol, wcol], [Pd])
            pa = pAcc[0]
            for ct in range(4):
                Pd = P4[ct % 2]
                sc(lambda hh, ct=ct: kcTc[hh * 64:(hh + 1) * 64, g * 512 + ct * 128: g * 512 + (ct + 1) * 128], None, Pd, [kcTc])
                mk.op("dve", lambda e, Pd=Pd, ct=ct: e.tensor_scalar(Pd[:], Pd[:], cval[:, ct:ct + 1], None, ALU.mult), [Pd, cval], [Pd])
                mk.op("pe", lambda e, Pd=Pd, ct=ct: e.matmul(pa[0:4, 0:194], Pd[:, :], vcA4[:, ct, g, 0:194], start=(ct == 0), stop=(ct == 3)),
                      [Pd, vcA], [pa])
            mk.op("act", lambda e: e.copy(ocs[:, 0:194], pa[0:4, 0:194]), [pa], [ocs])
            mk.op("dve", lambda e: e.tensor_scalar(rdn[:], ocs[:, 64:65], 1e-30, None, ALU.max), [ocs], [rdn])
            mk.op("dve", lambda e: e.reciprocal(rdn[:], rdn[:]), [rdn], [rdn])
            mk.op("dve", lambda e: e.tensor_scalar(obr[:, 0:64], ocs[:, 0:64], rdn[:, 0:1], None, ALU.mult), [ocs, rdn], [obr])
            mk.op("pe", lambda e: e.matmul(pM[0:1, 0:129], rdn[:, 0:1], ocs[:, 65:194], start=True, stop=True), [rdn, ocs], [pM])
            mk.op("dve", lambda e: e.tensor_copy(impr[:], fadd_s[:]), [fadd_s], [impr])
            mk.op("dve", lambda e: e.tensor_tensor(impr[:, 0:129], impr[:, 0:129], pM[0:1, 0:129], ALU.add), [impr, pM], [impr])
            mk.op("dve", lambda e: e.max(out=m8[:, 0:8], in_=impr[:]), [impr], [m8])
            mk.op("dve", lambda e: e.match_replace(out=wk[:], in_to_replace=m8[:, 0:8], in_values=impr[:], imm_value=-1e9), [impr, m8], [wk])
            mk.op("dve", lambda e: e.max(out=m8[:, 8:16], in_=wk[:]), [wk], [m8])
            mk.op("dve", lambda e: e.tensor_scalar(thr[:], m8[:, 15:16], -5000.0, None, ALU.max), [m8], [thr])
            mk.op("dve", lambda e: e.tensor_scalar(nmr[:], impr[:], thr[:, 0:1], 1.0, ALU.is_ge, ALU.subtract), [impr, thr], [nmr])
            nm2 = nmr[:, :].rearrange("p (j two) -> p j two", two=2)
            mk.op("pe", lambda e: e.matmul(pM[:, 256:321], sel2[:, 0:128], nm2[:, :, 0], start=True, stop=False), [sel2, nmr], [pM])
            mk.op("pe", lambda e: e.matmul(pM[:, 256:321], sel2[:, 128:256], nm2[:, :, 1], start=False, stop=True), [sel2, nmr], [pM])
            mk.op("dve", lambda e: e.tensor_scalar(mcol[:], pM[:, 256:321], 30000.0, None, ALU.mult), [pM], [mcol])
            mk.op("dve", lambda e: e.tensor_tensor(mcol[:, 64:65], mcol[:, 64:65], newcol[:], ALU.add), [mcol, newcol], [mcol])
            pa = pAcc[1]
            for j in range(NPG + 1):
                Pd = P4[j % 2]
                sc(lambda hh, j=j: KTs3[hh * 64:(hh + 1) * 64, g, j * 128:(j + 1) * 128], mcol[:, j:j + 1], Pd, [KTs])
                mk.op("pe", lambda e, Pd=Pd, j=j: e.matmul(pa[0:4, 0:65], Pd[:, :], Vs[:, j * 144 + g * 72: j * 144 + g * 72 + 65],
                                                         start=(j == 0), stop=(j == NPG)), [Pd, Vs], [pa])
            mk.op("act", lambda e: e.copy(osl[:, 0:65], pa[0:4, 0:65]), [pa], [osl])
            mk.op("dve", lambda e: e.tensor_scalar(rdn[:], osl[:, 64:65], 1e-30, None, ALU.max), [osl], [rdn])
            mk.op("dve", lambda e: e.reciprocal(rdn[:], rdn[:]), [rdn], [rdn])
            mk.op("dve", lambda e: e.tensor_scalar(obr[:, 64:128], osl[:, 0:64], rdn[:, 0:1], None, ALU.mult), [osl, rdn], [obr])
            pa = pAcc[0]
            for j in range(5):
                Pd = P4[j % 2]
                sc(lambda hh, j=j: KTw3[hh * 64:(hh + 1) * 64, g, j * 128:(j + 1) * 128], wcol[:, j:j + 1], Pd, [KTw])
                mk.op("pe", lambda e, Pd=Pd, j=j: e.matmul(pa[0:4, 256:321], Pd[:, :], Vw[:, j * 144 + g * 72: j * 144 + g * 72 + 65],
                                                         start=(j == 0), stop=(j == 4)), [Pd, Vw], [pa])
            mk.op("act", lambda e: e.copy(owi[:, 0:65], pa[0:4, 256:321]), [pa], [owi])
            mk.op("dve", lambda e: e.tensor_scalar(rdn[:], owi[:, 64:65], 1e-30, None, ALU.max), [owi], [rdn])
            mk.op("dve", lambda e: e.reciprocal(rdn[:], rdn[:]), [rdn], [rdn])
            mk.op("dve", lambda e: e.tensor_scalar(obr[:, 128:192], owi[:, 0:64], rdn[:, 0:1], None, ALU.mult), [owi, rdn], [obr])
            for hh in range(2):
                for pp in range(2):
                    slot = hh * 2 + pp
                    head = 4 * g + 2 * pp + hh
                    mk.dma("sp", onsa[si:si + 1, :].rearrange("p (br hd) -> p br hd", br=3)[:, :, head * 64:(head + 1) * 64],
                           obr[slot:slot + 1, :].rearrange("p (br d) -> p br d", br=3), reads=[obr], writes=[onsa])

    if SMP_LVL < 3:
        return
    mix = mk.sb("s_mix", [128, 1024]); yo = mk.sb("s_yo", [128, 1024]); otm = sq
    on3 = onsa[:, :].rearrange("p (br h d) -> p br h d", br=3, h=8)
    g3 = gates[:, :].rearrange("p (h t) -> p h t", t=3)
    mixn = mix[:, 512:1024].rearrange("p (h d) -> p h d", h=8)
    for br in range(3):
        dst = mixn if br == 0 else otm[:, :].rearrange("p (h d) -> p h d", h=8)
        mk.op("dve", lambda e, br=br, dst=dst: e.tensor_tensor(dst, on3[:, br, :, :], g3[:, :, br].unsqueeze(2).to_broadcast([128, 8, 64]), ALU.mult),
              [onsa, gates], [mix if br == 0 else otm])
        if br > 0:
            mk.op("dve", lambda e: e.tensor_tensor(mix[:, 512:1024], mix[:, 512:1024], otm[:], ALU.add), [mix, otm], [mix])
    sg = xsq
    mk.op("act", lambda e: e.activation(sg[:, 0:512], zB[:, 0:512], AF.Silu), [zB], [sg])
    mk.op("act", lambda e: e.activation(sg[:, 512:1024], zB[:, 1024:1536], AF.Silu), [zB], [sg])
    for nh in range(2):
        p = pZ[nh]
        for G in range(4):
            mk.op("pe", lambda e, p=p, G=G, nh=nh: e.matmul(
                p[:, :], yTs[:, G * 128:(G + 1) * 128], wglu[:, G * 1024 + nh * 512: G * 1024 + (nh + 1) * 512],
                start=(G == 0), stop=(G == 3)), [yTs, wglu], [p])
    mk.op("act", lambda e: e.activation(yo[:, 512:1024], pZ[1][:, :], AF.Sigmoid), [pZ[1]], [yo])
    mk.op("dve", lambda e: e.tensor_tensor(yo[:, 0:512], pZ[0][:, :], yo[:, 512:1024], ALU.mult), [pZ[0], yo], [yo])
    mk.op("dve", lambda e: e.tensor_tensor(mix[:, 0:512], yo[:, 0:512], sg[:, 0:512], ALU.mult), [yo, sg], [mix])
    mk.op("dve", lambda e: e.tensor_tensor(mix[:, 512:1024], mix[:, 512:1024], sg[:, 512:1024], ALU.mult), [mix, sg], [mix])
    mk.op("dve", lambda e: e.tensor_copy(xn[:], mix[:]), [mix], [xn])
    for g4 in range(2):
        for j in range(4):
            kt = g4 * 4 + j
            mk.op("pe", lambda e, j=j, kt=kt: e.transpose(pT[:, j * 128:(j + 1) * 128], xn[:, kt * 128:(kt + 1) * 128], ident_b[:]),
                  [xn, ident_b], [pT])
        mk.op("act", lambda e, g4=g4: e.copy(xnT[:, g4 * 512:(g4 + 1) * 512], pT[:, 0:512]), [pT], [xnT])
    for nh in range(2):
        p = pZ[nh]
        for kt in range(8):
            mk.op("pe", lambda e, p=p, kt=kt, nh=nh: e.matmul(
                p[:, :], xnT[:, kt * 128:(kt + 1) * 128], wout[:, kt * 1024 + nh * 512: kt * 1024 + (nh + 1) * 512],
                start=(kt == 0), stop=(kt == 7)), [xnT, wout], [p])
        mk.op("dve", lambda e, p=p, nh=nh: e.tensor_tensor(yo[:, nh * 512:(nh + 1) * 512], p[:, :], xb[:, nh * 512:(nh + 1) * 512], ALU.add),
              [p, xb], [yo])
    mk.dma("sp", L["ys_o"][:, :], yo[0:4, :], reads=[yo])


TWO_PI = 6.283185307179586
HALF_PI = 1.5707963267948966
CH = 512
NT_ALL = 32
COLS_A = [(0, 512, 0), (2048, 512, 512), (2560, 256, 1024)]


def build_nc(n_tiles=NT_ALL, do_ssm=True, do_nsa=True, do_smp=True):
    nc = bass.Bass("TRN2", target_bir_lowering=False)

    def din(name, shape, dt=F32):
        return nc.dram_tensor(name, list(shape), dt, kind="ExternalInput").ap()

    def dout(name, shape, dt=F32):
        return nc.dram_tensor(name, list(shape), dt, kind="ExternalOutput").ap()

    x_all = din("x_all", [SEQ, D_MODEL])
    w_in = din("w_in", [D_MODEL, IN_W])
    norm_w = din("norm_w", [D_MODEL])
    q_norm_w = din("q_norm_w", [64])
    k_norm_w = din("k_norm_w", [3, 64])
    cs_all = din("cs_all", [SEQ, 16])
    ident_in = din("ident", [128, 128])
    tv_in = din("tvals", [SEQ])
    lam_re = din("lam_re", [2048])
    lam_im = din("lam_im", [2048])
    log_step = din("log_step", [32])
    b_re = din("b_re", [2048, 16])
    b_im = din("b_im", [2048, 16])
    c_re = din("c_re", [32, 16, 64])
    c_im = din("c_im", [32, 16, 64])
    ssm_d = din("ssm_d", [512])

    gate_b = din("gate_b", [24])
    cmp_pe = din("cmp_pe", [2, 32, 64])
    cmp_w1 = din("cmp_w1", [2, 2048, 128])
    cmp_b1 = din("cmp_b1", [2, 128])
    cmp_w2 = din("cmp_w2", [2, 128, 64])
    w_glu = din("w_glu", [512, 1024])
    w_out = din("w_out", [1024, 1024])
    ov_in = din("ov_tab", [256, 64])
    cthr_in = din("cthr", [128, 2])
    fadd_in = din("fadd", [128, 16 * 64])
    eall_in = din("eall", [64, 32 * 128])
    caus_in = din("caus", [128, 128])
    winlo_in = din("winlo", [128, 128])
    pfx_in = din("pfxrow", [1, 512])
    y_o = dout("y_o", [HALF, D_MODEL])
    x_smp = din("x_smp", [128, D_MODEL])
    cs_smp = din("cs_smp", [128, 16])
    st_re = din("st_re", [4, 2048])
    st_im = din("st_im", [4, 2048])
    cache_rows = din("cache_rows", [2560 * 128, 512])
    cache_win_in = din("cache_win_s", [4, 512, 256])
    ptab_in = din("ptab", [4, 64], I32)
    pcol_in = din("pcol", [128, 1])
    cval_in = din("cval", [128, 4])
    wcol_in = din("wcol", [128, 5])
    newcol_in = din("newcol", [128, 1])
    fadds_in = din("fadds", [1, 130])
    sel2_in = din("sel2", [1, 256])
    ovs_in = din("ovs", [512, 129])
    ys_o = dout("ys_o", [4, D_MODEL])
    wins_o = dout("wins_o", [4, 512, 256])
    kvs_o = dout("kvs_o", [4, 512])
    sres_o = dout("sres_o", [4, 2048])
    sims_o = dout("sims_o", [4, 2048])
    kv_o = dout("kv_o", [HALF, 512])
    win_o = dout("win_o", [512, 256])
    sre_o = dout("sre_o", [2048])
    sim_o = dout("sim_o", [2048])

    with ExitStack() as es:
        es.enter_context(nc.allow_low_precision("bf16 matmul operands, fp32 accumulation"))
        es.enter_context(nc.allow_non_contiguous_dma("small strided parameter loads"))
        mk = MK(nc, es)

        ident_f = mk.sb("ident_f", [128, 128], F32)
        ident_b = mk.sb("ident_b", [128, 128], BF16)
        mk.dma("sp", ident_f[:], ident_in[:, :], writes=[ident_f])
        mk.op("dve", lambda e: e.tensor_copy(ident_b[:], ident_f[:]), [ident_f], [ident_b])
        eps_t = mk.sb("eps_t", [128, 1], F32)
        mk.op("dve", lambda e: e.memset(eps_t[:], RMS_EPS), [], [eps_t])
        hpi_t = mk.sb("hpi_t", [128, 1], F32)
        mk.op("dve", lambda e: e.memset(hpi_t[:], HALF_PI), [], [hpi_t])
        nw = mk.sb("nw", [128, 8], F32)
        mk.dma("sp", nw[:], norm_w.rearrange("(t p) -> p t", p=128), writes=[nw])
        qw = mk.sb("qw", [128, 64], F32)
        mk.dma("sp", qw[:], q_norm_w.partition_broadcast(128), writes=[qw])
        kw = mk.sb("kw", [128, 3 * 64], F32)
        mk.dma("sp", kw[:], k_norm_w.rearrange("a d -> (a d)").partition_broadcast(128), writes=[kw])

        snew = mk.sb("snew", [128, 768])
        yTs = mk.sb("yTs", [128, 4 * 128], BF16)
        mk.op("pool", lambda e: e.memset(yTs[:], 0.0), [], [yTs])
        esPr = ExitStack()
        mk.es = esPr
        KT = mk.sb("KT", [128, 4 * SEQ], BF16)
        kcT = mk.sb("kcT", [128, SEQ], BF16)
        vcT = mk.sb("vcT", [128, SEQ], BF16)
        Vs = mk.sb("Vs", [128, NT_ALL * 2 * 72], BF16)
        Vw = mk.sb("Vw", [128, NT_ALL * 2 * 72], BF16)
        mk.op("pool", lambda e: e.memset(Vs[:], 1.0), [], [Vs])
        mk.op("pool", lambda e: e.memset(Vw[:], 1.0), [], [Vw])
        yT = mk.sb("yT", [128, 4 * HALF], BF16)
        tv = mk.sb("tv", [128, SEQ])
        mk.dma("sp", tv[:], tv_in.partition_broadcast(128), writes=[tv])
        esU = ExitStack()
        mk.es = esU
        UTW = SEQ + 128
        uT = mk.sb("uT", [128, 4 * UTW], BF16)

        esA = ExitStack()
        mk.es = esA
        wbA = mk.sb("wbA", [128, 8 * 1280], BF16)
        wst = [mk.sb(f"wst{i}", [128, 1280], F32) for i in range(2)]
        for dt_ in range(8):
            st = wst[dt_ % 2]
            for (c0, cw, z0) in COLS_A:
                mk.dma("sp", st[:, z0:z0 + cw], w_in[dt_ * 128:(dt_ + 1) * 128, c0:c0 + cw], writes=[st])
            mk.op("dve", lambda e, st=st, dt_=dt_: e.tensor_scalar(
                wbA[:, dt_ * 1280:(dt_ + 1) * 1280], st[:], nw[:, dt_:dt_ + 1], None, ALU.mult),
                [st, nw], [wbA])

        xt = [mk.sb(f"xt{i}", [128, D_MODEL], F32) for i in range(2)]
        xsq = mk.sb("xsq", [128, D_MODEL], F32)
        ssum = mk.sb("ssum", [128, 1], F32)
        rstd = mk.sb("rstd", [128, 1], F32)
        xn = mk.sb("xn", [128, D_MODEL], BF16)
        xnT = mk.sb("xnT", [128, 8 * 128], BF16)
        zA = [mk.sb(f"zA{i}", [128, 1280], F32) for i in range(2)]
        cs = [mk.sb(f"cs{i}", [128, 16], F32) for i in range(2)]
        sq = mk.sb("sq", [128, 6 * 64], F32)
        hss = mk.sb("hss", [128, 6], F32)
        hrs = mk.sb("hrs", [128, 6], F32)
        rt1 = mk.sb("rt1", [128, 48], F32)
        rt2 = mk.sb("rt2", [128, 48], F32)
        rt3 = mk.sb("rt3", [128, 48], F32)
        kd = mk.sb("kd", [128, 4 * 128], BF16)
        kvb = mk.sb("kvb", [128, 256], BF16)
        ptr = [mk.ps(f"ptr{i}", [128, 512], BF16) for i in range(2)]
        pz = [mk.ps(f"pz{i}", [128, 512], F32) for i in range(3)]
        pu = mk.ps("pu", [128, 512], F32)

        def rms_and_transpose(xb):
            mk.op("act", lambda e: e.activation(xsq[:], xb[:], AF.Square, accum_out=ssum[:]), [xb], [xsq, ssum])
            mk.op("act", lambda e: e.activation(rstd[:], ssum[:], AF.Sqrt, bias=eps_t[:, 0:1], scale=1.0 / D_MODEL),
                  [ssum, eps_t], [rstd])
            mk.op("dve", lambda e: e.reciprocal(rstd[:], rstd[:]), [rstd], [rstd])
            mk.op("dve", lambda e: e.tensor_scalar(xn[:], xb[:], rstd[:, 0:1], None, ALU.mult), [xb, rstd], [xn])
            for g4 in range(2):
                p = ptr[g4]
                for j in range(4):
                    dt_ = g4 * 4 + j
                    mk.op("pe", lambda e, p=p, j=j, dt_=dt_: e.transpose(
                        p[:, j * 128:(j + 1) * 128], xn[:, dt_ * 128:(dt_ + 1) * 128], ident_b[:]),
                        [xn, ident_b], [p])
                if g4 == 0:
                    mk.op("act", lambda e, p=p: e.copy(xnT[:, 0:512], p[:]), [p], [xnT])
                else:
                    mk.op("dve", lambda e, p=p: e.tensor_copy(xnT[:, 512:1024], p[:]), [p], [xnT])

        for ti in range(n_tiles + 1):
            xb, zt, cst = xt[ti % 2], zA[ti % 2], cs[ti % 2]
            r0 = ti * 128
            smp = ti == n_tiles
            own = ti >= 16 and not smp
            if smp:
                mk.dma("sp", xb[:], x_smp[:, :], writes=[xb])
                mk.dma("sp", cst[:], cs_smp[:, :], writes=[cst])
            else:
                mk.dma("sp", xb[:], x_all[r0:r0 + 128, :], writes=[xb])
                mk.dma("sp", cst[:], cs_all[r0:r0 + 128, :], writes=[cst])
            rms_and_transpose(xb)
            for ci, (c0, cw, z0) in enumerate(COLS_A):
                p = pz[ci % 3]
                for dt_ in range(8):
                    mk.op("pe", lambda e, p=p, dt_=dt_, z0=z0, cw=cw: e.matmul(
                        p[:, 0:cw], xnT[:, dt_ * 128:(dt_ + 1) * 128],
                        wbA[:, dt_ * 1280 + z0: dt_ * 1280 + z0 + cw],
                        start=(dt_ == 0), stop=(dt_ == 7)), [xnT, wbA], [p])
                if ci % 2 == 0:
                    mk.op("act", lambda e, p=p, z0=z0, cw=cw: e.copy(zt[:, z0:z0 + cw], p[:, 0:cw]), [p], [zt])
                else:
                    mk.op("dve", lambda e, p=p, z0=z0, cw=cw: e.tensor_copy(zt[:, z0:z0 + cw], p[:, 0:cw]), [p], [zt])
            kvw = zt[:, 512:1280].rearrange("p (a k g d) -> p a k g d", a=3, k=2, g=2)
            kv_ = kvw[:, :, 0, :, :]
            sqk = sq[:, :].rearrange("p (a g d) -> p a g d", a=3, g=2)
            mk.op("act", lambda e: e.activation(sqk, kv_, AF.Square), [zt], [sq])
            mk.op("dve", lambda e: e.tensor_reduce(
                hss[:, :], sq[:, :].rearrange("p (h d) -> p h d", d=64), AX.X, ALU.add), [sq], [hss])
            mk.op("act", lambda e: e.activation(hrs[:], hss[:], AF.Sqrt, bias=eps_t[:, 0:1], scale=1.0 / 64),
                  [hss, eps_t], [hrs])
            mk.op("dve", lambda e: e.reciprocal(hrs[:], hrs[:]), [hrs], [hrs])
            mk.op("dve", lambda e: e.tensor_tensor(
                kv_, kv_, hrs[:, :].rearrange("p (a g) -> p a g", g=2).unsqueeze(3).to_broadcast([128, 3, 2, 64]),
                ALU.mult), [zt, hrs], [zt])
            mk.op("dve", lambda e: e.tensor_tensor(
                kv_, kv_, kw[:, :].rearrange("p (a d) -> p a d", d=64).unsqueeze(2).to_broadcast([128, 3, 2, 64]),
                ALU.mult), [zt, kw], [zt])
            cosk = cst[:, 0:8].unsqueeze(1).unsqueeze(1).to_broadcast([128, 3, 2, 8])
            sink = cst[:, 8:16].unsqueeze(1).unsqueeze(1).to_broadcast([128, 3, 2, 8])
            a1 = rt1[:, :].rearrange("p (a g d) -> p a g d", a=3, g=2)
            a2 = rt2[:, :].rearrange("p (a g d) -> p a g d", a=3, g=2)
            a3 = rt3[:, :].rearrange("p (a g d) -> p a g d", a=3, g=2)
            x1 = kv_[..., 0:8]
            x2 = kv_[..., 8:16]
            mk.op("dve", lambda e: e.tensor_tensor(a1, x2, sink, ALU.mult), [zt, cst], [rt1])
            mk.op("dve", lambda e: e.tensor_tensor(a2, x1, sink, ALU.mult), [zt, cst], [rt2])
            mk.op("dve", lambda e: e.tensor_tensor(a3, x1, cosk, ALU.mult), [zt, cst], [rt3])
            mk.op("dve", lambda e: e.tensor_tensor(x1, a3, a1, ALU.subtract), [rt3, rt1], [zt])
            mk.op("dve", lambda e: e.tensor_tensor(a3, x2, cosk, ALU.mult), [zt, cst], [rt3])
            mk.op("dve", lambda e: e.tensor_tensor(x2, a3, a2, ALU.add), [rt3, rt2], [zt])
            if own:
                o0 = (ti - 16) * 128
                mk.dma("sp", kv_o[o0:o0 + 128, :], zt[:, 512:1024], reads=[zt])
                if ti >= 28:
                    w0 = (ti - 28) * 128
                    mk.dma("sp", win_o[w0:w0 + 128, :], zt[:, 1024:1280], reads=[zt])
            if smp:
                mk.dma("sp", kvs_o[:, :], zt[0:4, 512:1024], reads=[zt])
                mk.op("act", lambda e: e.copy(snew[:], zt[:, 512:1280]), [zt], [snew])
            for G in range(4):
                mk.op("pe", lambda e, G=G: e.transpose(pu[:, G * 128:(G + 1) * 128], zt[:, G * 128:(G + 1) * 128], ident_f[:]),
                      [zt, ident_f], [pu])
            mk.op("act", lambda e, r0=r0: e.copy(
                uT[:, :].rearrange("p (G t) -> p G t", G=4)[:, :, r0:r0 + 128],
                pu[:, :].rearrange("p (G t) -> p G t", G=4)), [pu], [uT])
            if smp:
                continue
            mk.op("pool", lambda e: e.tensor_copy(kvb[:], zt[:, 512:768]), [zt], [kvb])
            mk.op("pool", lambda e: e.tensor_copy(
                kd[:, 0:256].rearrange("p (g r d) -> p g r d", g=2, r=2),
                zt[:, 768:896].rearrange("p (g d) -> p g d", g=2).unsqueeze(2).to_broadcast([128, 2, 2, 64])), [zt], [kd])
            mk.op("pool", lambda e: e.tensor_copy(
                kd[:, 256:512].rearrange("p (g r d) -> p g r d", g=2, r=2),
                zt[:, 1024:1152].rearrange("p (g d) -> p g d", g=2).unsqueeze(2).to_broadcast([128, 2, 2, 64])), [zt], [kd])
            p = ptr[0]
            for j in range(2):
                mk.op("pe", lambda e, j=j, p=p: e.transpose(p[:, j * 128:(j + 1) * 128], kvb[:, j * 128:(j + 1) * 128], ident_b[:]),
                      [kvb, ident_b], [p])
            mk.op("dve", lambda e, p=p, r0=r0: e.tensor_copy(kcT[:, r0:r0 + 128], p[:, 0:128]), [p], [kcT])
            mk.op("dve", lambda e, p=p, r0=r0: e.tensor_copy(vcT[:, r0:r0 + 128], p[:, 128:256]), [p], [vcT])
            p = ptr[1]
            for j in range(4):
                mk.op("pe", lambda e, j=j, p=p: e.transpose(p[:, j * 128:(j + 1) * 128], kd[:, j * 128:(j + 1) * 128], ident_b[:]),
                      [kd, ident_b], [p])
            mk.op("act", lambda e, p=p, r0=r0: e.copy(
                KT[:, :].rearrange("p (j t) -> p j t", j=4)[:, :, r0:r0 + 128],
                p[:, :].rearrange("p (j t) -> p j t", j=4)), [p], [KT])
            mk.op("pool", lambda e, ti=ti: e.tensor_copy(
                Vs[:, ti * 144:(ti + 1) * 144].rearrange("p (g d) -> p g d", g=2)[:, :, 0:64],
                zt[:, 896:1024].rearrange("p (g d) -> p g d", g=2)), [zt], [Vs])
            mk.op("pool", lambda e, ti=ti: e.tensor_copy(
                Vw[:, ti * 144:(ti + 1) * 144].rearrange("p (g d) -> p g d", g=2)[:, :, 0:64],
                zt[:, 1152:1280].rearrange("p (g d) -> p g d", g=2)), [zt], [Vw])

        mk.barrier()
        esA.close()

        if do_ssm:
            esS = ExitStack()
            mk.es = esS
            P16 = [128, 16]
            lr = mk.sb("lr", P16); li = mk.sb("li", P16); ls = mk.sb("ls", P16)
            mk.dma("sp", lr[:], lam_re.rearrange("(k p) -> p k", p=128), writes=[lr])
            mk.dma("sp", li[:], lam_im.rearrange("(k p) -> p k", p=128), writes=[li])
            lsv = log_step.rearrange("(k g) -> g k", g=2)
            mk.dma("sp", ls[0:64, :], lsv[0:1, :].partition_broadcast(64) if False else lsv[0, :].partition_broadcast(64), writes=[ls])
            mk.dma("sp", ls[64:128, :], lsv[1, :].partition_broadcast(64), writes=[ls])
            dtt = mk.sb("dtt", P16); rho = mk.sb("rho", P16); s2 = mk.sb("s2", P16)
            tmpa = mk.sb("tmpa", P16); tmpb = mk.sb("tmpb", P16); tmpi = mk.sb("tmpi", P16, I32)
            sina = mk.sb("sina", P16); cosa = mk.sb("cosa", P16); sred = mk.sb("sred", P16)
            are = mk.sb("are", P16); aim = mk.sb("aim", P16); fre = mk.sb("fre", P16); fim = mk.sb("fim", P16)
            mk.op("act", lambda e: e.activation(dtt[:], ls[:], AF.Exp), [ls], [dtt])
            mk.op("dve", lambda e: e.tensor_tensor(tmpa[:], lr[:], dtt[:], ALU.mult), [lr, dtt], [tmpa])
            mk.op("act", lambda e: e.activation(rho[:], tmpa[:], AF.Exp), [tmpa], [rho])
            mk.op("dve", lambda e: e.scalar_tensor_tensor(s2[:], li[:], 1.0 / TWO_PI, dtt[:], ALU.mult, ALU.mult),
                  [li, dtt], [s2])
            mk.op("dve", lambda e: e.tensor_copy(tmpi[:], s2[:]), [s2], [tmpi])
            mk.op("dve", lambda e: e.tensor_tensor(sred[:], s2[:], tmpi[:], ALU.subtract), [s2, tmpi], [sred])
            mk.op("act", lambda e: e.activation(sina[:], sred[:], AF.Sin, scale=TWO_PI), [sred], [sina])
            mk.op("dve", lambda e: e.tensor_scalar(tmpa[:], s2[:], 0.25, None, ALU.add), [s2], [tmpa])
            mk.op("dve", lambda e: e.tensor_copy(tmpi[:], tmpa[:]), [tmpa], [tmpi])
            mk.op("dve", lambda e: e.tensor_tensor(tmpb[:], s2[:], tmpi[:], ALU.subtract), [s2, tmpi], [tmpb])
            mk.op("act", lambda e: e.activation(cosa[:], tmpb[:], AF.Sin, bias=hpi_t[:, 0:1], scale=TWO_PI),
                  [tmpb, hpi_t], [cosa])
            mk.op("dve", lambda e: e.tensor_tensor(are[:], rho[:], cosa[:], ALU.mult), [rho, cosa], [are])
            mk.op("dve", lambda e: e.tensor_tensor(aim[:], rho[:], sina[:], ALU.mult), [rho, sina], [aim])
            den = mk.sb("den", P16); nr = mk.sb("nr", P16)
            mk.op("dve", lambda e: e.tensor_tensor(den[:], lr[:], lr[:], ALU.mult), [lr], [den])
            mk.op("dve", lambda e: e.tensor_tensor(tmpa[:], li[:], li[:], ALU.mult), [li], [tmpa])
            mk.op("dve", lambda e: e.tensor_tensor(den[:], den[:], tmpa[:], ALU.add), [den, tmpa], [den])
            mk.op("dve", lambda e: e.reciprocal(den[:], den[:]), [den], [den])
            mk.op("dve", lambda e: e.tensor_scalar(nr[:], are[:], -1.0, None, ALU.add), [are], [nr])
            mk.op("dve", lambda e: e.tensor_tensor(tmpa[:], nr[:], lr[:], ALU.mult), [nr, lr], [tmpa])
            mk.op("dve", lambda e: e.tensor_tensor(tmpb[:], aim[:], li[:], ALU.mult), [aim, li], [tmpb])
            mk.op("dve", lambda e: e.tensor_tensor(tmpa[:], tmpa[:], tmpb[:], ALU.add), [tmpa, tmpb], [tmpa])
            mk.op("dve", lambda e: e.tensor_tensor(fre[:], tmpa[:], den[:], ALU.mult), [tmpa, den], [fre])
            mk.op("dve", lambda e: e.tensor_tensor(tmpa[:], aim[:], lr[:], ALU.mult), [aim, lr], [tmpa])
            mk.op("dve", lambda e: e.tensor_tensor(tmpb[:], nr[:], li[:], ALU.mult), [nr, li], [tmpb])
            mk.op("dve", lambda e: e.tensor_tensor(tmpa[:], tmpa[:], tmpb[:], ALU.subtract), [tmpa, tmpb], [tmpa])
            mk.op("dve", lambda e: e.tensor_tensor(fim[:], tmpa[:], den[:], ALU.mult), [tmpa, den], [fim])
            LB = mk.sb("LB", [128, 32 * 128], BF16)
            LC = mk.sb("LC", [128, 32 * 128], BF16)
            pS = [mk.ps(f"pS{i}", [128, 512], F32) for i in range(6)]
            esP = ExitStack()
            mk.es = esP
            bre = mk.sb("bre", [128, 256]); bim = mk.sb("bim", [128, 256])
            bbr = mk.sb("bbr", [128, 256]); bbi = mk.sb("bbi", [128, 256]); bt = mk.sb("bt", [128, 256])
            mk.dma("sp", bre[:, :].rearrange("p (k c) -> p k c", c=16), b_re.rearrange("(k p) c -> p k c", p=128), writes=[bre])
            mk.dma("sp", bim[:, :].rearrange("p (k c) -> p k c", c=16), b_im.rearrange("(k p) c -> p k c", p=128), writes=[bim])
            v3 = lambda b: b[:, :].rearrange("p (k c) -> p k c", c=16)
            fb = lambda b: b[:, :].unsqueeze(2).to_broadcast([128, 16, 16])
            mk.op("dve", lambda e: e.tensor_tensor(v3(bbr), v3(bre), fb(fre), ALU.mult), [bre, fre], [bbr])
            mk.op("dve", lambda e: e.tensor_tensor(v3(bt), v3(bim), fb(fim), ALU.mult), [bim, fim], [bt])
            mk.op("dve", lambda e: e.tensor_tensor(bbr[:], bbr[:], bt[:], ALU.subtract), [bbr, bt], [bbr])
            mk.op("dve", lambda e: e.tensor_tensor(v3(bbi), v3(bim), fb(fre), ALU.mult), [bim, fre], [bbi])
            mk.op("dve", lambda e: e.tensor_tensor(v3(bt), v3(bre), fb(fim), ALU.mult), [bre, fim], [bt])
            mk.op("dve", lambda e: e.tensor_tensor(bbi[:], bbi[:], bt[:], ALU.add), [bbi, bt], [bbi])
            Mz = mk.sb("Mz", [128, 32 * 128], F32)
            mk.op("pool", lambda e: e.memset(Mz[:], 0.0), [], [Mz])
            mk.op("pool", lambda e: e.memset(LC[:], 0.0), [], [LC])
            for k in range(16):
                for part, bb in ((0, bbr), (1, bbi)):
                    idx = k * 2 + part
                    for gl in range(2):
                        gp = (2 * k + gl) % 8
                        mk.op("dve", lambda e, idx=idx, gl=gl, gp=gp, bb=bb, k=k: e.tensor_copy(
                            Mz[gl * 64:(gl + 1) * 64, idx * 128 + gp * 16: idx * 128 + gp * 16 + 16],
                            bb[gl * 64:(gl + 1) * 64, k * 16:(k + 1) * 16]), [bb], [Mz])
            for i4 in range(8):
                p = pS[i4 % 2]
                for j in range(4):
                    idx = i4 * 4 + j
                    mk.op("pe", lambda e, p=p, j=j, idx=idx: e.transpose(
                        p[:, j * 128:(j + 1) * 128], Mz[:, idx * 128:(idx + 1) * 128], ident_f[:]), [Mz, ident_f], [p])
                mk.op("act", lambda e, p=p, i4=i4: e.copy(LB[:, i4 * 512:(i4 + 1) * 512], p[:]), [p], [LB])
            XG = [mk.sb(f"XG{i}", [128, 128], F32) for i in range(2)]
            for G in range(4):
                for part, csrc in ((0, c_re), (1, c_im)):
                    X = XG[part]
                    mk.op("pool", lambda e, X=X: e.memset(X[:], 0.0), [], [X])
                    for g8 in range(8):
                        g = G * 8 + g8
                        mk.dma("sp", X[g8 * 16:(g8 + 1) * 16, (g % 2) * 64:(g % 2) * 64 + 64], csrc[g, :, :], writes=[X])
                    p = pS[2 + part]
                    mk.op("pe", lambda e, p=p, X=X: e.transpose(p[:, 0:128], X[:], ident_f[:]), [X, ident_f], [p])
                    for k4 in range(4):
                        idx = (G * 4 + k4) * 2 + part
                        mk.op("dve", lambda e, p=p, idx=idx, k4=k4, part=part: e.tensor_scalar(
                            LC[:, idx * 128 + k4 * 32: idx * 128 + k4 * 32 + 32], p[:, k4 * 32:k4 * 32 + 32],
                            (1.0 if part == 0 else -1.0), None, ALU.mult), [p], [LC])
            mk.barrier()
            esP.close()
            mk.es = esS
            dcol = mk.sb("dcol", [128, 4])
            mk.dma("sp", dcol[:], ssm_d.rearrange("(G p) -> p G", p=128), writes=[dcol])
            qtr = mk.sb("qtr", [128, 1])
            mk.op("dve", lambda e: e.memset(qtr[:], 0.25), [], [qtr])

            h0r = mk.sb("h0r", [128, 64]); h0i = mk.sb("h0i", [128, 64])
            for si in range(4):
                mk.dma("sp", h0r[:, :].rearrange("p (k s) -> p k s", s=4)[:, :, si], st_re[si].rearrange("(k p) -> p k", p=128), writes=[h0r])
                mk.dma("sp", h0i[:, :].rearrange("p (k s) -> p k s", s=4)[:, :, si], st_im[si].rearrange("(k p) -> p k", p=128), writes=[h0i])
            h1r = mk.sb("h1r", [128, 64]); h1i = mk.sb("h1i", [128, 64])
            h1rb = mk.sb("h1rb", [128, 64], BF16); h1ib = mk.sb("h1ib", [128, 64], BF16)
            t4a = mk.sb("t4a", [128, 4]); t4b = mk.sb("t4b", [128, 4])
            for k in range(16):
                G = k // 4
                pr, pi_ = pS[0], pS[1]
                ucs = uT[:, G * UTW + SEQ: G * UTW + SEQ + 4]
                mk.op("pe", lambda e, k=k, ucs=ucs: e.matmul(pr[:, 0:4], LB[:, (2 * k) * 128:(2 * k + 1) * 128], ucs, start=True, stop=True), [LB, uT], [pr])
                mk.op("pe", lambda e, k=k, ucs=ucs: e.matmul(pi_[:, 0:4], LB[:, (2 * k + 1) * 128:(2 * k + 2) * 128], ucs, start=True, stop=True), [LB, uT], [pi_])
                ks = slice(k * 4, k * 4 + 4)
                ar = are[:, k:k + 1]; ai = aim[:, k:k + 1]
                mk.op("dve", lambda e, ks=ks, ar=ar: e.tensor_scalar(t4a[:], h0r[:, ks], ar, None, ALU.mult), [h0r, are], [t4a])
                mk.op("dve", lambda e, ks=ks, ai=ai: e.tensor_scalar(t4b[:], h0i[:, ks], ai, None, ALU.mult), [h0i, aim], [t4b])
                mk.op("dve", lambda e: e.tensor_tensor(t4a[:], t4a[:], t4b[:], ALU.subtract), [t4a, t4b], [t4a])
                mk.op("dve", lambda e, ks=ks, pr=pr: e.tensor_tensor(h1r[:, ks], t4a[:], pr[:, 0:4], ALU.add), [t4a, pr], [h1r])
                mk.op("dve", lambda e, ks=ks, ar=ar: e.tensor_scalar(t4a[:], h0i[:, ks], ar, None, ALU.mult), [h0i, are], [t4a])
                mk.op("dve", lambda e, ks=ks, ai=ai: e.tensor_scalar(t4b[:], h0r[:, ks], ai, None, ALU.mult), [h0r, aim], [t4b])
                mk.op("dve", lambda e: e.tensor_tensor(t4a[:], t4a[:], t4b[:], ALU.add), [t4a, t4b], [t4a])
                mk.op("dve", lambda e, ks=ks, pi_=pi_: e.tensor_tensor(h1i[:, ks], t4a[:], pi_[:, 0:4], ALU.add), [t4a, pi_], [h1i])
            mk.op("dve", lambda e: e.tensor_copy(h1rb[:], h1r[:]), [h1r], [h1rb])
            mk.op("dve", lambda e: e.tensor_copy(h1ib[:], h1i[:]), [h1i], [h1ib])
            for si in range(4):
                mk.dma("sp", sres_o[si].rearrange("(k p) -> p k", p=128), h1r[:, :].rearrange("p (k s) -> p k s", s=4)[:, :, si], reads=[h1r])
                mk.dma("sp", sims_o[si].rearrange("(k p) -> p k", p=128), h1i[:, :].rearrange("p (k s) -> p k s", s=4)[:, :, si], reads=[h1i])
            for G in range(4):
                py = pS[4]
                for k4 in range(4):
                    k = G * 4 + k4
                    mk.op("pe", lambda e, k=k, k4=k4: e.matmul(py[:, 0:4], LC[:, (2 * k) * 128:(2 * k + 1) * 128], h1rb[:, k * 4:k * 4 + 4],
                                                           start=(k4 == 0), stop=False), [LC, h1rb], [py])
                    mk.op("pe", lambda e, k=k, k4=k4: e.matmul(py[:, 0:4], LC[:, (2 * k + 1) * 128:(2 * k + 2) * 128], h1ib[:, k * 4:k * 4 + 4],
                                                           start=False, stop=(k4 == 3)), [LC, h1ib], [py])
                mk.op("dve", lambda e, G=G: e.scalar_tensor_tensor(
                    yTs[:, G * 128: G * 128 + 4], uT[:, G * UTW + SEQ: G * UTW + SEQ + 4], dcol[:, G:G + 1], py[:, 0:4], ALU.mult, ALU.add),
                    [uT, dcol, py], [yTs])
            carry = mk.sb("carry", [128, 32])
            mk.op("dve", lambda e: e.memset(carry[:], 0.0), [], [carry])
            fin = mk.sb("fin", [128, 32])
            NB_ = 2
            xr = [mk.sb(f"xr{i}", [128, CH]) for i in range(NB_)]
            xi = [mk.sb(f"xi{i}", [128, CH]) for i in range(NB_)]
            ki = [mk.sb(f"ki{i}", [128, CH], I32) for i in range(NB_)]
            fs = [mk.sb(f"fs{i}", [128, CH]) for i in range(NB_)]
            sn = [mk.sb(f"sn{i}", [128, CH]) for i in range(NB_)]
            cn = [mk.sb(f"cn{i}", [128, CH]) for i in range(NB_)]
            ta = [mk.sb(f"ta{i}", [128, CH]) for i in range(NB_)]
            tb = [mk.sb(f"tb{i}", [128, CH]) for i in range(NB_)]
            gr = [mk.sb(f"gr{i}", [128, CH]) for i in range(NB_)]
            gi = [mk.sb(f"gi{i}", [128, CH]) for i in range(NB_)]
            hR = mk.sb("hR", [128, 4 * CH], BF16)
            hI = mk.sb("hI", [128, 4 * CH], BF16)
            n_chunks = SEQ // CH
            units = [(c, G, k4) for c in range(n_chunks) for G in range(4) for k4 in range(4)]

            def stage1(ui):
                c, G, k4 = units[ui]
                t0 = c * CH
                k = G * 4 + k4
                u_ = ui % NB_
                pr, pi_ = pS[(ui % 2) * 2], pS[(ui % 2) * 2 + 1]
                ucols = uT[:, G * UTW + t0: G * UTW + t0 + CH]
                mk.op("pe", lambda e: e.matmul(pr[:, 0:CH], LB[:, (2 * k) * 128:(2 * k + 1) * 128], ucols, start=True, stop=True), [LB, uT], [pr])
                mk.op("pe", lambda e: e.matmul(pi_[:, 0:CH], LB[:, (2 * k + 1) * 128:(2 * k + 2) * 128], ucols, start=True, stop=True), [LB, uT], [pi_])
                XR, XI, KI, FS, SN, CN, TA, TB = xr[u_], xi[u_], ki[u_], fs[u_], sn[u_], cn[u_], ta[u_], tb[u_]
                mk.op("act", lambda e: e.copy(XR[:], pr[:, 0:CH]), [pr], [XR])
                mk.op("act", lambda e: e.copy(XI[:], pi_[:, 0:CH]), [pi_], [XI])
                tvc = tv[:, t0:t0 + CH]
                sc = sred[:, k:k + 1]
                mk.op("dve", lambda e: e.tensor_scalar(KI[:], tvc, sc, None, ALU.mult), [tv, sred], [KI])
                mk.op("dve", lambda e: e.scalar_tensor_tensor(FS[:], tvc, sc, KI[:], ALU.mult, ALU.subtract), [tv, sred, KI], [FS])
                mk.op("act", lambda e: e.activation(SN[:], FS[:], AF.Sin, scale=TWO_PI), [FS], [SN])
                mk.op("dve", lambda e: e.tensor_scalar(KI[:], tvc, sc, qtr[:, 0:1], ALU.mult, ALU.add), [tv, sred, qtr], [KI])
                mk.op("dve", lambda e: e.scalar_tensor_tensor(FS[:], tvc, sc, KI[:], ALU.mult, ALU.subtract), [tv, sred, KI], [FS])
                mk.op("act", lambda e: e.activation(CN[:], FS[:], AF.Sin, bias=hpi_t[:, 0:1], scale=TWO_PI), [FS, hpi_t], [CN])
                ER = "pool"
                mk.op(ER, lambda e: e.tensor_tensor(TA[:], CN[:], XR[:], ALU.mult), [CN, XR], [TA])
                mk.op(ER, lambda e: e.tensor_tensor(TB[:], SN[:], XI[:], ALU.mult), [SN, XI], [TB])
                mk.op(ER, lambda e: e.tensor_tensor(TA[:], TA[:], TB[:], ALU.add), [TA, TB], [TA])
                mk.op(ER, lambda e: e.tensor_tensor(TB[:], CN[:], XI[:], ALU.mult), [CN, XI], [TB])
                mk.op(ER, lambda e: e.tensor_tensor(XI[:], SN[:], XR[:], ALU.mult), [SN, XR], [XI])
                mk.op(ER, lambda e: e.tensor_tensor(TB[:], TB[:], XI[:], ALU.subtract), [TB, XI], [TB])

            def stage2(ui):
                c, G, k4 = units[ui]
                t0 = c * CH
                own = t0 >= HALF
                last = c == n_chunks - 1
                k = G * 4 + k4
                u_ = ui % NB_
                E = "dve"
                XR, XI, SN, CN, TA, TB, GR, GI = xr[u_], xi[u_], sn[u_], cn[u_], ta[u_], tb[u_], gr[u_], gi[u_]
                rb = rho[:, k:k + 1].to_broadcast([128, CH])
                mk.op(E, lambda e: e.tensor_tensor_scan(GR[:], rb, TA[:], carry[:, k:k + 1], ALU.mult, ALU.add), [rho, TA, carry], [GR])
                mk.op(E, lambda e: e.tensor_tensor_scan(GI[:], rb, TB[:], carry[:, 16 + k:17 + k], ALU.mult, ALU.add), [rho, TB, carry], [GI])
                mk.op("act", lambda e: e.copy(carry[:, k:k + 1], GR[:, CH - 1:CH]), [GR], [carry])
                mk.op("act", lambda e: e.copy(carry[:, 16 + k:17 + k], GI[:, CH - 1:CH]), [GI], [carry])
                if own:
                    hr_ = hR[:, k4 * CH:(k4 + 1) * CH]
                    hi_ = hI[:, k4 * CH:(k4 + 1) * CH]
                    mk.op(E, lambda e: e.tensor_tensor(XR[:], CN[:], GR[:], ALU.mult), [CN, GR], [XR])
                    mk.op(E, lambda e: e.tensor_tensor(XI[:], SN[:], GI[:], ALU.mult), [SN, GI], [XI])
                    mk.op(E, lambda e: e.tensor_tensor(hr_, XR[:], XI[:], ALU.subtract), [XR, XI], [hR])
                    mk.op(E, lambda e: e.tensor_tensor(XR[:], CN[:], GI[:], ALU.mult), [CN, GI], [XR])
                    mk.op(E, lambda e: e.tensor_tensor(XI[:], SN[:], GR[:], ALU.mult), [SN, GR], [XI])
                    mk.op(E, lambda e: e.tensor_tensor(hi_, XR[:], XI[:], ALU.add), [XR, XI], [hI])
                    if last:
                        Lc = CH - 1
                        mk.op(E, lambda e: e.tensor_tensor(XR[:, 0:1], CN[:, Lc:Lc + 1], GR[:, Lc:Lc + 1], ALU.mult), [CN, GR], [XR])
                        mk.op(E, lambda e: e.tensor_tensor(XI[:, 0:1], SN[:, Lc:Lc + 1], GI[:, Lc:Lc + 1], ALU.mult), [SN, GI], [XI])
                        mk.op(E, lambda e: e.tensor_tensor(fin[:, k:k + 1], XR[:, 0:1], XI[:, 0:1], ALU.subtract), [XR, XI], [fin])
                        mk.op(E, lambda e: e.tensor_tensor(XR[:, 0:1], CN[:, Lc:Lc + 1], GI[:, Lc:Lc + 1], ALU.mult), [CN, GI], [XR])
                        mk.op(E, lambda e: e.tensor_tensor(XI[:, 0:1], SN[:, Lc:Lc + 1], GR[:, Lc:Lc + 1], ALU.mult), [SN, GR], [XI])
                        mk.op(E, lambda e: e.tensor_tensor(fin[:, 16 + k:17 + k], XR[:, 0:1], XI[:, 0:1], ALU.add), [XR, XI], [fin])
                    if k4 == 3:
                        py = pS[4 + (G % 2)]
                        for kk in range(4):
                            kq = G * 4 + kk
                            mk.op("pe", lambda e, kq=kq, kk=kk: e.matmul(
                                py[:, 0:CH], LC[:, (2 * kq) * 128:(2 * kq + 1) * 128], hR[:, kk * CH:(kk + 1) * CH],
                                start=(kk == 0), stop=False), [LC, hR], [py])
                            mk.op("pe", lambda e, kq=kq, kk=kk: e.matmul(
                                py[:, 0:CH], LC[:, (2 * kq + 1) * 128:(2 * kq + 2) * 128], hI[:, kk * CH:(kk + 1) * CH],
                                start=False, stop=(kk == 3)), [LC, hI], [py])
                        o0 = t0 - HALF
                        mk.op("dve", lambda e: e.scalar_tensor_tensor(
                            yT[:, G * HALF + o0: G * HALF + o0 + CH], uT[:, G * UTW + t0: G * UTW + t0 + CH],
                            dcol[:, G:G + 1], py[:, 0:CH], ALU.mult, ALU.add), [uT, dcol, py], [yT])

            stage1(0)
            for ui in range(len(units)):
                if ui + 1 < len(units):
                    stage1(ui + 1)
                stage2(ui)
            mk.dma("sp", sre_o.rearrange("(k p) -> p k", p=128), fin[:, 0:16], reads=[fin])
            mk.dma("sp", sim_o.rearrange("(k p) -> p k", p=128), fin[:, 16:32], reads=[fin])
            mk.barrier()
            esS.close()
        esU.close()
        mk.es = esPr
        if do_nsa:
            nsa_and_mixer(locals())
        mk.barrier()
        esPr.close()
        mk.es = es
        if do_smp:
            esSm = ExitStack()
            sample_phase(locals())
            mk.barrier()
            esSm.close()
            mk.es = es

        mk.finish("sp")
        print(f"[build] instructions ~{mk.n_inst}, sems {mk.nsem}")
    return nc


def _rope_tables(pos):
    half = 8
    inv = (500000.0 ** (-np.arange(half, dtype=np.float32) / half)).astype(np.float32)
    ang = pos.astype(np.float32)[:, None] * inv[None, :]
    return np.concatenate([np.cos(ang), np.sin(ang)], axis=1).astype(np.float32)


def kernel(**inputs):
    f = lambda k: np.ascontiguousarray(np.asarray(inputs[k], dtype=np.float32)[0])
    x_prompt = np.asarray(inputs["x_prompt"], dtype=np.float32)
    cs = _rope_tables(np.arange(SEQ))
    cs_smp = np.ascontiguousarray(np.repeat(_rope_tables(np.array([8192])), 128, axis=0))
    x_sample = np.asarray(inputs["x_sample"], dtype=np.float32)
    st_re_all = np.asarray(inputs["state_ssm_re"], dtype=np.float32)[0]
    st_im_all = np.asarray(inputs["state_ssm_im"], dtype=np.float32)[0]
    common = {
        "w_in": f("w_in"), "norm_w": f("norm_w"), "q_norm_w": f("q_norm_w"), "k_norm_w": f("k_norm_w"),
        "ident": np.eye(128, dtype=np.float32), "tvals": np.arange(SEQ, dtype=np.float32),
        "lam_re": f("ssm_lam_re").reshape(2048), "lam_im": f("ssm_lam_im").reshape(2048),
        "log_step": f("ssm_log_step"), "b_re": f("ssm_b_re").reshape(2048, 16), "b_im": f("ssm_b_im").reshape(2048, 16),
        "c_re": f("ssm_c_re"), "c_im": f("ssm_c_im"), "ssm_d": f("ssm_d"),
    }
    f2 = lambda k: np.ascontiguousarray(np.asarray(inputs[k], dtype=np.float32)[0])
    common.update({
        "gate_b": f2("gate_b").reshape(24), "cmp_pe": f2("cmp_pe"), "cmp_w1": f2("cmp_w1"), "cmp_b1": f2("cmp_b1"),
        "cmp_w2": f2("cmp_w2"), "w_glu": f2("w_glu"), "w_out": f2("w_out"),
    })
    cidx = np.arange(256); nidx = np.arange(64)
    ov = ((16 * cidx[:, None] < 64 * (nidx[None, :] + 1)) & (16 * cidx[:, None] + 32 > 64 * nidx[None, :])).astype(np.float32)
    pidx = np.arange(128)
    caus = np.where(pidx[:, None] > pidx[None, :], -30000.0, 0.0).astype(np.float32)
    winlo = np.where(pidx[:, None] <= pidx[None, :], -30000.0, 0.0).astype(np.float32)
    eall = np.zeros((64, 32, 128), np.float32)
    for tk in range(32):
        for p in range(128):
            eall[2 * tk + p // 64, tk, p] = 30000.0
    common.update({"ov_tab": ov, "caus": caus, "winlo": winlo, "eall": eall.reshape(64, 32 * 128)})
    cache_rows = np.asarray(inputs["cache_kv"], dtype=np.float32)[0].reshape(2560 * 128, 512)
    cache_win = np.asarray(inputs["cache_win"], dtype=np.float32)[0].reshape(32, 512, 256)
    ptab_all = np.asarray(inputs["page_table"], dtype=np.int32)
    pp_ = np.arange(128)
    cval = np.stack([((ct * 128 + pp_) <= 510).astype(np.float32) for ct in range(4)], axis=1)
    wcol = np.zeros((128, 5), np.float32)
    wcol[0, 0] = -30000.0
    wcol[1:, 4] = -30000.0
    newcol = np.full((128, 1), -30000.0, np.float32)
    newcol[0, 0] = 0.0
    nn_ = np.arange(130)
    fadds = np.where((nn_ == 0) | (nn_ == 127) | (nn_ == 128), 1e4 + nn_, 0.0)
    fadds[129] = -1e4 - 129
    sel2 = np.concatenate([(pp_ < 64).astype(np.float32), (pp_ >= 64).astype(np.float32)])[None, :]
    c512 = np.arange(512); n129 = np.arange(129)
    ovs = ((16 * c512[:, None] < 64 * (n129[None, :] + 1)) & (16 * c512[:, None] + 32 > 64 * n129[None, :])).astype(np.float32)
    common.update({"cache_rows": cache_rows, "pcol": pp_.astype(np.float32).reshape(128, 1), "cval": np.ascontiguousarray(cval),
                   "wcol": wcol, "newcol": newcol, "fadds": fadds.astype(np.float32).reshape(1, 130), "sel2": np.ascontiguousarray(sel2),
                   "ovs": ovs})
    nc = build_nc()
    in_maps = []
    for c in range(N_CORES):
        b, h = c // 2, c % 2
        if h == 1:
            xa = np.ascontiguousarray(x_prompt[b])
            csa = cs
        else:
            xa = np.concatenate([np.zeros((HALF, D_MODEL), np.float32), x_prompt[b, :HALF]], axis=0)
            csa = np.concatenate([cs[:HALF], cs[:HALF]], axis=0)
        m = dict(common)
        cmin = 0 if h == 1 else 128
        cc = np.arange(256)
        cth = np.where((cc >= cmin) & (cc < 255), 16.0 * cc + 31.0, 1e9).astype(np.float32)
        m["cthr"] = np.ascontiguousarray(cth.reshape(2, 128).T)
        nv0 = 0 if h == 1 else 32
        fa = np.zeros((128, 16, 64), np.float32)
        for i in range(16):
            vq = (16 + i) * 128 + np.arange(128)
            qb_ = vq // 64
            nn = np.arange(64)
            valid = (nn[None, :] <= qb_[:, None]) & (nn[None, :] >= nv0)
            forced = (nn[None, :] == nv0) | (nn[None, :] >= qb_[:, None] - 1)
            fa[:, i, :] = np.where(valid, np.where(forced, 1e4 + nn[None, :], 0.0), -1e4 - nn[None, :])
        m["fadd"] = fa.reshape(128, 16 * 64)
        m["pfxrow"] = np.full((1, 512), 0.0 if h == 1 else -30000.0, np.float32)
        xs_ = np.zeros((128, D_MODEL), np.float32)
        xs_[0:4] = x_sample[4 * c:4 * c + 4, 0]
        m["cache_win_s"] = np.ascontiguousarray(cache_win[4 * c:4 * c + 4])
        m["ptab"] = np.ascontiguousarray(ptab_all[4 * c:4 * c + 4])
        m["x_smp"] = xs_
        m["cs_smp"] = cs_smp
        m["st_re"] = np.ascontiguousarray(st_re_all[4 * c:4 * c + 4].reshape(4, 2048))
        m["st_im"] = np.ascontiguousarray(st_im_all[4 * c:4 * c + 4].reshape(4, 2048))
        m["x_all"] = xa
        m["cs_all"] = np.ascontiguousarray(csa)
        in_maps.append(m)
    res = run_bass_kernel_spmd(nc, in_maps, core_ids=list(range(N_CORES)))
    R = res.results
    y_prompt = np.zeros((4, SEQ, D_MODEL), np.float32)
    kv_prompt = np.zeros((1, 4, SEQ, 4, 2, 64), np.float32)
    win_prompt = np.zeros((1, 4, 512, 2, 2, 64), np.float32)
    sre = np.zeros((1, 4, 32, 64), np.float32)
    sim = np.zeros((1, 4, 32, 64), np.float32)
    y_sample = np.zeros((32, 1, D_MODEL), np.float32)
    kv_sample = np.zeros((1, 32, 1, 4, 2, 64), np.float32)
    win_sample = np.zeros((1, 32, 512, 2, 2, 64), np.float32)
    sre_s = np.zeros((1, 32, 32, 64), np.float32)
    sim_s = np.zeros((1, 32, 32, 64), np.float32)
    for c in range(N_CORES):
        b, h = c // 2, c % 2
        kv_sample[0, 4 * c:4 * c + 4, 0] = R[c]["kvs_o"].reshape(4, 4, 2, 64)
        y_sample[4 * c:4 * c + 4, 0] = R[c]["ys_o"]
        win_sample[0, 4 * c:4 * c + 4] = R[c]["wins_o"].reshape(4, 512, 2, 2, 64)
        sre_s[0, 4 * c:4 * c + 4] = R[c]["sres_o"].reshape(4, 32, 64)
        sim_s[0, 4 * c:4 * c + 4] = R[c]["sims_o"].reshape(4, 32, 64)
        y_prompt[b, h * HALF:(h + 1) * HALF] = R[c]["y_o"]
        kv_prompt[0, b, h * HALF:(h + 1) * HALF] = R[c]["kv_o"].reshape(HALF, 4, 2, 64)
        if h == 1:
            win_prompt[0, b] = R[c]["win_o"].reshape(512, 2, 2, 64)
            sre[0, b] = R[c]["sre_o"].reshape(32, 64)
            sim[0, b] = R[c]["sim_o"].reshape(32, 64)
    return (y_prompt, y_sample, kv_prompt, kv_sample, win_prompt, win_sample, sre, sim, sre_s, sim_s)
```

```python
import numpy as np
from contextlib import ExitStack
import concourse.bass as bass
import concourse.mybir as mybir
from concourse.bass_utils import run_bass_kernel_spmd

F32 = mybir.dt.float32
BF16 = mybir.dt.bfloat16
I32 = mybir.dt.int32
ALU = mybir.AluOpType
AF = mybir.ActivationFunctionType
AX = mybir.AxisListType

D_MODEL = 1024
IN_W = 2840
N_CORES = 8
SEQ = 4096
HALF = 2048
RMS_EPS = 1e-6


class Buf:
    __slots__ = ("t", "w", "r", "name")

    def __init__(self, t, name=""):
        self.t = t
        self.w = None
        self.r = {}
        self.name = name

    def __getitem__(self, k):
        return self.t[k]


class MK:
    CAP = 30000

    def __init__(self, nc, es, n_dma_sems=40):
        self.nc, self.es = nc, es
        self.es_sem = es
        self.q = {"pe": nc.tensor, "act": nc.scalar, "dve": nc.vector,
                  "pool": nc.gpsimd, "sp": nc.sync}
        self.cur = {}
        self.waited = {k: {} for k in self.q}
        self.nsem = 0
        self.dma_pool = [[self.new_sem(), 0] for _ in range(n_dma_sems)]
        self.dma_rr = 0
        self.n_inst = 0

    def new_sem(self):
        s = self.es_sem.enter_context(self.nc.semaphore(f"s{self.nsem}"))
        self.nsem += 1
        return s

    def sb(self, name, shape, dt=F32):
        return Buf(self.es.enter_context(self.nc.sbuf_tensor("S_" + name, list(shape), dt)), name)

    def ps(self, name, shape, dt=F32):
        return Buf(self.es.enter_context(self.nc.psum_tensor("P_" + name, list(shape), dt)), name)

    def _wait(self, q, tok):
        sem, val, src = tok
        if src == "pe" and q == "pe":
            return
        w = self.waited[q]
        if w.get(id(sem), 0) >= val:
            return
        self.q[q].wait_ge(sem, val)
        w[id(sem)] = val
        self.n_inst += 1

    def _deps(self, q, reads, writes):
        for b in reads:
            if b.w is not None:
                self._wait(q, b.w)
        for b in writes:
            if b.w is not None:
                self._wait(q, b.w)
            for t in b.r.values():
                self._wait(q, t)

    def _mark(self, tok, reads, writes):
        for b in reads:
            b.r[id(tok[0])] = tok
        for b in writes:
            b.w = tok
            b.r = {}

    def op(self, q, fn, reads=(), writes=()):
        self._deps(q, reads, writes)
        inst = fn(self.q[q])
        c = self.cur.get(q)
        if c is None or c[1] >= self.CAP:
            c = [self.new_sem(), 0]
            self.cur[q] = c
        c[1] += 1
        inst.then_inc(c[0], 1)
        tok = (c[0], c[1], q)
        self._mark(tok, reads, writes)
        self.n_inst += 1
        return tok

    def dma(self, q, out, in_, reads=(), writes=(), **kw):
        self._deps(q, reads, writes)
        slot = self.dma_pool[self.dma_rr]
        self.dma_rr = (self.dma_rr + 1) % len(self.dma_pool)
        if slot[1] > 0:
            self._wait(q, (slot[0], slot[1], "dma"))
        inst = self.q[q].dma_start(out=out, in_=in_, **kw)
        slot[1] += 16
        inst.then_inc(slot[0], 16)
        tok = (slot[0], slot[1], "dma")
        self._mark(tok, reads, writes)
        self.n_inst += 1
        return tok

    def barrier(self):
        toks = [(c[0], c[1], q) for q, c in self.cur.items()]
        toks += [(sl[0], sl[1], "dma") for sl in self.dma_pool if sl[1] > 0]
        for q in self.q:
            for t in toks:
                sem, val, src = t
                w = self.waited[q]
                if w.get(id(sem), 0) >= val:
                    continue
                self.q[q].wait_ge(sem, val)
                w[id(sem)] = val
                self.n_inst += 1

    def finish(self, q="sp"):
        for slot in self.dma_pool:
            if slot[1] > 0:
                self._wait(q, (slot[0], slot[1], "dma"))


COLS_B = [(512, 512, 0), (1024, 512, 512), (1536, 512, 1024), (2816, 24, 1536)]
NEG = 30000.0
NSA_STOP = 99
NSA_LVL = 99
NSA_SUB = 99


def nsa_and_mixer(L):
    mk = L["mk"]; nc = L["nc"]
    ident_b, ident_f, eps_t = L["ident_b"], L["ident_f"], L["eps_t"]
    nw, qw, tv = L["nw"], L["qw"], L["tv"]
    KT, kcT, vcT, Vs, Vw, yT = L["KT"], L["kcT"], L["vcT"], L["Vs"], L["Vw"], L["yT"]
    w_in, x_all, cs_all = L["w_in"], L["x_all"], L["cs_all"]

    wbB = mk.sb("wbB", [128, 8 * 1560], BF16)
    wglu = mk.sb("wglu", [128, 4 * 1024], BF16)
    wout = mk.sb("wout", [128, 8 * 1024], BF16)
    gb = mk.sb("gb", [128, 24])
    mk.dma("sp", gb[:], L["gate_b"].partition_broadcast(128), writes=[gb])
    fadd = mk.sb("fadd", [128, 16 * 64])
    mk.dma("sp", fadd[:], L["fadd_in"][:, :], writes=[fadd])
    cthr = mk.sb("cthr", [128, 2])
    mk.dma("sp", cthr[:], L["cthr_in"][:, :], writes=[cthr])
    zeros_b = mk.sb("zeros_b", [128, 128], BF16)
    mk.op("dve", lambda e: e.memset(zeros_b[:], 0.0), [], [zeros_b])
    Eall = mk.sb("Eall", [128, 32 * 128], BF16)
    mk.op("pool", lambda e: e.memset(Eall[64:128, :], 0.0), [], [Eall])
    caus4 = mk.sb("caus4", [128, 512], BF16)
    winlo4 = mk.sb("winlo4", [128, 512], BF16)
    pfxrow = mk.sb("pfxrow", [128, 512], BF16)
    kcTc = mk.sb("kcTc", [128, 2 * 256], BF16)
    vcA = mk.sb("vcA", [128, 2 * 2 * 136], BF16)
    b1c = mk.sb("b1c", [128, 2])
    mk.dma("sp", b1c[:], L["cmp_b1"].rearrange("k h -> h k"), writes=[b1c])

    esT = ExitStack()
    mk.es = esT
    stg = [mk.sb(f"stgB{i}", [128, 1560], F32) for i in range(2)]
    for dt_ in range(8):
        st = stg[dt_ % 2]
        for (c0, cw, z0) in COLS_B:
            mk.dma("sp", st[:, z0:z0 + cw], w_in[dt_ * 128:(dt_ + 1) * 128, c0:c0 + cw], writes=[st])
        mk.op("dve", lambda e, st=st, dt_=dt_: e.tensor_scalar(
            wbB[:, dt_ * 1560:(dt_ + 1) * 1560], st[:], nw[:, dt_:dt_ + 1], None, ALU.mult), [st, nw], [wbB])
    for G in range(4):
        st = stg[G % 2]
        mk.dma("sp", st[:, 0:1024], L["w_glu"][G * 128:(G + 1) * 128, :], writes=[st])
        mk.op("dve", lambda e, st=st, G=G: e.tensor_copy(wglu[:, G * 1024:(G + 1) * 1024], st[:, 0:1024]), [st], [wglu])
    for kt in range(8):
        st = stg[kt % 2]
        mk.dma("sp", st[:, 0:1024], L["w_out"][kt * 128:(kt + 1) * 128, :], writes=[st])
        mk.op("dve", lambda e, st=st, kt=kt: e.tensor_copy(wout[:, kt * 1024:(kt + 1) * 1024], st[:, 0:1024]), [st], [wout])
    for hh in range(4):
        st = stg[hh % 2]
        mk.dma("sp", st[0:64, 0:1024], L["eall_in"][:, hh * 1024:(hh + 1) * 1024], writes=[st])
        mk.op("dve", lambda e, st=st, hh=hh: e.tensor_copy(Eall[0:64, hh * 1024:(hh + 1) * 1024], st[0:64, 0:1024]), [st], [Eall])
    for (src, dst) in ((L["caus_in"], caus4), (L["winlo_in"], winlo4)):
        st = stg[0]
        mk.dma("sp", st[:, 0:128], src[:, :], writes=[st])
        mk.op("dve", lambda e, st=st, dst=dst: e.tensor_copy(
            dst[:, :].rearrange("p (r q) -> p r q", r=4), st[:, 0:128].unsqueeze(1).to_broadcast([128, 4, 128])), [st], [dst])
    st = stg[1]
    mk.dma("sp", st[:, 0:512], L["pfx_in"].rearrange("a n -> (a n)").partition_broadcast(128), writes=[st])
    mk.op("dve", lambda e, st=st: e.tensor_copy(pfxrow[:], st[:, 0:512]), [st], [pfxrow])
    st = stg[0]
    mk.dma("sp", st[:, 0:128].rearrange("p (ct n) -> p ct n", ct=2), L["ov_in"].rearrange("(ct p) n -> p ct n", p=128), writes=[st])
    vcA4 = vcA[:, :].rearrange("p (g ct w) -> p g ct w", g=2, ct=2)
    mk.op("dve", lambda e: e.memset(vcA[:], 1.0), [], [vcA])
    for g in range(2):
        mk.op("dve", lambda e, g=g, st=st: e.tensor_copy(
            vcA4[:, g, :, 65:129], st[:, 0:128].rearrange("p (ct n) -> p ct n", ct=2)), [st], [vcA])

    pC = [mk.ps(f"pC{i}", [128, 512], F32) for i in range(2)]
    W1 = mk.sb("W1c", [128, 32 * 128], BF16)
    w2b = mk.sb("w2b", [128, 128], BF16)
    peT = mk.sb("peT", [128, 32])
    kA = mk.sb("kA", [128, SEQ], BF16)
    kB = mk.sb("kB", [128, SEQ], BF16)
    hidT = mk.sb("hidT", [128, 256], BF16)
    w1s = stg
    mk.op("dve", lambda e: e.memset(kcTc[:], 0.0), [], [kcTc])
    mk.op("dve", lambda e: e.memset(hidT[:], 0.0), [], [hidT])
    for kind, src in ((0, kcT), (1, vcT)):
        for half in range(2):
            mk.dma("sp", peT[half * 64:(half + 1) * 64, :], L["cmp_pe"][kind].rearrange("j d -> d j"), writes=[peT])
        for jq in range(4):
            ws = w1s[jq % 2]
            for half in range(2):
                mk.dma("sp", ws[half * 64:(half + 1) * 64, 0:1024].rearrange("p (j h) -> p j h", j=8),
                       L["cmp_w1"][kind].rearrange("(j d) h -> d j h", d=64)[:, jq * 8:(jq + 1) * 8, :], writes=[ws])
            mk.op("dve", lambda e, ws=ws, jq=jq: e.tensor_copy(W1[:, jq * 1024:(jq + 1) * 1024], ws[:, 0:1024]), [ws], [W1])
        st = stg[1]
        mk.dma("sp", st[:, 0:64], L["cmp_w2"][kind], writes=[st])
        mk.op("dve", lambda e, st=st: e.tensor_copy(
            w2b[:, :].rearrange("p (r d) -> p r d", r=2), st[:, 0:64].unsqueeze(1).to_broadcast([128, 2, 64])), [st], [w2b])
        sv = src[:, :].rearrange("p (i j) -> p i j", j=16)
        mk.op("dve", lambda e, sv=sv: e.tensor_tensor(
            kA[:, :].rearrange("p (i j) -> p i j", j=16), sv, peT[:, 0:16].unsqueeze(1).to_broadcast([128, 256, 16]), ALU.add),
            [src, peT], [kA])
        mk.op("pool", lambda e, sv=sv: e.tensor_tensor(
            kB[:, :].rearrange("p (i j) -> p i j", j=16), sv, peT[:, 16:32].unsqueeze(1).to_broadcast([128, 256, 16]), ALU.add),
            [src, peT], [kB])
        for g in range(2):
            ph = pC[0]
            lo, hi = g * 64, g * 64 + 64
            for j in range(32):
                if j < 16:
                    rhs = kA[lo:hi, :].rearrange("p (i j) -> p i j", j=16)[:, 0:255, j]
                else:
                    rhs = kB[lo:hi, :].rearrange("p (i j) -> p i j", j=16)[:, 1:256, j - 16]
                mk.op("pe", lambda e, ph=ph, j=j, rhs=rhs, lo=lo, hi=hi: e.matmul(
                    ph[:, 0:255], W1[lo:hi, j * 128:(j + 1) * 128], rhs, start=(j == 0), stop=(j == 31)),
                    [W1, kA, kB], [ph])
            mk.op("act", lambda e, ph=ph, kind=kind: e.activation(
                hidT[:, 0:255], ph[:, 0:255], AF.Gelu_apprx_tanh, bias=b1c[:, kind:kind + 1]), [ph, b1c], [hidT])
            po = pC[1]
            if kind == 0:
                mk.op("pe", lambda e, po=po: e.matmul(po[:, 0:256], w2b[:, :], hidT[:, :], start=True, stop=True), [w2b, hidT], [po])
                mk.op("dve", lambda e, po=po, g=g: e.tensor_copy(kcTc[:, g * 256: g * 256 + 255], po[:, 0:255]), [po], [kcTc])
            else:
                for ct in range(2):
                    mk.op("pe", lambda e, po=po, ct=ct: e.matmul(
                        po[:, ct * 64:(ct + 1) * 64], hidT[:, ct * 128:(ct + 1) * 128], w2b[:, 0:64], start=True, stop=True),
                        [hidT, w2b], [po])
                mk.op("dve", lambda e, po=po, g=g: e.tensor_copy(
                    vcA4[:, g, :, 0:64], po[:, 0:128].rearrange("p (ct d) -> p ct d", ct=2)), [po], [vcA])
    mk.barrier()
    esT.close()
    mk.es = L["esPr"]
    if NSA_STOP <= 1:
        return

    xb = mk.sb("xb_B", [128, D_MODEL]); xsq = mk.sb("xsq_B", [128, D_MODEL])
    ssum = mk.sb("ssum_B", [128, 1]); rstd = mk.sb("rstd_B", [128, 1])
    xn = mk.sb("xn_B", [128, D_MODEL], BF16); xnT = mk.sb("xnT_B", [128, 1024], BF16)
    zB = mk.sb("zB", [128, 1560]); cst = mk.sb("cs_B", [128, 16])
    sq = mk.sb("sq_B", [128, 512]); hss = mk.sb("hss_B", [128, 8]); hrs = mk.sb("hrs_B", [128, 8])
    r1 = mk.sb("r1_B", [128, 64]); r2 = mk.sb("r2_B", [128, 64]); r3 = mk.sb("r3_B", [128, 64])
    qb = mk.sb("qb_B", [128, 512], BF16)
    QT = mk.sb("QT_B", [128, 2 * 512], BF16)
    mk.op("pool", lambda e: e.memset(QT[:], 0.0), [], [QT])

    gates = mk.sb("gates_B", [128, 24])
    Pt = [mk.sb(f"Pt{i}", [128, 512], BF16) for i in range(2)]
    mcm = mk.sb("mcm", [128, 128], BF16)
    ocA = mk.sb("ocA", [128, 4 * 129]); osl = mk.sb("osl", [128, 4 * 65]); owi = mk.sb("owi", [128, 4 * 65])
    rd = mk.sb("rd_B", [128, 4]); imp = mk.sb("imp_B", [128, 64]); impt = mk.sb("impt_B", [128, 4 * 64])
    m8 = mk.sb("m8_B", [128, 16]); wk = mk.sb("wk_B", [128, 64]); thr = mk.sb("thr_B", [128, 1])
    nm = mk.sb("nm_B", [128, 64]); nmT4 = mk.sb("nmT4", [128, 512], BF16)
    mk.op("pool", lambda e: e.memset(nmT4[:], 0.0), [], [nmT4])
    mix = mk.sb("mix_B", [128, 1024]); mixb = xn; mixT = xnT
    sg = xsq; yo = zB; ab = yo
    otmp = impt
    pZ = [mk.ps(f"pZ_B{i}", [128, 512], F32) for i in range(2)]
    pSc = [mk.ps(f"pSc{i}", [128, 512], F32) for i in range(2)]
    pO = [mk.ps(f"pO{i}", [128, 512], F32) for i in range(2)]
    pM = mk.ps("pM_B", [128, 512], F32)
    pT = mk.ps("pT_B", [128, 512], BF16)
    KT3 = KT[:, :].rearrange("p (j t) -> p j t", j=4)
    sc_i = [0]

    def scores(g, kt_ap, extra, P):
        ps = pSc[sc_i[0] % 2]
        sc_i[0] += 1
        mk.op("pe", lambda e: e.matmul(ps[:, :], kt_ap, QT[:, g * 512:(g + 1) * 512],
                                       start=True, stop=(len(extra) == 0)), [KT, kcTc, QT], [ps])
        for ei, (lh, rh, deps) in enumerate(extra):
            mk.op("pe", lambda e, lh=lh, rh=rh, ei=ei: e.matmul(
                ps[:, :], lh, rh, start=False, stop=(ei == len(extra) - 1)), deps, [ps])
        mk.op("act", lambda e: e.activation(P[:], ps[:], AF.Exp), [ps], [P])

    oT = sq

    def pv_finish(po, ob):
        mk.op("act", lambda e: e.copy(oT[0:65, :], po[0:65, :]), [po], [oT])
        for sl in range(4):
            mk.op("pe", lambda e, sl=sl: e.transpose(pM[:, sl * 128: sl * 128 + 65], oT[0:65, sl * 128:(sl + 1) * 128], ident_f[0:65, 0:65]),
                  [oT, ident_f], [pM])
        mk.op("act", lambda e: e.copy(ob[:, :].rearrange("p (s w) -> p s w", s=4),
                                      pM[:, :].rearrange("p (s w) -> p s w", s=4)[:, :, 0:65]), [pM], [ob])

    for i in range(16 if NSA_STOP > 2 else 1):
        T = 16 + i
        r0 = T * 128
        mk.dma("sp", xb[:], x_all[r0:r0 + 128, :], writes=[xb])
        mk.dma("sp", cst[:], cs_all[r0:r0 + 128, :], writes=[cst])
        mk.op("act", lambda e: e.activation(xsq[:], xb[:], AF.Square, accum_out=ssum[:]), [xb], [xsq, ssum])
        mk.op("act", lambda e: e.activation(rstd[:], ssum[:], AF.Sqrt, bias=eps_t[:, 0:1], scale=1.0 / D_MODEL), [ssum, eps_t], [rstd])
        mk.op("dve", lambda e: e.reciprocal(rstd[:], rstd[:]), [rstd], [rstd])
        mk.op("dve", lambda e: e.tensor_scalar(xn[:], xb[:], rstd[:, 0:1], None, ALU.mult), [xb, rstd], [xn])
        for g4 in range(2):
            for j in range(4):
                dt_ = g4 * 4 + j
                mk.op("pe", lambda e, j=j, dt_=dt_: e.transpose(pT[:, j * 128:(j + 1) * 128], xn[:, dt_ * 128:(dt_ + 1) * 128], ident_b[:]),
                      [xn, ident_b], [pT])
            mk.op("act", lambda e, g4=g4: e.copy(xnT[:, g4 * 512:(g4 + 1) * 512], pT[:]), [pT], [xnT])
        for ci, (c0, cw, z0) in enumerate(COLS_B):
            p = pZ[ci % 2]
            for dt_ in range(8):
                mk.op("pe", lambda e, p=p, dt_=dt_, z0=z0, cw=cw: e.matmul(
                    p[:, 0:cw], xnT[:, dt_ * 128:(dt_ + 1) * 128], wbB[:, dt_ * 1560 + z0: dt_ * 1560 + z0 + cw],
                    start=(dt_ == 0), stop=(dt_ == 7)), [xnT, wbB], [p])
            mk.op("act" if ci % 2 == 0 else "dve",
                  (lambda e, p=p, z0=z0, cw=cw: e.copy(zB[:, z0:z0 + cw], p[:, 0:cw])) if ci % 2 == 0 else
                  (lambda e, p=p, z0=z0, cw=cw: e.tensor_copy(zB[:, z0:z0 + cw], p[:, 0:cw])), [p], [zB])
        qv = zB[:, 512:1024].rearrange("p (h d) -> p h d", d=64)
        mk.op("act", lambda e: e.activation(sq[:, :].rearrange("p (h d) -> p h d", d=64), qv, AF.Square), [zB], [sq])
        mk.op("dve", lambda e: e.tensor_reduce(hss[:, :], sq[:, :].rearrange("p (h d) -> p h d", d=64), AX.X, ALU.add), [sq], [hss])
        mk.op("act", lambda e: e.activation(hrs[:], hss[:], AF.Sqrt, bias=eps_t[:, 0:1], scale=1.0 / 64), [hss, eps_t], [hrs])
        mk.op("dve", lambda e: e.reciprocal(hrs[:], hrs[:]), [hrs], [hrs])
        mk.op("dve", lambda e: e.tensor_tensor(qv, qv, hrs[:, :].unsqueeze(2).to_broadcast([128, 8, 64]), ALU.mult), [zB, hrs], [zB])
        mk.op("dve", lambda e: e.scalar_tensor_tensor(qv, qv, 0.125, qw[:, :].unsqueeze(1).to_broadcast([128, 8, 64]), ALU.mult, ALU.mult),
              [zB, qw], [zB])
        cosq = cst[:, 0:8].unsqueeze(1).to_broadcast([128, 8, 8])
        sinq = cst[:, 8:16].unsqueeze(1).to_broadcast([128, 8, 8])
        a1 = r1[:, :].rearrange("p (h d) -> p h d", d=8); a2 = r2[:, :].rearrange("p (h d) -> p h d", d=8)
        a3 = r3[:, :].rearrange("p (h d) -> p h d", d=8)
        x1 = qv[..., 0:8]; x2 = qv[..., 8:16]
        mk.op("dve", lambda e: e.tensor_tensor(a1, x2, sinq, ALU.mult), [zB, cst], [r1])
        mk.op("dve", lambda e: e.tensor_tensor(a2, x1, sinq, ALU.mult), [zB, cst], [r2])
        mk.op("dve", lambda e: e.tensor_tensor(a3, x1, cosq, ALU.mult), [zB, cst], [r3])
        mk.op("dve", lambda e: e.tensor_tensor(x1, a3, a1, ALU.subtract), [r3, r1], [zB])
        mk.op("dve", lambda e: e.tensor_tensor(a3, x2, cosq, ALU.mult), [zB, cst], [r3])
        mk.op("dve", lambda e: e.tensor_tensor(x2, a3, a2, ALU.add), [r3, r2], [zB])
        mk.op("dve", lambda e: e.tensor_copy(qb[:], zB[:, 512:1024]), [zB], [qb])
        for j in range(4):
            mk.op("pe", lambda e, j=j: e.transpose(pT[:, j * 128:(j + 1) * 128], qb[:, j * 128:(j + 1) * 128], ident_b[:]), [qb, ident_b], [pT])
        QTv = QT[:, :].rearrange("p (g hh pp q) -> p g hh pp q", g=2, hh=2, pp=2)
        pTv = pT[:, :].rearrange("p (g pp q) -> p g pp q", g=2, pp=2)
        mk.op("act", lambda e: e.copy(QTv[0:64, :, 0, :, :], pTv[0:64, :, :, :]), [pT], [QT])
        mk.op("act", lambda e: e.copy(QTv[64:128, :, 1, :, :], pTv[64:128, :, :, :]), [pT], [QT])
        mk.op("dve", lambda e: e.tensor_tensor(gates[:], zB[:, 1536:1560], gb[:], ALU.add), [zB, gb], [gates])
        mk.op("act", lambda e: e.activation(gates[:], gates[:], AF.Sigmoid), [gates], [gates])
        qrow = tv[:, r0:r0 + 128]

        if NSA_LVL < 1:
            continue
        for g in range(2):
            gview = gates[:, g * 12:(g + 1) * 12].rearrange("p (pp hh t) -> p hh pp t", pp=2, hh=2)
            mixv = mix[:, 512 + g * 256: 512 + (g + 1) * 256].rearrange("p (pp hh d) -> p hh pp d", pp=2, hh=2)
            poA, poB = pO[0], pO[1]
            for ct in range(2):
                P = Pt[ct % 2]
                scores(g, kcTc[:, g * 256 + ct * 128: g * 256 + (ct + 1) * 128], [], P)
                mk.op("dve", lambda e, ct=ct: e.tensor_scalar(mcm[:], qrow, cthr[:, ct:ct + 1], None, ALU.is_ge), [tv, cthr], [mcm])
                mk.op("dve", lambda e, P=P: e.tensor_tensor(
                    P[:, :].rearrange("p (s q) -> p s q", s=4), P[:, :].rearrange("p (s q) -> p s q", s=4),
                    mcm[:, :].unsqueeze(1).to_broadcast([128, 4, 128]), ALU.mult), [P, mcm], [P])
                mk.op("pe", lambda e, P=P, ct=ct: e.matmul(poA[0:65, :], vcA4[:, g, ct, 0:65], P[:, :], start=(ct == 0), stop=(ct == 1)),
                      [P, vcA], [poA])
                mk.op("pe", lambda e, P=P, ct=ct: e.matmul(poB[0:64, :], vcA4[:, g, ct, 65:129], P[:, :], start=(ct == 0), stop=(ct == 1)),
                      [P, vcA], [poB])
            oc3w = ocA[:, :].rearrange("p (s w) -> p s w", s=4)
            mk.op("act", lambda e: e.copy(oT[0:65, :], poA[0:65, :]), [poA], [oT])
            for sl in range(4):
                mk.op("pe", lambda e, sl=sl: e.transpose(pM[:, sl * 128: sl * 128 + 65], oT[0:65, sl * 128:(sl + 1) * 128], ident_f[0:65, 0:65]),
                      [oT, ident_f], [pM])
            mk.op("act", lambda e: e.copy(oc3w[:, :, 0:65], pM[:, :].rearrange("p (s w) -> p s w", s=4)[:, :, 0:65]), [pM], [ocA])
            mk.op("act", lambda e: e.copy(oT[0:64, :], poB[0:64, :]), [poB], [oT])
            for sl in range(4):
                mk.op("pe", lambda e, sl=sl: e.transpose(pM[:, sl * 128: sl * 128 + 64], oT[0:64, sl * 128:(sl + 1) * 128], ident_f[0:64, 0:64]),
                      [oT, ident_f], [pM])
            mk.op("act", lambda e: e.copy(oc3w[:, :, 65:129], pM[:, :].rearrange("p (s w) -> p s w", s=4)[:, :, 0:64]), [pM], [ocA])
            oc3 = ocA[:, :].rearrange("p (s w) -> p s w", s=4)
            mk.op("dve", lambda e: e.tensor_scalar(rd[:], oc3[:, :, 64], 1e-30, None, ALU.max), [ocA], [rd])
            mk.op("dve", lambda e: e.reciprocal(rd[:], rd[:]), [rd], [rd])
            mk.op("dve", lambda e: e.tensor_tensor(
                impt[:, :].rearrange("p (s n) -> p s n", s=4), oc3[:, :, 65:129], rd[:, :].unsqueeze(2).to_broadcast([128, 4, 64]), ALU.mult),
                [ocA, rd], [impt])
            mk.op("dve", lambda e: e.tensor_reduce(imp[:], impt[:, :].rearrange("p (s n) -> p n s", s=4), AX.X, ALU.add), [impt], [imp])
            mk.op("dve", lambda e: e.tensor_tensor(
                rd[:, :].rearrange("p (hh pp) -> p hh pp", hh=2), rd[:, :].rearrange("p (hh pp) -> p hh pp", hh=2), gview[:, :, :, 0], ALU.mult),
                [rd, gates], [rd])
            mk.op("dve", lambda e: e.tensor_tensor(
                mixv, oc3[:, :, 0:64].rearrange("p (hh pp) d -> p hh pp d", hh=2),
                rd[:, :].rearrange("p (hh pp) -> p hh pp", hh=2).unsqueeze(3).to_broadcast([128, 2, 2, 64]), ALU.mult), [ocA, rd], [mix])
            if NSA_LVL < 2:
                continue
            mk.op("dve", lambda e, i=i: e.tensor_tensor(imp[:], imp[:], fadd[:, i * 64:(i + 1) * 64], ALU.add), [imp, fadd], [imp])
            mk.op("dve", lambda e: e.max(out=m8[:, 0:8], in_=imp[:]), [imp], [m8])
            mk.op("dve", lambda e: e.match_replace(out=wk[:], in_to_replace=m8[:, 0:8], in_values=imp[:], imm_value=-1e9), [imp, m8], [wk])
            mk.op("dve", lambda e: e.max(out=m8[:, 8:16], in_=wk[:]), [wk], [m8])
            mk.op("dve", lambda e: e.tensor_scalar(thr[:], m8[:, 15:16], -5000.0, None, ALU.max), [m8], [thr])
            mk.op("dve", lambda e: e.tensor_scalar(nm[:], imp[:], thr[:, 0:1], 1.0, ALU.is_ge, ALU.subtract), [imp, thr], [nm])
            mk.op("pe", lambda e: e.transpose(pM[0:64, 0:128], nm[:], ident_f[:]), [nm, ident_f], [pM])
            mk.op("dve", lambda e: e.tensor_copy(
                nmT4[0:64, :].rearrange("p (s q) -> p s q", s=4), pM[0:64, 0:128].unsqueeze(1).to_broadcast([64, 4, 128])), [pM], [nmT4])
            if NSA_LVL < 3:
                continue
            po = pO[0]
            prev = None
            for tk in range(T + 1):
                extra = [(Eall[:, tk * 128:(tk + 1) * 128], nmT4[:, :], [Eall, nmT4])]
                if tk == T:
                    extra.append((ident_b[:, :], caus4[:, :], [ident_b, caus4]))
                P = Pt[tk % 2]
                scores(g, KT3[:, g, tk * 128:(tk + 1) * 128], extra, P)
                if prev is not None:
                    prev()
                vt = Vs[:, tk * 144 + g * 72: tk * 144 + g * 72 + 65]
                prev = (lambda P=P, vt=vt, tk=tk, po=po: mk.op("pe", lambda e: e.matmul(
                    po[0:65, :], vt, P[:, :], start=(tk == 0), stop=(tk == T)), [P, Vs], [po]))
            prev()
            pv_finish(po, osl)
            if NSA_LVL < 4:
                continue
            po = pO[1]
            prev = None
            tks = list(range(T - 4, T + 1))
            for tk in tks:
                extra = []
                if tk == T - 4:
                    extra.append((ident_b[:, :], winlo4[:, :], [ident_b, winlo4]))
                if tk == T:
                    extra.append((ident_b[:, :], caus4[:, :], [ident_b, caus4]))
                if tk < 16:
                    extra.append((ident_b[:, :], pfxrow[:, :], [ident_b, pfxrow]))
                P = Pt[tk % 2]
                scores(g, KT3[:, 2 + g, tk * 128:(tk + 1) * 128], extra, P)
                if prev is not None:
                    prev()
                vt = Vw[:, tk * 144 + g * 72: tk * 144 + g * 72 + 65]
                prev = (lambda P=P, vt=vt, tk=tk, po=po: mk.op("pe", lambda e: e.matmul(
                    po[0:65, :], vt, P[:, :], start=(tk == tks[0]), stop=(tk == T)), [P, Vw], [po]))
            prev()
            pv_finish(po, owi)
            if NSA_LVL < 5:
                continue
            for bi, ob in ((1, osl), (2, owi)):
                o3 = ob[:, :].rearrange("p (s w) -> p s w", s=4)
                mk.op("dve", lambda e, o3=o3: e.tensor_scalar(rd[:], o3[:, :, 64], 1e-30, None, ALU.max), [ob], [rd])
                mk.op("dve", lambda e: e.reciprocal(rd[:], rd[:]), [rd], [rd])
                mk.op("dve", lambda e, bi=bi: e.tensor_tensor(
                    rd[:, :].rearrange("p (hh pp) -> p hh pp", hh=2), rd[:, :].rearrange("p (hh pp) -> p hh pp", hh=2), gview[:, :, :, bi], ALU.mult),
                    [rd, gates], [rd])
                ot = otmp[:, :].rearrange("p (hh pp d) -> p hh pp d", hh=2, pp=2)
                mk.op("dve", lambda e, o3=o3, ot=ot: e.tensor_tensor(
                    ot, o3[:, :, 0:64].rearrange("p (hh pp) d -> p hh pp d", hh=2),
                    rd[:, :].rearrange("p (hh pp) -> p hh pp", hh=2).unsqueeze(3).to_broadcast([128, 2, 2, 64]), ALU.mult), [ob, rd], [otmp])
                mk.op("dve", lambda e, ot=ot: e.tensor_tensor(mixv, mixv, ot, ALU.add), [mix, otmp], [mix])

        if NSA_LVL < 6:
            continue
        mk.op("act", lambda e: e.activation(sg[:, 0:512], zB[:, 0:512], AF.Silu), [zB], [sg])
        mk.op("act", lambda e: e.activation(sg[:, 512:1024], zB[:, 1024:1536], AF.Silu), [zB], [sg])
        for nh in range(2):
            p = pZ[nh]
            for G in range(4):
                mk.op("pe", lambda e, p=p, G=G, nh=nh, i=i: e.matmul(
                    p[:, :], yT[:, G * HALF + i * 128: G * HALF + (i + 1) * 128],
                    wglu[:, G * 1024 + nh * 512: G * 1024 + (nh + 1) * 512], start=(G == 0), stop=(G == 3)), [yT, wglu], [p])
        mk.op("act", lambda e: e.activation(ab[:, 512:1024], pZ[1][:, :], AF.Sigmoid), [pZ[1]], [ab])
        mk.op("dve", lambda e: e.tensor_tensor(ab[:, 0:512], pZ[0][:, :], ab[:, 512:1024], ALU.mult), [pZ[0], ab], [ab])
        mk.op("dve", lambda e: e.tensor_tensor(mix[:, 0:512], ab[:, 0:512], sg[:, 0:512], ALU.mult), [ab, sg], [mix])
        mk.op("dve", lambda e: e.tensor_tensor(mix[:, 512:1024], mix[:, 512:1024], sg[:, 512:1024], ALU.mult), [mix, sg], [mix])
        mk.op("dve", lambda e: e.tensor_copy(mixb[:], mix[:]), [mix], [mixb])
        for g4 in range(2):
            for j in range(4):
                kt = g4 * 4 + j
                mk.op("pe", lambda e, j=j, kt=kt: e.transpose(pT[:, j * 128:(j + 1) * 128], mixb[:, kt * 128:(kt + 1) * 128], ident_b[:]),
                      [mixb, ident_b], [pT])
            mk.op("act", lambda e, g4=g4: e.copy(mixT[:, g4 * 512:(g4 + 1) * 512], pT[:]), [pT], [mixT])
        for nh in range(2):
            p = pZ[nh]
            for kt in range(8):
                mk.op("pe", lambda e, p=p, kt=kt, nh=nh: e.matmul(
                    p[:, :], mixT[:, kt * 128:(kt + 1) * 128], wout[:, kt * 1024 + nh * 512: kt * 1024 + (nh + 1) * 512],
                    start=(kt == 0), stop=(kt == 7)), [mixT, wout], [p])
            mk.op("dve", lambda e, p=p, nh=nh: e.tensor_tensor(yo[:, nh * 512:(nh + 1) * 512], p[:, :], xb[:, nh * 512:(nh + 1) * 512], ALU.add),
                  [p, xb], [yo])
        mk.dma("sp", L["y_o"][i * 128:(i + 1) * 128, :], yo[:, 0:1024], reads=[yo])


SMP_LVL = 99
SMP_SUB = 99
NPG = 64
LK = NPG * 128 + 128


def sample_phase(L):
    mk = L["mk"]; nc = L["nc"]
    mk.es = L["esSm"]
    ident_b, ident_f, eps_t = L["ident_b"], L["ident_f"], L["eps_t"]
    nw, qw, snew, yTs = L["nw"], L["qw"], L["snew"], L["yTs"]
    w_in = L["w_in"]
    cache = L["cache_rows"]; cwin = L["cache_win_in"]; ptab = L["ptab_in"]

    wglu = mk.sb("s_wglu", [128, 4 * 1024], BF16)
    wout = mk.sb("s_wout", [128, 8 * 1024], BF16)
    gb = mk.sb("s_gb", [128, 24])
    mk.dma("sp", gb[:], L["gate_b"].partition_broadcast(128), writes=[gb])
    W1 = [mk.sb(f"s_W1{k}", [128, 32 * 128], BF16) for k in range(2)]
    w2b = [mk.sb(f"s_w2b{k}", [128, 128], BF16) for k in range(2)]
    peT = [mk.sb(f"s_peT{k}", [128, 32], BF16) for k in range(2)]
    b1c = mk.sb("s_b1c", [128, 2])
    mk.dma("sp", b1c[:], L["cmp_b1"].rearrange("k h -> h k"), writes=[b1c])
    pcol = mk.sb("s_pcol", [128, 1])
    mk.dma("sp", pcol[:], L["pcol_in"][:, :], writes=[pcol])
    cval = mk.sb("s_cval", [128, 4])
    mk.dma("sp", cval[:], L["cval_in"][:, :], writes=[cval])
    wcol = mk.sb("s_wcol", [128, 5])
    mk.dma("sp", wcol[:], L["wcol_in"][:, :], writes=[wcol])
    newcol = mk.sb("s_newcol", [128, 1])
    mk.dma("sp", newcol[:], L["newcol_in"][:, :], writes=[newcol])
    fadd_s = mk.sb("s_fadd", [1, 130])
    mk.dma("sp", fadd_s[:], L["fadds_in"][:, :], writes=[fadd_s])
    sel2 = mk.sb("s_sel2", [1, 256])
    mk.dma("sp", sel2[:], L["sel2_in"][:, :], writes=[sel2])
    vcA = mk.sb("s_vcA", [128, 4 * 2 * 200], BF16)
    vcA4 = vcA[:, :].rearrange("p (ct g w) -> p ct g w", ct=4, g=2)
    mk.op("dve", lambda e: e.memset(vcA[:], 1.0), [], [vcA])
    pT = mk.ps("s_pT", [128, 1024], BF16)
    pZ = [mk.ps(f"s_pZ{i}", [128, 512], F32) for i in range(2)]
    pSc = [mk.ps(f"s_pSc{i}", [128, 512], F32) for i in range(2)]
    pAcc = [mk.ps(f"s_pAcc{i}", [128, 512], F32) for i in range(2)]
    pM = mk.ps("s_pM", [128, 512], F32)
    bcol = mk.sb("s_bcol", [128, 2])

    xb = mk.sb("s_xb", [128, D_MODEL]); xsq = mk.sb("s_xsq", [128, D_MODEL])
    ssum = mk.sb("s_ssum", [128, 1]); rstd = mk.sb("s_rstd", [128, 1])
    xn = mk.sb("s_xn", [128, D_MODEL], BF16); xnT = mk.sb("s_xnT", [128, 1024], BF16)
    zB = mk.sb("s_zB", [128, 1560]); cst = mk.sb("s_cs", [128, 16])
    sq = mk.sb("s_sq", [128, 512]); hss = mk.sb("s_hss", [128, 8]); hrs = mk.sb("s_hrs", [128, 8])
    r1 = mk.sb("s_r1", [128, 64]); r2 = mk.sb("s_r2", [128, 64]); r3 = mk.sb("s_r3", [128, 64])
    qb = mk.sb("s_qb", [128, 512], BF16)
    QT = mk.sb("s_QT", [128, 4 * 128], BF16)
    gates = mk.sb("s_gates", [128, 24])
    snb = mk.sb("s_snb", [128, 512], BF16)
    knT = mk.sb("s_knT", [128, 512], BF16)
    vnb = mk.sb("s_vnb", [128, 256], BF16)
    esT = ExitStack()
    mk.es = esT
    wbB = mk.sb("s_wbB", [128, 8 * 1560], BF16)
    stg = [mk.sb(f"s_stg{i}", [128, 1560], F32) for i in range(2)]
    for dt_ in range(8):
        st = stg[dt_ % 2]
        for (c0, cw, z0) in COLS_B:
            mk.dma("sp", st[:, z0:z0 + cw], w_in[dt_ * 128:(dt_ + 1) * 128, c0:c0 + cw], writes=[st])
        mk.op("dve", lambda e, st=st, dt_=dt_: e.tensor_scalar(
            wbB[:, dt_ * 1560:(dt_ + 1) * 1560], st[:], nw[:, dt_:dt_ + 1], None, ALU.mult), [st, nw], [wbB])
    for G in range(4):
        st = stg[G % 2]
        mk.dma("sp", st[:, 0:1024], L["w_glu"][G * 128:(G + 1) * 128, :], writes=[st])
        mk.op("dve", lambda e, st=st, G=G: e.tensor_copy(wglu[:, G * 1024:(G + 1) * 1024], st[:, 0:1024]), [st], [wglu])
    for kt in range(8):
        st = stg[kt % 2]
        mk.dma("sp", st[:, 0:1024], L["w_out"][kt * 128:(kt + 1) * 128, :], writes=[st])
        mk.op("dve", lambda e, st=st, kt=kt: e.tensor_copy(wout[:, kt * 1024:(kt + 1) * 1024], st[:, 0:1024]), [st], [wout])
    for ct in range(4):
        st = stg[ct % 2]
        mk.dma("sp", st[:, 0:129], L["ovs_in"][ct * 128:(ct + 1) * 128, :], writes=[st])
        for g in range(2):
            mk.op("dve", lambda e, st=st, ct=ct, g=g: e.tensor_copy(vcA4[:, ct, g, 65:194], st[:, 0:129]), [st], [vcA])
    for kind in range(2):
        for jq in range(4):
            ws = stg[jq % 2]
            for half in range(2):
                mk.dma("sp", ws[half * 64:(half + 1) * 64, 0:1024].rearrange("p (j h) -> p j h", j=8),
                       L["cmp_w1"][kind].rearrange("(j d) h -> d j h", d=64)[:, jq * 8:(jq + 1) * 8, :], writes=[ws])
            mk.op("dve", lambda e, ws=ws, jq=jq, kind=kind: e.tensor_copy(W1[kind][:, jq * 1024:(jq + 1) * 1024], ws[:, 0:1024]), [ws], [W1[kind]])
        st = stg[0]
        mk.dma("sp", st[:, 0:64], L["cmp_w2"][kind], writes=[st])
        mk.op("dve", lambda e, st=st, kind=kind: e.tensor_copy(
            w2b[kind][:, :].rearrange("p (r d) -> p r d", r=2), st[:, 0:64].unsqueeze(1).to_broadcast([128, 2, 64])), [st], [w2b[kind]])
        st = stg[1]
        for half in range(2):
            mk.dma("sp", st[half * 64:(half + 1) * 64, 0:32], L["cmp_pe"][kind].rearrange("j d -> d j"), writes=[st])
        mk.op("dve", lambda e, st=st, kind=kind: e.tensor_copy(peT[kind][:], st[:, 0:32]), [st], [peT[kind]])
    for kind in range(2):
        for j in range(32):
            mk.op("pe", lambda e, kind=kind, j=j: e.matmul(
                pM[:, kind:kind + 1], W1[kind][0:64, j * 128:(j + 1) * 128], peT[kind][0:64, j:j + 1],
                start=(j == 0), stop=(j == 31)), [W1[kind], peT[kind]], [pM])
        mk.op("dve", lambda e, kind=kind: e.tensor_tensor(bcol[:, kind:kind + 1], pM[:, kind:kind + 1], b1c[:, kind:kind + 1], ALU.add),
              [pM, b1c], [bcol])

    mk.dma("sp", xb[:], L["x_smp"][:, :], writes=[xb])
    mk.dma("sp", cst[:], L["cs_smp"][:, :], writes=[cst])
    mk.op("act", lambda e: e.activation(xsq[:], xb[:], AF.Square, accum_out=ssum[:]), [xb], [xsq, ssum])
    mk.op("act", lambda e: e.activation(rstd[:], ssum[:], AF.Sqrt, bias=eps_t[:, 0:1], scale=1.0 / D_MODEL), [ssum, eps_t], [rstd])
    mk.op("dve", lambda e: e.reciprocal(rstd[:], rstd[:]), [rstd], [rstd])
    mk.op("dve", lambda e: e.tensor_scalar(xn[:], xb[:], rstd[:, 0:1], None, ALU.mult), [xb, rstd], [xn])
    for g4 in range(2):
        for j in range(4):
            dt_ = g4 * 4 + j
            mk.op("pe", lambda e, j=j, dt_=dt_: e.transpose(pT[:, j * 128:(j + 1) * 128], xn[:, dt_ * 128:(dt_ + 1) * 128], ident_b[:]),
                  [xn, ident_b], [pT])
        mk.op("act", lambda e, g4=g4: e.copy(xnT[:, g4 * 512:(g4 + 1) * 512], pT[:, 0:512]), [pT], [xnT])
    for ci, (c0, cw, z0) in enumerate(COLS_B):
        p = pZ[ci % 2]
        for dt_ in range(8):
            mk.op("pe", lambda e, p=p, dt_=dt_, z0=z0, cw=cw: e.matmul(
                p[:, 0:cw], xnT[:, dt_ * 128:(dt_ + 1) * 128], wbB[:, dt_ * 1560 + z0: dt_ * 1560 + z0 + cw],
                start=(dt_ == 0), stop=(dt_ == 7)), [xnT, wbB], [p])
        mk.op("dve", lambda e, p=p, z0=z0, cw=cw: e.tensor_copy(zB[:, z0:z0 + cw], p[:, 0:cw]), [p], [zB])
    qv = zB[:, 512:1024].rearrange("p (h d) -> p h d", d=64)
    mk.op("act", lambda e: e.activation(sq[:, :].rearrange("p (h d) -> p h d", d=64), qv, AF.Square), [zB], [sq])
    mk.op("dve", lambda e: e.tensor_reduce(hss[:, :], sq[:, :].rearrange("p (h d) -> p h d", d=64), AX.X, ALU.add), [sq], [hss])
    mk.op("act", lambda e: e.activation(hrs[:], hss[:], AF.Sqrt, bias=eps_t[:, 0:1], scale=1.0 / 64), [hss, eps_t], [hrs])
    mk.op("dve", lambda e: e.reciprocal(hrs[:], hrs[:]), [hrs], [hrs])
    mk.op("dve", lambda e: e.tensor_tensor(qv, qv, hrs[:, :].unsqueeze(2).to_broadcast([128, 8, 64]), ALU.mult), [zB, hrs], [zB])
    mk.op("dve", lambda e: e.scalar_tensor_tensor(qv, qv, 0.125, qw[:, :].unsqueeze(1).to_broadcast([128, 8, 64]), ALU.mult, ALU.mult),
          [zB, qw], [zB])
    cosq = cst[:, 0:8].unsqueeze(1).to_broadcast([128, 8, 8])
    sinq = cst[:, 8:16].unsqueeze(1).to_broadcast([128, 8, 8])
    a1 = r1[:, :].rearrange("p (h d) -> p h d", d=8); a2 = r2[:, :].rearrange("p (h d) -> p h d", d=8)
    a3 = r3[:, :].rearrange("p (h d) -> p h d", d=8)
    x1 = qv[..., 0:8]; x2 = qv[..., 8:16]
    mk.op("dve", lambda e: e.tensor_tensor(a1, x2, sinq, ALU.mult), [zB, cst], [r1])
    mk.op("dve", lambda e: e.tensor_tensor(a2, x1, sinq, ALU.mult), [zB, cst], [r2])
    mk.op("dve", lambda e: e.tensor_tensor(a3, x1, cosq, ALU.mult), [zB, cst], [r3])
    mk.op("dve", lambda e: e.tensor_tensor(x1, a3, a1, ALU.subtract), [r3, r1], [zB])
    mk.op("dve", lambda e: e.tensor_tensor(a3, x2, cosq, ALU.mult), [zB, cst], [r3])
    mk.op("dve", lambda e: e.tensor_tensor(x2, a3, a2, ALU.add), [r3, r2], [zB])
    mk.op("dve", lambda e: e.tensor_copy(qb[:], zB[:, 512:1024]), [zB], [qb])
    for j in range(4):
        mk.op("pe", lambda e, j=j: e.transpose(pT[:, j * 128:(j + 1) * 128], qb[:, j * 128:(j + 1) * 128], ident_b[:]), [qb, ident_b], [pT])
    mk.op("act", lambda e: e.copy(QT[:], pT[:, 0:512]), [pT], [QT])
    QT3 = QT[:, :].rearrange("p (pr q) -> p pr q", pr=4)
    mk.op("dve", lambda e: e.tensor_tensor(gates[:], zB[:, 1536:1560], gb[:], ALU.add), [zB, gb], [gates])
    mk.op("act", lambda e: e.activation(gates[:], gates[:], AF.Sigmoid), [gates], [gates])
    mk.op("dve", lambda e: e.tensor_copy(
        snb[:, 0:256].rearrange("p (g r d) -> p g r d", g=2, r=2),
        snew[:, 256:384].rearrange("p (g d) -> p g d", g=2).unsqueeze(2).to_broadcast([128, 2, 2, 64])), [snew], [snb])
    mk.op("dve", lambda e: e.tensor_copy(
        snb[:, 256:512].rearrange("p (g r d) -> p g r d", g=2, r=2),
        snew[:, 512:640].rearrange("p (g d) -> p g d", g=2).unsqueeze(2).to_broadcast([128, 2, 2, 64])), [snew], [snb])
    for j in range(4):
        mk.op("pe", lambda e, j=j: e.transpose(pT[:, j * 128:(j + 1) * 128], snb[:, j * 128:(j + 1) * 128], ident_b[:]), [snb, ident_b], [pT])
    mk.op("act", lambda e: e.copy(knT[:], pT[:, 0:512]), [pT], [knT])
    mk.op("dve", lambda e: e.tensor_copy(vnb[:, 0:128], snew[:, 384:512]), [snew], [vnb])
    mk.op("dve", lambda e: e.tensor_copy(vnb[:, 128:256], snew[:, 640:768]), [snew], [vnb])

    mk.barrier()
    esT.close()
    mk.es = L["esSm"]
    CT = mk.sb("s_CT", [128, 2 * 8192], BF16)
    KTs = mk.sb("s_KTs", [128, 2 * LK], BF16)
    Vs = mk.sb("s_Vs", [128, 65 * 144], BF16)
    KTw = mk.sb("s_KTw", [128, 2 * 640], BF16)
    Vw = mk.sb("s_Vw", [128, 5 * 144], BF16)
    mk.op("pool", lambda e: e.memset(Vs[:], 1.0), [], [Vs])
    mk.op("pool", lambda e: e.memset(Vw[:], 1.0), [], [Vw])
    mk.op("pool", lambda e: e.memset(KTs[:], 0.0), [], [KTs])
    mk.op("pool", lambda e: e.memset(KTw[:], 0.0), [], [KTw])
    pg = [mk.sb(f"s_pg{i}", [128, 512]) for i in range(2)]
    pgb = [mk.sb(f"s_pgb{i}", [128, 512], BF16) for i in range(2)]
    ptf = mk.sb("s_ptf", [128, 64]); pti = mk.sb("s_pti", [128, 64], I32); idx = mk.sb("s_idx", [128, 64], I32)
    hidT = mk.sb("s_hidT", [128, 512], BF16)
    mk.op("dve", lambda e: e.memset(hidT[:], 0.0), [], [hidT])
    kcTc = mk.sb("s_kcTc", [128, 2 * 512], BF16)
    P4 = [mk.sb(f"s_P4{i}", [128, 4], BF16) for i in range(2)]
    ocs = mk.sb("s_ocs", [4, 200]); osl = mk.sb("s_osl", [4, 72]); owi = mk.sb("s_owi", [4, 72])
    rdn = mk.sb("s_rdn", [4, 1]); obr = mk.sb("s_obr", [4, 3 * 64])
    impr = mk.sb("s_impr", [1, 130]); m8 = mk.sb("s_m8", [1, 16]); wk = mk.sb("s_wk", [1, 130]); thr = mk.sb("s_thr", [1, 1])
    nmr = mk.sb("s_nmr", [1, 130]); mcol = mk.sb("s_mcol", [128, 65])
    onsa = mk.sb("s_onsa", [128, 3 * 512])
    mk.op("dve", lambda e: e.memset(onsa[:], 0.0), [], [onsa])
    wpg = [mk.sb(f"s_wpg{i}", [128, 256]) for i in range(2)]
    CT3 = CT[:, :].rearrange("p (a t) -> p a t", a=2)
    KTs3 = KTs[:, :].rearrange("p (g t) -> p g t", g=2)
    KTw3 = KTw[:, :].rearrange("p (g t) -> p g t", g=2)

    def gather(q, dst_buf, dst_ap, idx_ap):
        mk._deps(q, [idx], [dst_buf])
        slot = mk.dma_pool[mk.dma_rr]
        mk.dma_rr = (mk.dma_rr + 1) % len(mk.dma_pool)
        if slot[1] > 0:
            mk._wait(q, (slot[0], slot[1], "dma"))
        inst = nc.gpsimd.indirect_dma_start(out=dst_ap, out_offset=None, in_=cache[:, :],
                                            in_offset=bass.IndirectOffsetOnAxis(ap=idx_ap, axis=0))
        slot[1] += 16
        inst.then_inc(slot[0], 16)
        tok = (slot[0], slot[1], "dma")
        mk._mark(tok, [idx], [dst_buf])
        mk.n_inst += 1

    for si in range(4 if SMP_LVL > -3 else 0):
        mk.dma("sp", pti[:], ptab[si, :].partition_broadcast(128), writes=[pti])
        mk.op("dve", lambda e: e.tensor_copy(ptf[:], pti[:]), [pti], [ptf])
        mk.op("dve", lambda e: e.tensor_scalar(ptf[:], ptf[:], 128.0, pcol[:, 0:1], ALU.mult, ALU.add), [ptf, pcol], [ptf])
        mk.op("dve", lambda e: e.tensor_copy(idx[:], ptf[:]), [ptf], [idx])
        if SMP_LVL < -1:
            continue
        for j in range(NPG):
            g_ = pg[j % 2]; b_ = pgb[j % 2]
            gather("pool", g_, g_[:, :], idx[:, j:j + 1])
            if SMP_SUB < 1:
                continue
            mk.op("dve", lambda e, g_=g_, b_=b_: e.tensor_copy(b_[:, 0:256], g_[:, 0:256]), [g_], [b_])
            mk.op("pool", lambda e, g_=g_, b_=b_: e.tensor_copy(
                b_[:, 256:512].rearrange("p (g r d) -> p g r d", g=2, r=2),
                g_[:, 256:384].rearrange("p (g d) -> p g d", g=2).unsqueeze(2).to_broadcast([128, 2, 2, 64])), [g_], [b_])
            mk.op("pool", lambda e, g_=g_, j=j: e.tensor_copy(
                Vs[:, j * 144:(j + 1) * 144].rearrange("p (g d) -> p g d", g=2)[:, :, 0:64],
                g_[:, 384:512].rearrange("p (g d) -> p g d", g=2)), [g_], [Vs])
            if SMP_SUB < 2:
                continue
            for q4 in range(4):
                mk.op("pe", lambda e, q4=q4, b_=b_: e.transpose(pT[:, q4 * 128:(q4 + 1) * 128], b_[:, q4 * 128:(q4 + 1) * 128], ident_b[:]),
                      [b_, ident_b], [pT])
            mk.op("act", lambda e, j=j: e.copy(CT3[:, :, j * 128:(j + 1) * 128], pT[:, 0:256].rearrange("p (a t) -> p a t", a=2)), [pT], [CT])
            mk.op("act", lambda e, j=j: e.copy(KTs3[:, :, j * 128:(j + 1) * 128], pT[:, 256:512].rearrange("p (a t) -> p a t", a=2)),
                  [pT], [KTs])
        if SMP_LVL < 0:
            continue
        for g in range(2):
            mk.op("dve", lambda e, g=g, si=si: e.tensor_copy(KTs3[:, g, 8192:8193], knT[:, g * 128 + si: g * 128 + si + 1]), [knT], [KTs])
            mk.op("dve", lambda e, g=g, si=si: e.tensor_copy(KTw3[:, g, 512:513], knT[:, (2 + g) * 128 + si: (2 + g) * 128 + si + 1]), [knT], [KTw])
        mk.dma("sp", Vs[0:1, 64 * 144: 65 * 144].rearrange("p (g d) -> p g d", g=2)[:, :, 0:64],
               vnb[si:si + 1, 0:128].rearrange("p (g d) -> p g d", g=2), reads=[vnb], writes=[Vs])
        mk.dma("sp", Vw[0:1, 4 * 144: 5 * 144].rearrange("p (g d) -> p g d", g=2)[:, :, 0:64],
               vnb[si:si + 1, 128:256].rearrange("p (g d) -> p g d", g=2), reads=[vnb], writes=[Vw])
        for wt in range(4):
            wp = wpg[wt % 2]; b_ = pgb[wt % 2]
            mk.dma("sp", wp[:], cwin[si, wt * 128:(wt + 1) * 128, :], writes=[wp])
            mk.op("pool", lambda e, wp=wp, b_=b_: e.tensor_copy(
                b_[:, 0:256].rearrange("p (g r d) -> p g r d", g=2, r=2),
                wp[:, 0:128].rearrange("p (g d) -> p g d", g=2).unsqueeze(2).to_broadcast([128, 2, 2, 64])), [wp], [b_])
            mk.op("pool", lambda e, wp=wp, wt=wt: e.tensor_copy(
                Vw[:, wt * 144:(wt + 1) * 144].rearrange("p (g d) -> p g d", g=2)[:, :, 0:64],
                wp[:, 128:256].rearrange("p (g d) -> p g d", g=2)), [wp], [Vw])
            for q4 in range(2):
                mk.op("pe", lambda e, q4=q4, b_=b_: e.transpose(pT[:, q4 * 128:(q4 + 1) * 128], b_[:, q4 * 128:(q4 + 1) * 128], ident_b[:]),
                      [b_, ident_b], [pT])
            mk.op("act", lambda e, wt=wt: e.copy(KTw3[:, :, wt * 128:(wt + 1) * 128], pT[:, 0:256].rearrange("p (a t) -> p a t", a=2)),
                  [pT], [KTw])
        mk.dma("sp", L["wins_o"][si, 0:511, :], cwin[si, 1:512, :])
        mk.dma("sp", L["wins_o"][si, 511:512, :], snew[si:si + 1, 512:768], reads=[snew])
        if SMP_LVL < 1:
            continue
        for kind in range(2):
            for g in range(2):
                lo, hi = g * 64, g * 64 + 64
                ph = pZ[0]
                src = CT3[lo:hi, kind, :].rearrange("p (i j) -> p i j", j=16)
                for j in range(32):
                    rhs = src[:, 0:511, j] if j < 16 else src[:, 1:512, j - 16]
                    mk.op("pe", lambda e, j=j, rhs=rhs, lo=lo, hi=hi, kind=kind: e.matmul(
                        ph[:, 0:511], W1[kind][lo:hi, j * 128:(j + 1) * 128], rhs, start=(j == 0), stop=(j == 31)),
                        [W1[kind], CT], [ph])
                mk.op("act", lambda e, kind=kind: e.activation(hidT[:, 0:511], ph[:, 0:511], AF.Gelu_apprx_tanh, bias=bcol[:, kind:kind + 1]),
                      [ph, bcol], [hidT])
                po = pZ[1]
                if kind == 0:
                    mk.op("pe", lambda e: e.matmul(po[:, 0:512], w2b[0][:, :], hidT[:, :], start=True, stop=True), [w2b[0], hidT], [po])
                    mk.op("dve", lambda e, g=g: e.tensor_copy(kcTc[:, g * 512:(g + 1) * 512], po[:, 0:512]), [po], [kcTc])
                else:
                    for ct in range(4):
                        mk.op("pe", lambda e, ct=ct: e.matmul(po[:, ct * 64:(ct + 1) * 64], hidT[:, ct * 128:(ct + 1) * 128], w2b[1][:, 0:64],
                                                              start=True, stop=True), [hidT, w2b[1]], [po])
                    mk.op("dve", lambda e, g=g: e.tensor_copy(vcA4[:, :, g, 0:64], po[:, 0:256].rearrange("p (ct d) -> p ct d", ct=4)), [po], [vcA])
        if SMP_LVL < 2:
            continue
        for g in range(2):
            def sc(kt_fn, bias_ap, Pd, extra_reads, par=0):
                for hh in range(2):
                    ps = (pSc if par % 2 == 0 else pZ)[hh]
                    mk.op("pe", lambda e, ps=ps, hh=hh: e.matmul(ps[:, 0:2], kt_fn(hh), QT3[hh * 64:(hh + 1) * 64, 2 * g:2 * g + 2, si],
                                                                 start=True, stop=True), [QT] + extra_reads, [ps])
                    if bias_ap is None:
                        mk.op("act", lambda e, ps=ps, hh=hh: e.activation(Pd[:, hh * 2:hh * 2 + 2], ps[:, 0:2], AF.Exp), [ps], [Pd])
                    else:
                        mk.op("act", lambda e, ps=ps, hh=hh: e.activation(Pd[:, hh * 2:hh * 2 + 2], ps[:, 0:2], AF.Exp, bias=bias_ap),
                              [ps, mcol, wcol], [Pd])
            pa = pAcc[0]
            for ct in range(4):
                Pd = P4[ct % 2]
                sc(lambda hh, ct=ct: kcTc[hh * 64:(hh + 1) * 64, g * 512 + ct * 128: g * 512 + (ct + 1) * 128], None, Pd, [kcTc])
                mk.op("dve", lambda e, Pd=Pd, ct=ct: e.tensor_scalar(Pd[:], Pd[:], cval[:, ct:ct + 1], None, ALU.mult), [Pd, cval], [Pd])
                mk.op("pe", lambda e, Pd=Pd, ct=ct: e.matmul(pa[0:4, 0:194], Pd[:, :], vcA4[:, ct, g, 0:194], start=(ct == 0), stop=(ct == 3)),
                      [Pd, vcA], [pa])
            mk.op("act", lambda e: e.copy(ocs[:, 0:194], pa[0:4, 0:194]), [pa], [ocs])
            mk.op("dve", lambda e: e.tensor_scalar(rdn[:], ocs[:, 64:65], 1e-30, None, ALU.max), [ocs], [rdn])
            mk.op("dve", lambda e: e.reciprocal(rdn[:], rdn[:]), [rdn], [rdn])
            mk.op("dve", lambda e: e.tensor_scalar(obr[:, 0:64], ocs[:, 0:64], rdn[:, 0:1], None, ALU.mult), [ocs, rdn], [obr])
            mk.op("pe", lambda e: e.matmul(pM[0:1, 0:129], rdn[:, 0:1], ocs[:, 65:194], start=True, stop=True), [rdn, ocs], [pM])
            mk.op("dve", lambda e: e.tensor_copy(impr[:], fadd_s[:]), [fadd_s], [impr])
            mk.op("dve", lambda e: e.tensor_tensor(impr[:, 0:129], impr[:, 0:129], pM[0:1, 0:129], ALU.add), [impr, pM], [impr])
            mk.op("dve", lambda e: e.max(out=m8[:, 0:8], in_=impr[:]), [impr], [m8])
            mk.op("dve", lambda e: e.match_replace(out=wk[:], in_to_replace=m8[:, 0:8], in_values=impr[:], imm_value=-1e9), [impr, m8], [wk])
            mk.op("dve", lambda e: e.max(out=m8[:, 8:16], in_=wk[:]), [wk], [m8])
            mk.op("dve", lambda e: e.tensor_scalar(thr[:], m8[:, 15:16], -5000.0, None, ALU.max), [m8], [thr])
            mk.op("dve", lambda e: e.tensor_scalar(nmr[:], impr[:], thr[:, 0:1], 1.0, ALU.is_ge, ALU.subtract), [impr, thr], [nmr])
            nm2 = nmr[:, :].rearrange("p (j two) -> p j two", two=2)
            mk.op("pe", lambda e: e.matmul(pM[:, 256:321], sel2[:, 0:128], nm2[:, :, 0], start=True, stop=False), [sel2, nmr], [pM])
            mk.op("pe", lambda e: e.matmul(pM[:, 256:321], sel2[:, 128:256], nm2[:, :, 1], start=False, stop=True), [sel2, nmr], [pM])
            mk.op("dve", lambda e: e.tensor_scalar(mcol[:], pM[:, 256:321], 30000.0, None, ALU.mult), [pM], [mcol])
            mk.op("dve", lambda e: e.tensor_tensor(mcol[:, 64:65], mcol[:, 64:65], newcol[:], ALU.add), [mcol, newcol], [mcol])
            pa = pAcc[1]
            prev = None
            for j in range(NPG + 1):
                Pd = P4[j % 2]
                sc(lambda hh, j=j: KTs3[hh * 64:(hh + 1) * 64, g, j * 128:(j + 1) * 128], mcol[:, j:j + 1], Pd, [KTs], par=j)
                if prev is not None:
                    prev()
                prev = (lambda Pd=Pd, j=j, pa=pa: mk.op("pe", lambda e: e.matmul(
                    pa[0:4, 0:65], Pd[:, :], Vs[:, j * 144 + g * 72: j * 144 + g * 72 + 65],
                    start=(j == 0), stop=(j == NPG)), [Pd, Vs], [pa]))
            prev()
            mk.op("act", lambda e: e.copy(osl[:, 0:65], pa[0:4, 0:65]), [pa], [osl])
            mk.op("dve", lambda e: e.tensor_scalar(rdn[:], osl[:, 64:65], 1e-30, None, ALU.max), [osl], [rdn])
            mk.op("dve", lambda e: e.reciprocal(rdn[:], rdn[:]), [rdn], [rdn])
            mk.op("dve", lambda e: e.tensor_scalar(obr[:, 64:128], osl[:, 0:64], rdn[:, 0:1], None, ALU.mult), [osl, rdn], [obr])
            pa = pAcc[0]
            for j in range(5):
                Pd = P4[j % 2]
                sc(lambda hh, j=j: KTw3[hh * 64:(hh + 1) * 64, g, j * 128:(j + 1) * 128], wcol[:, j:j + 1], Pd, [KTw])
                mk.op("pe", lambda e, Pd=Pd, j=j: e.matmul(pa[0:4, 256:321], Pd[:, :], Vw[:, j * 144 + g * 72: j * 144 + g * 72 + 65],
                                                         start=(j == 0), stop=(j == 4)), [Pd, Vw], [pa])
            mk.op("act", lambda e: e.copy(owi[:, 0:65], pa[0:4, 256:321]), [pa], [owi])
            mk.op("dve", lambda e: e.tensor_scalar(rdn[:], owi[:, 64:65], 1e-30, None, ALU.max), [owi], [rdn])
            mk.op("dve", lambda e: e.reciprocal(rdn[:], rdn[:]), [rdn], [rdn])
            mk.op("dve", lambda e: e.tensor_scalar(obr[:, 128:192], owi[:, 0:64], rdn[:, 0:1], None, ALU.mult), [owi, rdn], [obr])
            for hh in range(2):
                for pp in range(2):
                    slot = hh * 2 + pp
                    head = 4 * g + 2 * pp + hh
                    mk.dma("sp", onsa[si:si + 1, :].rearrange("p (br hd) -> p br hd", br=3)[:, :, head * 64:(head + 1) * 64],
                           obr[slot:slot + 1, :].rearrange("p (br d) -> p br d", br=3), reads=[obr], writes=[onsa])

    if SMP_LVL < 3:
        return
    mix = mk.sb("s_mix", [128, 1024]); yo = mk.sb("s_yo", [128, 1024]); otm = sq
    on3 = onsa[:, :].rearrange("p (br h d) -> p br h d", br=3, h=8)
    g3 = gates[:, :].rearrange("p (h t) -> p h t", t=3)
    mixn = mix[:, 512:1024].rearrange("p (h d) -> p h d", h=8)
    for br in range(3):
        dst = mixn if br == 0 else otm[:, :].rearrange("p (h d) -> p h d", h=8)
        mk.op("dve", lambda e, br=br, dst=dst: e.tensor_tensor(dst, on3[:, br, :, :], g3[:, :, br].unsqueeze(2).to_broadcast([128, 8, 64]), ALU.mult),
              [onsa, gates], [mix if br == 0 else otm])
        if br > 0:
            mk.op("dve", lambda e: e.tensor_tensor(mix[:, 512:1024], mix[:, 512:1024], otm[:], ALU.add), [mix, otm], [mix])
    sg = xsq
    mk.op("act", lambda e: e.activation(sg[:, 0:512], zB[:, 0:512], AF.Silu), [zB], [sg])
    mk.op("act", lambda e: e.activation(sg[:, 512:1024], zB[:, 1024:1536], AF.Silu), [zB], [sg])
    for nh in range(2):
        p = pZ[nh]
        for G in range(4):
            mk.op("pe", lambda e, p=p, G=G, nh=nh: e.matmul(
                p[:, :], yTs[:, G * 128:(G + 1) * 128], wglu[:, G * 1024 + nh * 512: G * 1024 + (nh + 1) * 512],
                start=(G == 0), stop=(G == 3)), [yTs, wglu], [p])
    mk.op("act", lambda e: e.activation(yo[:, 512:1024], pZ[1][:, :], AF.Sigmoid), [pZ[1]], [yo])
    mk.op("dve", lambda e: e.tensor_tensor(yo[:, 0:512], pZ[0][:, :], yo[:, 512:1024], ALU.mult), [pZ[0], yo], [yo])
    mk.op("dve", lambda e: e.tensor_tensor(mix[:, 0:512], yo[:, 0:512], sg[:, 0:512], ALU.mult), [yo, sg], [mix])
    mk.op("dve", lambda e: e.tensor_tensor(mix[:, 512:1024], mix[:, 512:1024], sg[:, 512:1024], ALU.mult), [mix, sg], [mix])
    mk.op("dve", lambda e: e.tensor_copy(xn[:], mix[:]), [mix], [xn])
    for g4 in range(2):
        for j in range(4):
            kt = g4 * 4 + j
            mk.op("pe", lambda e, j=j, kt=kt: e.transpose(pT[:, j * 128:(j + 1) * 128], xn[:, kt * 128:(kt + 1) * 128], ident_b[:]),
                  [xn, ident_b], [pT])
        mk.op("act", lambda e, g4=g4: e.copy(xnT[:, g4 * 512:(g4 + 1) * 512], pT[:, 0:512]), [pT], [xnT])
    for nh in range(2):
        p = pZ[nh]
        for kt in range(8):
            mk.op("pe", lambda e, p=p, kt=kt, nh=nh: e.matmul(
                p[:, :], xnT[:, kt * 128:(kt + 1) * 128], wout[:, kt * 1024 + nh * 512: kt * 1024 + (nh + 1) * 512],
                start=(kt == 0), stop=(kt == 7)), [xnT, wout], [p])
        mk.op("dve", lambda e, p=p, nh=nh: e.tensor_tensor(yo[:, nh * 512:(nh + 1) * 512], p[:, :], xb[:, nh * 512:(nh + 1) * 512], ALU.add),
              [p, xb], [yo])
    mk.dma("sp", L["ys_o"][:, :], yo[0:4, :], reads=[yo])


TWO_PI = 6.283185307179586
HALF_PI = 1.5707963267948966
CH = 512
NT_ALL = 32
COLS_A = [(0, 512, 0), (2048, 512, 512), (2560, 256, 1024)]


def build_nc(n_tiles=NT_ALL, do_ssm=True, do_nsa=True, do_smp=True):
    nc = bass.Bass("TRN2", target_bir_lowering=False)

    def din(name, shape, dt=F32):
        return nc.dram_tensor(name, list(shape), dt, kind="ExternalInput").ap()

    def dout(name, shape, dt=F32):
        return nc.dram_tensor(name, list(shape), dt, kind="ExternalOutput").ap()

    x_all = din("x_all", [SEQ, D_MODEL])
    w_in = din("w_in", [D_MODEL, IN_W])
    norm_w = din("norm_w", [D_MODEL])
    q_norm_w = din("q_norm_w", [64])
    k_norm_w = din("k_norm_w", [3, 64])
    cs_all = din("cs_all", [SEQ, 16])
    ident_in = din("ident", [128, 128])
    tv_in = din("tvals", [SEQ])
    lam_re = din("lam_re", [2048])
    lam_im = din("lam_im", [2048])
    log_step = din("log_step", [32])
    b_re = din("b_re", [2048, 16])
    b_im = din("b_im", [2048, 16])
    c_re = din("c_re", [32, 16, 64])
    c_im = din("c_im", [32, 16, 64])
    ssm_d = din("ssm_d", [512])

    gate_b = din("gate_b", [24])
    cmp_pe = din("cmp_pe", [2, 32, 64])
    cmp_w1 = din("cmp_w1", [2, 2048, 128])
    cmp_b1 = din("cmp_b1", [2, 128])
    cmp_w2 = din("cmp_w2", [2, 128, 64])
    w_glu = din("w_glu", [512, 1024])
    w_out = din("w_out", [1024, 1024])
    ov_in = din("ov_tab", [256, 64])
    cthr_in = din("cthr", [128, 2])
    fadd_in = din("fadd", [128, 16 * 64])
    eall_in = din("eall", [64, 32 * 128])
    caus_in = din("caus", [128, 128])
    winlo_in = din("winlo", [128, 128])
    pfx_in = din("pfxrow", [1, 512])
    y_o = dout("y_o", [HALF, D_MODEL])
    x_smp = din("x_smp", [128, D_MODEL])
    cs_smp = din("cs_smp", [128, 16])
    st_re = din("st_re", [4, 2048])
    st_im = din("st_im", [4, 2048])
    cache_rows = din("cache_rows", [2560 * 128, 512])
    cache_win_in = din("cache_win_s", [4, 512, 256])
    ptab_in = din("ptab", [4, 64], I32)
    pcol_in = din("pcol", [128, 1])
    cval_in = din("cval", [128, 4])
    wcol_in = din("wcol", [128, 5])
    newcol_in = din("newcol", [128, 1])
    fadds_in = din("fadds", [1, 130])
    sel2_in = din("sel2", [1, 256])
    ovs_in = din("ovs", [512, 129])
    ys_o = dout("ys_o", [4, D_MODEL])
    wins_o = dout("wins_o", [4, 512, 256])
    kvs_o = dout("kvs_o", [4, 512])
    sres_o = dout("sres_o", [4, 2048])
    sims_o = dout("sims_o", [4, 2048])
    kv_o = dout("kv_o", [HALF, 512])
    win_o = dout("win_o", [512, 256])
    sre_o = dout("sre_o", [2048])
    sim_o = dout("sim_o", [2048])

    with ExitStack() as es:
        es.enter_context(nc.allow_low_precision("bf16 matmul operands, fp32 accumulation"))
        es.enter_context(nc.allow_non_contiguous_dma("small strided parameter loads"))
        mk = MK(nc, es)

        ident_f = mk.sb("ident_f", [128, 128], F32)
        ident_b = mk.sb("ident_b", [128, 128], BF16)
        mk.dma("sp", ident_f[:], ident_in[:, :], writes=[ident_f])
        mk.op("dve", lambda e: e.tensor_copy(ident_b[:], ident_f[:]), [ident_f], [ident_b])
        eps_t = mk.sb("eps_t", [128, 1], F32)
        mk.op("dve", lambda e: e.memset(eps_t[:], RMS_EPS), [], [eps_t])
        hpi_t = mk.sb("hpi_t", [128, 1], F32)
        mk.op("dve", lambda e: e.memset(hpi_t[:], HALF_PI), [], [hpi_t])
        nw = mk.sb("nw", [128, 8], F32)
        mk.dma("sp", nw[:], norm_w.rearrange("(t p) -> p t", p=128), writes=[nw])
        qw = mk.sb("qw", [128, 64], F32)
        mk.dma("sp", qw[:], q_norm_w.partition_broadcast(128), writes=[qw])
        kw = mk.sb("kw", [128, 3 * 64], F32)
        mk.dma("sp", kw[:], k_norm_w.rearrange("a d -> (a d)").partition_broadcast(128), writes=[kw])

        snew = mk.sb("snew", [128, 768])
        yTs = mk.sb("yTs", [128, 4 * 128], BF16)
        mk.op("pool", lambda e: e.memset(yTs[:], 0.0), [], [yTs])
        esPr = ExitStack()
        mk.es = esPr
        KT = mk.sb("KT", [128, 4 * SEQ], BF16)
        kcT = mk.sb("kcT", [128, SEQ], BF16)
        vcT = mk.sb("vcT", [128, SEQ], BF16)
        Vs = mk.sb("Vs", [128, NT_ALL * 2 * 72], BF16)
        Vw = mk.sb("Vw", [128, NT_ALL * 2 * 72], BF16)
        mk.op("pool", lambda e: e.memset(Vs[:], 1.0), [], [Vs])
        mk.op("pool", lambda e: e.memset(Vw[:], 1.0), [], [Vw])
        yT = mk.sb("yT", [128, 4 * HALF], BF16)
        tv = mk.sb("tv", [128, SEQ])
        mk.dma("sp", tv[:], tv_in.partition_broadcast(128), writes=[tv])
        esU = ExitStack()
        mk.es = esU
        UTW = SEQ + 128
        uT = mk.sb("uT", [128, 4 * UTW], BF16)

        esA = ExitStack()
        mk.es = esA
        wbA = mk.sb("wbA", [128, 8 * 1280], BF16)
        wst = [mk.sb(f"wst{i}", [128, 1280], F32) for i in range(2)]
        for dt_ in range(8):
            st = wst[dt_ % 2]
            for (c0, cw, z0) in COLS_A:
                mk.dma("sp", st[:, z0:z0 + cw], w_in[dt_ * 128:(dt_ + 1) * 128, c0:c0 + cw], writes=[st])
            mk.op("dve", lambda e, st=st, dt_=dt_: e.tensor_scalar(
                wbA[:, dt_ * 1280:(dt_ + 1) * 1280], st[:], nw[:, dt_:dt_ + 1], None, ALU.mult),
                [st, nw], [wbA])

        xt = [mk.sb(f"xt{i}", [128, D_MODEL], F32) for i in range(2)]
        xsq = mk.sb("xsq", [128, D_MODEL], F32)
        ssum = mk.sb("ssum", [128, 1], F32)
        rstd = mk.sb("rstd", [128, 1], F32)
        xn = mk.sb("xn", [128, D_MODEL], BF16)
        xnT = mk.sb("xnT", [128, 8 * 128], BF16)
        zA = [mk.sb(f"zA{i}", [128, 1280], F32) for i in range(2)]
        cs = [mk.sb(f"cs{i}", [128, 16], F32) for i in range(2)]
        sq = mk.sb("sq", [128, 6 * 64], F32)
        hss = mk.sb("hss", [128, 6], F32)
        hrs = mk.sb("hrs", [128, 6], F32)
        rt1 = mk.sb("rt1", [128, 48], F32)
        rt2 = mk.sb("rt2", [128, 48], F32)
        rt3 = mk.sb("rt3", [128, 48], F32)
        kd = mk.sb("kd", [128, 4 * 128], BF16)
        kvb = mk.sb("kvb", [128, 256], BF16)
        ptr = [mk.ps(f"ptr{i}", [128, 512], BF16) for i in range(2)]
        pz = [mk.ps(f"pz{i}", [128, 512], F32) for i in range(3)]
        pu = mk.ps("pu", [128, 512], F32)

        def rms_and_transpose(xb):
            mk.op("act", lambda e: e.activation(xsq[:], xb[:], AF.Square, accum_out=ssum[:]), [xb], [xsq, ssum])
            mk.op("act", lambda e: e.activation(rstd[:], ssum[:], AF.Sqrt, bias=eps_t[:, 0:1], scale=1.0 / D_MODEL),
                  [ssum, eps_t], [rstd])
            mk.op("dve", lambda e: e.reciprocal(rstd[:], rstd[:]), [rstd], [rstd])
            mk.op("dve", lambda e: e.tensor_scalar(xn[:], xb[:], rstd[:, 0:1], None, ALU.mult), [xb, rstd], [xn])
            for g4 in range(2):
                p = ptr[g4]
                for j in range(4):
                    dt_ = g4 * 4 + j
                    mk.op("pe", lambda e, p=p, j=j, dt_=dt_: e.transpose(
                        p[:, j * 128:(j + 1) * 128], xn[:, dt_ * 128:(dt_ + 1) * 128], ident_b[:]),
                        [xn, ident_b], [p])
                if g4 == 0:
                    mk.op("act", lambda e, p=p: e.copy(xnT[:, 0:512], p[:]), [p], [xnT])
                else:
                    mk.op("dve", lambda e, p=p: e.tensor_copy(xnT[:, 512:1024], p[:]), [p], [xnT])

        for ti in range(n_tiles + 1):
            xb, zt, cst = xt[ti % 2], zA[ti % 2], cs[ti % 2]
            r0 = ti * 128
            smp = ti == n_tiles
            own = ti >= 16 and not smp
            if smp:
                mk.dma("sp", xb[:], x_smp[:, :], writes=[xb])
                mk.dma("sp", cst[:], cs_smp[:, :], writes=[cst])
            else:
                mk.dma("sp", xb[:], x_all[r0:r0 + 128, :], writes=[xb])
                mk.dma("sp", cst[:], cs_all[r0:r0 + 128, :], writes=[cst])
            rms_and_transpose(xb)
            for ci, (c0, cw, z0) in enumerate(COLS_A):
                p = pz[ci % 3]
                for dt_ in range(8):
                    mk.op("pe", lambda e, p=p, dt_=dt_, z0=z0, cw=cw: e.matmul(
                        p[:, 0:cw], xnT[:, dt_ * 128:(dt_ + 1) * 128],
                        wbA[:, dt_ * 1280 + z0: dt_ * 1280 + z0 + cw],
                        start=(dt_ == 0), stop=(dt_ == 7)), [xnT, wbA], [p])
                if ci % 2 == 0:
                    mk.op("act", lambda e, p=p, z0=z0, cw=cw: e.copy(zt[:, z0:z0 + cw], p[:, 0:cw]), [p], [zt])
                else:
                    mk.op("dve", lambda e, p=p, z0=z0, cw=cw: e.tensor_copy(zt[:, z0:z0 + cw], p[:, 0:cw]), [p], [zt])
            kvw = zt[:, 512:1280].rearrange("p (a k g d) -> p a k g d", a=3, k=2, g=2)
            kv_ = kvw[:, :, 0, :, :]
            sqk = sq[:, :].rearrange("p (a g d) -> p a g d", a=3, g=2)
            mk.op("act", lambda e: e.activation(sqk, kv_, AF.Square), [zt], [sq])
            mk.op("dve", lambda e: e.tensor_reduce(
                hss[:, :], sq[:, :].rearrange("p (h d) -> p h d", d=64), AX.X, ALU.add), [sq], [hss])
            mk.op("act", lambda e: e.activation(hrs[:], hss[:], AF.Sqrt, bias=eps_t[:, 0:1], scale=1.0 / 64),
                  [hss, eps_t], [hrs])
            mk.op("dve", lambda e: e.reciprocal(hrs[:], hrs[:]), [hrs], [hrs])
            mk.op("dve", lambda e: e.tensor_tensor(
                kv_, kv_, hrs[:, :].rearrange("p (a g) -> p a g", g=2).unsqueeze(3).to_broadcast([128, 3, 2, 64]),
                ALU.mult), [zt, hrs], [zt])
            mk.op("dve", lambda e: e.tensor_tensor(
                kv_, kv_, kw[:, :].rearrange("p (a d) -> p a d", d=64).unsqueeze(2).to_broadcast([128, 3, 2, 64]),
                ALU.mult), [zt, kw], [zt])
            cosk = cst[:, 0:8].unsqueeze(1).unsqueeze(1).to_broadcast([128, 3, 2, 8])
            sink = cst[:, 8:16].unsqueeze(1).unsqueeze(1).to_broadcast([128, 3, 2, 8])
            a1 = rt1[:, :].rearrange("p (a g d) -> p a g d", a=3, g=2)
            a2 = rt2[:, :].rearrange("p (a g d) -> p a g d", a=3, g=2)
            a3 = rt3[:, :].rearrange("p (a g d) -> p a g d", a=3, g=2)
            x1 = kv_[..., 0:8]
            x2 = kv_[..., 8:16]
            mk.op("dve", lambda e: e.tensor_tensor(a1, x2, sink, ALU.mult), [zt, cst], [rt1])
            mk.op("dve", lambda e: e.tensor_tensor(a2, x1, sink, ALU.mult), [zt, cst], [rt2])
            mk.op("dve", lambda e: e.tensor_tensor(a3, x1, cosk, ALU.mult), [zt, cst], [rt3])
            mk.op("dve", lambda e: e.tensor_tensor(x1, a3, a1, ALU.subtract), [rt3, rt1], [zt])
            mk.op("dve", lambda e: e.tensor_tensor(a3, x2, cosk, ALU.mult), [zt, cst], [rt3])
            mk.op("dve", lambda e: e.tensor_tensor(x2, a3, a2, ALU.add), [rt3, rt2], [zt])
            if own:
                o0 = (ti - 16) * 128
                mk.dma("sp", kv_o[o0:o0 + 128, :], zt[:, 512:1024], reads=[zt])
                if ti >= 28:
                    w0 = (ti - 28) * 128
                    mk.dma("sp", win_o[w0:w0 + 128, :], zt[:, 1024:1280], reads=[zt])
            if smp:
                mk.dma("sp", kvs_o[:, :], zt[0:4, 512:1024], reads=[zt])
                mk.op("act", lambda e: e.copy(snew[:], zt[:, 512:1280]), [zt], [snew])
            for G in range(4):
                mk.op("pe", lambda e, G=G: e.transpose(pu[:, G * 128:(G + 1) * 128], zt[:, G * 128:(G + 1) * 128], ident_f[:]),
                      [zt, ident_f], [pu])
            mk.op("act", lambda e, r0=r0: e.copy(
                uT[:, :].rearrange("p (G t) -> p G t", G=4)[:, :, r0:r0 + 128],
                pu[:, :].rearrange("p (G t) -> p G t", G=4)), [pu], [uT])
            if smp:
                continue
            mk.op("pool", lambda e: e.tensor_copy(kvb[:], zt[:, 512:768]), [zt], [kvb])
            mk.op("pool", lambda e: e.tensor_copy(
                kd[:, 0:256].rearrange("p (g r d) -> p g r d", g=2, r=2),
                zt[:, 768:896].rearrange("p (g d) -> p g d", g=2).unsqueeze(2).to_broadcast([128, 2, 2, 64])), [zt], [kd])
            mk.op("pool", lambda e: e.tensor_copy(
                kd[:, 256:512].rearrange("p (g r d) -> p g r d", g=2, r=2),
                zt[:, 1024:1152].rearrange("p (g d) -> p g d", g=2).unsqueeze(2).to_broadcast([128, 2, 2, 64])), [zt], [kd])
            p = ptr[0]
            for j in range(2):
                mk.op("pe", lambda e, j=j, p=p: e.transpose(p[:, j * 128:(j + 1) * 128], kvb[:, j * 128:(j + 1) * 128], ident_b[:]),
                      [kvb, ident_b], [p])
            mk.op("dve", lambda e, p=p, r0=r0: e.tensor_copy(kcT[:, r0:r0 + 128], p[:, 0:128]), [p], [kcT])
            mk.op("dve", lambda e, p=p, r0=r0: e.tensor_copy(vcT[:, r0:r0 + 128], p[:, 128:256]), [p], [vcT])
            p = ptr[1]
            for j in range(4):
                mk.op("pe", lambda e, j=j, p=p: e.transpose(p[:, j * 128:(j + 1) * 128], kd[:, j * 128:(j + 1) * 128], ident_b[:]),
                      [kd, ident_b], [p])
            mk.op("act", lambda e, p=p, r0=r0: e.copy(
                KT[:, :].rearrange("p (j t) -> p j t", j=4)[:, :, r0:r0 + 128],
                p[:, :].rearrange("p (j t) -> p j t", j=4)), [p], [KT])
            mk.op("pool", lambda e, ti=ti: e.tensor_copy(
                Vs[:, ti * 144:(ti + 1) * 144].rearrange("p (g d) -> p g d", g=2)[:, :, 0:64],
                zt[:, 896:1024].rearrange("p (g d) -> p g d", g=2)), [zt], [Vs])
            mk.op("pool", lambda e, ti=ti: e.tensor_copy(
                Vw[:, ti * 144:(ti + 1) * 144].rearrange("p (g d) -> p g d", g=2)[:, :, 0:64],
                zt[:, 1152:1280].rearrange("p (g d) -> p g d", g=2)), [zt], [Vw])

        mk.barrier()
        esA.close()

        if do_ssm:
            esS = ExitStack()
            mk.es = esS
            P16 = [128, 16]
            lr = mk.sb("lr", P16); li = mk.sb("li", P16); ls = mk.sb("ls", P16)
            mk.dma("sp", lr[:], lam_re.rearrange("(k p) -> p k", p=128), writes=[lr])
            mk.dma("sp", li[:], lam_im.rearrange("(k p) -> p k", p=128), writes=[li])
            lsv = log_step.rearrange("(k g) -> g k", g=2)
            mk.dma("sp", ls[0:64, :], lsv[0:1, :].partition_broadcast(64) if False else lsv[0, :].partition_broadcast(64), writes=[ls])
            mk.dma("sp", ls[64:128, :], lsv[1, :].partition_broadcast(64), writes=[ls])
            dtt = mk.sb("dtt", P16); rho = mk.sb("rho", P16); s2 = mk.sb("s2", P16)
            tmpa = mk.sb("tmpa", P16); tmpb = mk.sb("tmpb", P16); tmpi = mk.sb("tmpi", P16, I32)
            sina = mk.sb("sina", P16); cosa = mk.sb("cosa", P16); sred = mk.sb("sred", P16)
            are = mk.sb("are", P16); aim = mk.sb("aim", P16); fre = mk.sb("fre", P16); fim = mk.sb("fim", P16)
            mk.op("act", lambda e: e.activation(dtt[:], ls[:], AF.Exp), [ls], [dtt])
            mk.op("dve", lambda e: e.tensor_tensor(tmpa[:], lr[:], dtt[:], ALU.mult), [lr, dtt], [tmpa])
            mk.op("act", lambda e: e.activation(rho[:], tmpa[:], AF.Exp), [tmpa], [rho])
            mk.op("dve", lambda e: e.scalar_tensor_tensor(s2[:], li[:], 1.0 / TWO_PI, dtt[:], ALU.mult, ALU.mult),
                  [li, dtt], [s2])
            mk.op("dve", lambda e: e.tensor_copy(tmpi[:], s2[:]), [s2], [tmpi])
            mk.op("dve", lambda e: e.tensor_tensor(sred[:], s2[:], tmpi[:], ALU.subtract), [s2, tmpi], [sred])
            mk.op("act", lambda e: e.activation(sina[:], sred[:], AF.Sin, scale=TWO_PI), [sred], [sina])
            mk.op("dve", lambda e: e.tensor_scalar(tmpa[:], s2[:], 0.25, None, ALU.add), [s2], [tmpa])
            mk.op("dve", lambda e: e.tensor_copy(tmpi[:], tmpa[:]), [tmpa], [tmpi])
            mk.op("dve", lambda e: e.tensor_tensor(tmpb[:], s2[:], tmpi[:], ALU.subtract), [s2, tmpi], [tmpb])
            mk.op("act", lambda e: e.activation(cosa[:], tmpb[:], AF.Sin, bias=hpi_t[:, 0:1], scale=TWO_PI),
                  [tmpb, hpi_t], [cosa])
            mk.op("dve", lambda e: e.tensor_tensor(are[:], rho[:], cosa[:], ALU.mult), [rho, cosa], [are])
            mk.op("dve", lambda e: e.tensor_tensor(aim[:], rho[:], sina[:], ALU.mult), [rho, sina], [aim])
            den = mk.sb("den", P16); nr = mk.sb("nr", P16)
            mk.op("dve", lambda e: e.tensor_tensor(den[:], lr[:], lr[:], ALU.mult), [lr], [den])
            mk.op("dve", lambda e: e.tensor_tensor(tmpa[:], li[:], li[:], ALU.mult), [li], [tmpa])
            mk.op("dve", lambda e: e.tensor_tensor(den[:], den[:], tmpa[:], ALU.add), [den, tmpa], [den])
            mk.op("dve", lambda e: e.reciprocal(den[:], den[:]), [den], [den])
            mk.op("dve", lambda e: e.tensor_scalar(nr[:], are[:], -1.0, None, ALU.add), [are], [nr])
            mk.op("dve", lambda e: e.tensor_tensor(tmpa[:], nr[:], lr[:], ALU.mult), [nr, lr], [tmpa])
            mk.op("dve", lambda e: e.tensor_tensor(tmpb[:], aim[:], li[:], ALU.mult), [aim, li], [tmpb])
            mk.op("dve", lambda e: e.tensor_tensor(tmpa[:], tmpa[:], tmpb[:], ALU.add), [tmpa, tmpb], [tmpa])
            mk.op("dve", lambda e: e.tensor_tensor(fre[:], tmpa[:], den[:], ALU.mult), [tmpa, den], [fre])
            mk.op("dve", lambda e: e.tensor_tensor(tmpa[:], aim[:], lr[:], ALU.mult), [aim, lr], [tmpa])
            mk.op("dve", lambda e: e.tensor_tensor(tmpb[:], nr[:], li[:], ALU.mult), [nr, li], [tmpb])
            mk.op("dve", lambda e: e.tensor_tensor(tmpa[:], tmpa[:], tmpb[:], ALU.subtract), [tmpa, tmpb], [tmpa])
            mk.op("dve", lambda e: e.tensor_tensor(fim[:], tmpa[:], den[:], ALU.mult), [tmpa, den], [fim])
            LB = mk.sb("LB", [128, 32 * 128], BF16)
            LC = mk.sb("LC", [128, 32 * 128], BF16)
            pS = [mk.ps(f"pS{i}", [128, 512], F32) for i in range(6)]
            esP = ExitStack()
            mk.es = esP
            bre = mk.sb("bre", [128, 256]); bim = mk.sb("bim", [128, 256])
            bbr = mk.sb("bbr", [128, 256]); bbi = mk.sb("bbi", [128, 256]); bt = mk.sb("bt", [128, 256])
            mk.dma("sp", bre[:, :].rearrange("p (k c) -> p k c", c=16), b_re.rearrange("(k p) c -> p k c", p=128), writes=[bre])
            mk.dma("sp", bim[:, :].rearrange("p (k c) -> p k c", c=16), b_im.rearrange("(k p) c -> p k c", p=128), writes=[bim])
            v3 = lambda b: b[:, :].rearrange("p (k c) -> p k c", c=16)
            fb = lambda b: b[:, :].unsqueeze(2).to_broadcast([128, 16, 16])
            mk.op("dve", lambda e: e.tensor_tensor(v3(bbr), v3(bre), fb(fre), ALU.mult), [bre, fre], [bbr])
            mk.op("dve", lambda e: e.tensor_tensor(v3(bt), v3(bim), fb(fim), ALU.mult), [bim, fim], [bt])
            mk.op("dve", lambda e: e.tensor_tensor(bbr[:], bbr[:], bt[:], ALU.subtract), [bbr, bt], [bbr])
            mk.op("dve", lambda e: e.tensor_tensor(v3(bbi), v3(bim), fb(fre), ALU.mult), [bim, fre], [bbi])
            mk.op("dve", lambda e: e.tensor_tensor(v3(bt), v3(bre), fb(fim), ALU.mult), [bre, fim], [bt])
            mk.op("dve", lambda e: e.tensor_tensor(bbi[:], bbi[:], bt[:], ALU.add), [bbi, bt], [bbi])
            Mz = mk.sb("Mz", [128, 32 * 128], F32)
            mk.op("pool", lambda e: e.memset(Mz[:], 0.0), [], [Mz])
            mk.op("pool", lambda e: e.memset(LC[:], 0.0), [], [LC])
            for k in range(16):
                for part, bb in ((0, bbr), (1, bbi)):
                    idx = k * 2 + part
                    for gl in range(2):
                        gp = (2 * k + gl) % 8
                        mk.op("dve", lambda e, idx=idx, gl=gl, gp=gp, bb=bb, k=k: e.tensor_copy(
                            Mz[gl * 64:(gl + 1) * 64, idx * 128 + gp * 16: idx * 128 + gp * 16 + 16],
                            bb[gl * 64:(gl + 1) * 64, k * 16:(k + 1) * 16]), [bb], [Mz])
            for i4 in range(8):
                p = pS[i4 % 2]
                for j in range(4):
                    idx = i4 * 4 + j
                    mk.op("pe", lambda e, p=p, j=j, idx=idx: e.transpose(
                        p[:, j * 128:(j + 1) * 128], Mz[:, idx * 128:(idx + 1) * 128], ident_f[:]), [Mz, ident_f], [p])
                mk.op("act", lambda e, p=p, i4=i4: e.copy(LB[:, i4 * 512:(i4 + 1) * 512], p[:]), [p], [LB])
            XG = [mk.sb(f"XG{i}", [128, 128], F32) for i in range(2)]
            for G in range(4):
                for part, csrc in ((0, c_re), (1, c_im)):
                    X = XG[part]
                    mk.op("pool", lambda e, X=X: e.memset(X[:], 0.0), [], [X])
                    for g8 in range(8):
                        g = G * 8 + g8
                        mk.dma("sp", X[g8 * 16:(g8 + 1) * 16, (g % 2) * 64:(g % 2) * 64 + 64], csrc[g, :, :], writes=[X])
                    p = pS[2 + part]
                    mk.op("pe", lambda e, p=p, X=X: e.transpose(p[:, 0:128], X[:], ident_f[:]), [X, ident_f], [p])
                    for k4 in range(4):
                        idx = (G * 4 + k4) * 2 + part
                        mk.op("dve", lambda e, p=p, idx=idx, k4=k4, part=part: e.tensor_scalar(
                            LC[:, idx * 128 + k4 * 32: idx * 128 + k4 * 32 + 32], p[:, k4 * 32:k4 * 32 + 32],
                            (1.0 if part == 0 else -1.0), None, ALU.mult), [p], [LC])
            mk.barrier()
            esP.close()
            mk.es = esS
            dcol = mk.sb("dcol", [128, 4])
            mk.dma("sp", dcol[:], ssm_d.rearrange("(G p) -> p G", p=128), writes=[dcol])
            qtr = mk.sb("qtr", [128, 1])
            mk.op("dve", lambda e: e.memset(qtr[:], 0.25), [], [qtr])

            h0r = mk.sb("h0r", [128, 64]); h0i = mk.sb("h0i", [128, 64])
            for si in range(4):
                mk.dma("sp", h0r[:, :].rearrange("p (k s) -> p k s", s=4)[:, :, si], st_re[si].rearrange("(k p) -> p k", p=128), writes=[h0r])
                mk.dma("sp", h0i[:, :].rearrange("p (k s) -> p k s", s=4)[:, :, si], st_im[si].rearrange("(k p) -> p k", p=128), writes=[h0i])
            h1r = mk.sb("h1r", [128, 64]); h1i = mk.sb("h1i", [128, 64])
            h1rb = mk.sb("h1rb", [128, 64], BF16); h1ib = mk.sb("h1ib", [128, 64], BF16)
            t4a = mk.sb("t4a", [128, 4]); t4b = mk.sb("t4b", [128, 4])
            for k in range(16):
                G = k // 4
                pr, pi_ = pS[0], pS[1]
                ucs = uT[:, G * UTW + SEQ: G * UTW + SEQ + 4]
                mk.op("pe", lambda e, k=k, ucs=ucs: e.matmul(pr[:, 0:4], LB[:, (2 * k) * 128:(2 * k + 1) * 128], ucs, start=True, stop=True), [LB, uT], [pr])
                mk.op("pe", lambda e, k=k, ucs=ucs: e.matmul(pi_[:, 0:4], LB[:, (2 * k + 1) * 128:(2 * k + 2) * 128], ucs, start=True, stop=True), [LB, uT], [pi_])
                ks = slice(k * 4, k * 4 + 4)
                ar = are[:, k:k + 1]; ai = aim[:, k:k + 1]
                mk.op("dve", lambda e, ks=ks, ar=ar: e.tensor_scalar(t4a[:], h0r[:, ks], ar, None, ALU.mult), [h0r, are], [t4a])
                mk.op("dve", lambda e, ks=ks, ai=ai: e.tensor_scalar(t4b[:], h0i[:, ks], ai, None, ALU.mult), [h0i, aim], [t4b])
                mk.op("dve", lambda e: e.tensor_tensor(t4a[:], t4a[:], t4b[:], ALU.subtract), [t4a, t4b], [t4a])
                mk.op("dve", lambda e, ks=ks, pr=pr: e.tensor_tensor(h1r[:, ks], t4a[:], pr[:, 0:4], ALU.add), [t4a, pr], [h1r])
                mk.op("dve", lambda e, ks=ks, ar=ar: e.tensor_scalar(t4a[:], h0i[:, ks], ar, None, ALU.mult), [h0i, are], [t4a])
                mk.op("dve", lambda e, ks=ks, ai=ai: e.tensor_scalar(t4b[:], h0r[:, ks], ai, None, ALU.mult), [h0r, aim], [t4b])
                mk.op("dve", lambda e: e.tensor_tensor(t4a[:], t4a[:], t4b[:], ALU.add), [t4a, t4b], [t4a])
                mk.op("dve", lambda e, ks=ks, pi_=pi_: e.tensor_tensor(h1i[:, ks], t4a[:], pi_[:, 0:4], ALU.add), [t4a, pi_], [h1i])
            mk.op("dve", lambda e: e.tensor_copy(h1rb[:], h1r[:]), [h1r], [h1rb])
            mk.op("dve", lambda e: e.tensor_copy(h1ib[:], h1i[:]), [h1i], [h1ib])
            for si in range(4):
                mk.dma("sp", sres_o[si].rearrange("(k p) -> p k", p=128), h1r[:, :].rearrange("p (k s) -> p k s", s=4)[:, :, si], reads=[h1r])
                mk.dma("sp", sims_o[si].rearrange("(k p) -> p k", p=128), h1i[:, :].rearrange("p (k s) -> p k s", s=4)[:, :, si], reads=[h1i])
            for G in range(4):
                py = pS[4]
                for k4 in range(4):
                    k = G * 4 + k4
                    mk.op("pe", lambda e, k=k, k4=k4: e.matmul(py[:, 0:4], LC[:, (2 * k) * 128:(2 * k + 1) * 128], h1rb[:, k * 4:k * 4 + 4],
                                                           start=(k4 == 0), stop=False), [LC, h1rb], [py])
                    mk.op("pe", lambda e, k=k, k4=k4: e.matmul(py[:, 0:4], LC[:, (2 * k + 1) * 128:(2 * k + 2) * 128], h1ib[:, k * 4:k * 4 + 4],
                                                           start=False, stop=(k4 == 3)), [LC, h1ib], [py])
                mk.op("dve", lambda e, G=G: e.scalar_tensor_tensor(
                    yTs[:, G * 128: G * 128 + 4], uT[:, G * UTW + SEQ: G * UTW + SEQ + 4], dcol[:, G:G + 1], py[:, 0:4], ALU.mult, ALU.add),
                    [uT, dcol, py], [yTs])
            carry = mk.sb("carry", [128, 32])
            mk.op("dve", lambda e: e.memset(carry[:], 0.0), [], [carry])
            fin = mk.sb("fin", [128, 32])
            NB_ = 2
            xr = [mk.sb(f"xr{i}", [128, CH]) for i in range(NB_)]
            xi = [mk.sb(f"xi{i}", [128, CH]) for i in range(NB_)]
            ki = [mk.sb(f"ki{i}", [128, CH], I32) for i in range(NB_)]
            fs = [mk.sb(f"fs{i}", [128, CH]) for i in range(NB_)]
            sn = [mk.sb(f"sn{i}", [128, CH]) for i in range(NB_)]
            cn = [mk.sb(f"cn{i}", [128, CH]) for i in range(NB_)]
            ta = [mk.sb(f"ta{i}", [128, CH]) for i in range(NB_)]
            tb = [mk.sb(f"tb{i}", [128, CH]) for i in range(NB_)]
            gr = [mk.sb(f"gr{i}", [128, CH]) for i in range(NB_)]
            gi = [mk.sb(f"gi{i}", [128, CH]) for i in range(NB_)]
            hR = mk.sb("hR", [128, 4 * CH], BF16)
            hI = mk.sb("hI", [128, 4 * CH], BF16)
            n_chunks = SEQ // CH
            units = [(c, G, k4) for c in range(n_chunks) for G in range(4) for k4 in range(4)]

            def stage1(ui):
                c, G, k4 = units[ui]
                t0 = c * CH
                k = G * 4 + k4
                u_ = ui % NB_
                pr, pi_ = pS[(ui % 2) * 2], pS[(ui % 2) * 2 + 1]
                ucols = uT[:, G * UTW + t0: G * UTW + t0 + CH]
                mk.op("pe", lambda e: e.matmul(pr[:, 0:CH], LB[:, (2 * k) * 128:(2 * k + 1) * 128], ucols, start=True, stop=True), [LB, uT], [pr])
                mk.op("pe", lambda e: e.matmul(pi_[:, 0:CH], LB[:, (2 * k + 1) * 128:(2 * k + 2) * 128], ucols, start=True, stop=True), [LB, uT], [pi_])
                XR, XI, KI, FS, SN, CN, TA, TB = xr[u_], xi[u_], ki[u_], fs[u_], sn[u_], cn[u_], ta[u_], tb[u_]
                mk.op("act", lambda e: e.copy(XR[:], pr[:, 0:CH]), [pr], [XR])
                mk.op("act", lambda e: e.copy(XI[:], pi_[:, 0:CH]), [pi_], [XI])
                tvc = tv[:, t0:t0 + CH]
                sc = sred[:, k:k + 1]
                mk.op("dve", lambda e: e.tensor_scalar(KI[:], tvc, sc, None, ALU.mult), [tv, sred], [KI])
                mk.op("dve", lambda e: e.scalar_tensor_tensor(FS[:], tvc, sc, KI[:], ALU.mult, ALU.subtract), [tv, sred, KI], [FS])
                mk.op("act", lambda e: e.activation(SN[:], FS[:], AF.Sin, scale=TWO_PI), [FS], [SN])
                mk.op("dve", lambda e: e.tensor_scalar(KI[:], tvc, sc, qtr[:, 0:1], ALU.mult, ALU.add), [tv, sred, qtr], [KI])
                mk.op("dve", lambda e: e.scalar_tensor_tensor(FS[:], tvc, sc, KI[:], ALU.mult, ALU.subtract), [tv, sred, KI], [FS])
                mk.op("act", lambda e: e.activation(CN[:], FS[:], AF.Sin, bias=hpi_t[:, 0:1], scale=TWO_PI), [FS, hpi_t], [CN])
                ER = "pool"
                mk.op(ER, lambda e: e.tensor_tensor(TA[:], CN[:], XR[:], ALU.mult), [CN, XR], [TA])
                mk.op(ER, lambda e: e.tensor_tensor(TB[:], SN[:], XI[:], ALU.mult), [SN, XI], [TB])
                mk.op(ER, lambda e: e.tensor_tensor(TA[:], TA[:], TB[:], ALU.add), [TA, TB], [TA])
                mk.op(ER, lambda e: e.tensor_tensor(TB[:], CN[:], XI[:], ALU.mult), [CN, XI], [TB])
                mk.op(ER, lambda e: e.tensor_tensor(XI[:], SN[:], XR[:], ALU.mult), [SN, XR], [XI])
                mk.op(ER, lambda e: e.tensor_tensor(TB[:], TB[:], XI[:], ALU.subtract), [TB, XI], [TB])

            def stage2(ui):
                c, G, k4 = units[ui]
                t0 = c * CH
                own = t0 >= HALF
                last = c == n_chunks - 1
                k = G * 4 + k4
                u_ = ui % NB_
                E = "dve"
                XR, XI, SN, CN, TA, TB, GR, GI = xr[u_], xi[u_], sn[u_], cn[u_], ta[u_], tb[u_], gr[u_], gi[u_]
                rb = rho[:, k:k + 1].to_broadcast([128, CH])
                mk.op(E, lambda e: e.tensor_tensor_scan(GR[:], rb, TA[:], carry[:, k:k + 1], ALU.mult, ALU.add), [rho, TA, carry], [GR])
                mk.op(E, lambda e: e.tensor_tensor_scan(GI[:], rb, TB[:], carry[:, 16 + k:17 + k], ALU.mult, ALU.add), [rho, TB, carry], [GI])
                mk.op("act", lambda e: e.copy(carry[:, k:k + 1], GR[:, CH - 1:CH]), [GR], [carry])
                mk.op("act", lambda e: e.copy(carry[:, 16 + k:17 + k], GI[:, CH - 1:CH]), [GI], [carry])
                if own:
                    hr_ = hR[:, k4 * CH:(k4 + 1) * CH]
                    hi_ = hI[:, k4 * CH:(k4 + 1) * CH]
                    mk.op(E, lambda e: e.tensor_tensor(XR[:], CN[:], GR[:], ALU.mult), [CN, GR], [XR])
                    mk.op(E, lambda e: e.tensor_tensor(XI[:], SN[:], GI[:], ALU.mult), [SN, GI], [XI])
                    mk.op(E, lambda e: e.tensor_tensor(hr_, XR[:], XI[:], ALU.subtract), [XR, XI], [hR])
                    mk.op(E, lambda e: e.tensor_tensor(XR[:], CN[:], GI[:], ALU.mult), [CN, GI], [XR])
                    mk.op(E, lambda e: e.tensor_tensor(XI[:], SN[:], GR[:], ALU.mult), [SN, GR], [XI])
                    mk.op(E, lambda e: e.tensor_tensor(hi_, XR[:], XI[:], ALU.add), [XR, XI], [hI])
                    if last:
                        Lc = CH - 1
                        mk.op(E, lambda e: e.tensor_tensor(XR[:, 0:1], CN[:, Lc:Lc + 1], GR[:, Lc:Lc + 1], ALU.mult), [CN, GR], [XR])
                        mk.op(E, lambda e: e.tensor_tensor(XI[:, 0:1], SN[:, Lc:Lc + 1], GI[:, Lc:Lc + 1], ALU.mult), [SN, GI], [XI])
                        mk.op(E, lambda e: e.tensor_tensor(fin[:, k:k + 1], XR[:, 0:1], XI[:, 0:1], ALU.subtract), [XR, XI], [fin])
                        mk.op(E, lambda e: e.tensor_tensor(XR[:, 0:1], CN[:, Lc:Lc + 1], GI[:, Lc:Lc + 1], ALU.mult), [CN, GI], [XR])
                        mk.op(E, lambda e: e.tensor_tensor(XI[:, 0:1], SN[:, Lc:Lc + 1], GR[:, Lc:Lc + 1], ALU.mult), [SN, GR], [XI])
                        mk.op(E, lambda e: e.tensor_tensor(fin[:, 16 + k:17 + k], XR[:, 0:1], XI[:, 0:1], ALU.add), [XR, XI], [fin])
                    if k4 == 3:
                        py = pS[4 + (G % 2)]
                        for kk in range(4):
                            kq = G * 4 + kk
                            mk.op("pe", lambda e, kq=kq, kk=kk: e.matmul(
                                py[:, 0:CH], LC[:, (2 * kq) * 128:(2 * kq + 1) * 128], hR[:, kk * CH:(kk + 1) * CH],
                                start=(kk == 0), stop=False), [LC, hR], [py])
                            mk.op("pe", lambda e, kq=kq, kk=kk: e.matmul(
                                py[:, 0:CH], LC[:, (2 * kq + 1) * 128:(2 * kq + 2) * 128], hI[:, kk * CH:(kk + 1) * CH],
                                start=False, stop=(kk == 3)), [LC, hI], [py])
                        o0 = t0 - HALF
                        mk.op("dve", lambda e: e.scalar_tensor_tensor(
                            yT[:, G * HALF + o0: G * HALF + o0 + CH], uT[:, G * UTW + t0: G * UTW + t0 + CH],
                            dcol[:, G:G + 1], py[:, 0:CH], ALU.mult, ALU.add), [uT, dcol, py], [yT])

            stage1(0)
            for ui in range(len(units)):
                if ui + 1 < len(units):
                    stage1(ui + 1)
                stage2(ui)
            mk.dma("sp", sre_o.rearrange("(k p) -> p k", p=128), fin[:, 0:16], reads=[fin])
            mk.dma("sp", sim_o.rearrange("(k p) -> p k", p=128), fin[:, 16:32], reads=[fin])
            mk.barrier()
            esS.close()
        esU.close()
        mk.es = esPr
        if do_nsa:
            nsa_and_mixer(locals())
        mk.barrier()
        esPr.close()
        mk.es = es
        if do_smp:
            esSm = ExitStack()
            sample_phase(locals())
            mk.barrier()
            esSm.close()
            mk.es = es

        mk.finish("sp")
        print(f"[build] instructions ~{mk.n_inst}, sems {mk.nsem}")
    return nc


def _rope_tables(pos):
    half = 8
    inv = (500000.0 ** (-np.arange(half, dtype=np.float32) / half)).astype(np.float32)
    ang = pos.astype(np.float32)[:, None] * inv[None, :]
    return np.concatenate([np.cos(ang), np.sin(ang)], axis=1).astype(np.float32)


def kernel(**inputs):
    f = lambda k: np.ascontiguousarray(np.asarray(inputs[k], dtype=np.float32)[0])
    x_prompt = np.asarray(inputs["x_prompt"], dtype=np.float32)
    cs = _rope_tables(np.arange(SEQ))
    cs_smp = np.ascontiguousarray(np.repeat(_rope_tables(np.array([8192])), 128, axis=0))
    x_sample = np.asarray(inputs["x_sample"], dtype=np.float32)
    st_re_all = np.asarray(inputs["state_ssm_re"], dtype=np.float32)[0]
    st_im_all = np.asarray(inputs["state_ssm_im"], dtype=np.float32)[0]
    common = {
        "w_in": f("w_in"), "norm_w": f("norm_w"), "q_norm_w": f("q_norm_w"), "k_norm_w": f("k_norm_w"),
        "ident": np.eye(128, dtype=np.float32), "tvals": np.arange(SEQ, dtype=np.float32),
        "lam_re": f("ssm_lam_re").reshape(2048), "lam_im": f("ssm_lam_im").reshape(2048),
        "log_step": f("ssm_log_step"), "b_re": f("ssm_b_re").reshape(2048, 16), "b_im": f("ssm_b_im").reshape(2048, 16),
        "c_re": f("ssm_c_re"), "c_im": f("ssm_c_im"), "ssm_d": f("ssm_d"),
    }
    f2 = lambda k: np.ascontiguousarray(np.asarray(inputs[k], dtype=np.float32)[0])
    common.update({
        "gate_b": f2("gate_b").reshape(24), "cmp_pe": f2("cmp_pe"), "cmp_w1": f2("cmp_w1"), "cmp_b1": f2("cmp_b1"),
        "cmp_w2": f2("cmp_w2"), "w_glu": f2("w_glu"), "w_out": f2("w_out"),
    })
    cidx = np.arange(256); nidx = np.arange(64)
    ov = ((16 * cidx[:, None] < 64 * (nidx[None, :] + 1)) & (16 * cidx[:, None] + 32 > 64 * nidx[None, :])).astype(np.float32)
    pidx = np.arange(128)
    caus = np.where(pidx[:, None] > pidx[None, :], -30000.0, 0.0).astype(np.float32)
    winlo = np.where(pidx[:, None] <= pidx[None, :], -30000.0, 0.0).astype(np.float32)
    eall = np.zeros((64, 32, 128), np.float32)
    for tk in range(32):
        for p in range(128):
            eall[2 * tk + p // 64, tk, p] = 30000.0
    common.update({"ov_tab": ov, "caus": caus, "winlo": winlo, "eall": eall.reshape(64, 32 * 128)})
    cache_rows = np.asarray(inputs["cache_kv"], dtype=np.float32)[0].reshape(2560 * 128, 512)
    cache_win = np.asarray(inputs["cache_win"], dtype=np.float32)[0].reshape(32, 512, 256)
    ptab_all = np.asarray(inputs["page_table"], dtype=np.int32)
    pp_ = np.arange(128)
    cval = np.stack([((ct * 128 + pp_) <= 510).astype(np.float32) for ct in range(4)], axis=1)
    wcol = np.zeros((128, 5), np.float32)
    wcol[0, 0] = -30000.0
    wcol[1:, 4] = -30000.0
    newcol = np.full((128, 1), -30000.0, np.float32)
    newcol[0, 0] = 0.0
    nn_ = np.arange(130)
    fadds = np.where((nn_ == 0) | (nn_ == 127) | (nn_ == 128), 1e4 + nn_, 0.0)
    fadds[129] = -1e4 - 129
    sel2 = np.concatenate([(pp_ < 64).astype(np.float32), (pp_ >= 64).astype(np.float32)])[None, :]
    c512 = np.arange(512); n129 = np.arange(129)
    ovs = ((16 * c512[:, None] < 64 * (n129[None, :] + 1)) & (16 * c512[:, None] + 32 > 64 * n129[None, :])).astype(np.float32)
    common.update({"cache_rows": cache_rows, "pcol": pp_.astype(np.float32).reshape(128, 1), "cval": np.ascontiguousarray(cval),
                   "wcol": wcol, "newcol": newcol, "fadds": fadds.astype(np.float32).reshape(1, 130), "sel2": np.ascontiguousarray(sel2),
                   "ovs": ovs})
    nc = build_nc()
    in_maps = []
    for c in range(N_CORES):
        b, h = c // 2, c % 2
        if h == 1:
            xa = np.ascontiguousarray(x_prompt[b])
            csa = cs
        else:
            xa = np.concatenate([np.zeros((HALF, D_MODEL), np.float32), x_prompt[b, :HALF]], axis=0)
            csa = np.concatenate([cs[:HALF], cs[:HALF]], axis=0)
        m = dict(common)
        cmin = 0 if h == 1 else 128
        cc = np.arange(256)
        cth = np.where((cc >= cmin) & (cc < 255), 16.0 * cc + 31.0, 1e9).astype(np.float32)
        m["cthr"] = np.ascontiguousarray(cth.reshape(2, 128).T)
        nv0 = 0 if h == 1 else 32
        fa = np.zeros((128, 16, 64), np.float32)
        for i in range(16):
            vq = (16 + i) * 128 + np.arange(128)
            qb_ = vq // 64
            nn = np.arange(64)
            valid = (nn[None, :] <= qb_[:, None]) & (nn[None, :] >= nv0)
            forced = (nn[None, :] == nv0) | (nn[None, :] >= qb_[:, None] - 1)
            fa[:, i, :] = np.where(valid, np.where(forced, 1e4 + nn[None, :], 0.0), -1e4 - nn[None, :])
        m["fadd"] = fa.reshape(128, 16 * 64)
        m["pfxrow"] = np.full((1, 512), 0.0 if h == 1 else -30000.0, np.float32)
        xs_ = np.zeros((128, D_MODEL), np.float32)
        xs_[0:4] = x_sample[4 * c:4 * c + 4, 0]
        m["cache_win_s"] = np.ascontiguousarray(cache_win[4 * c:4 * c + 4])
        m["ptab"] = np.ascontiguousarray(ptab_all[4 * c:4 * c + 4])
        m["x_smp"] = xs_
        m["cs_smp"] = cs_smp
        m["st_re"] = np.ascontiguousarray(st_re_all[4 * c:4 * c + 4].reshape(4, 2048))
        m["st_im"] = np.ascontiguousarray(st_im_all[4 * c:4 * c + 4].reshape(4, 2048))
        m["x_all"] = xa
        m["cs_all"] = np.ascontiguousarray(csa)
        in_maps.append(m)
    res = run_bass_kernel_spmd(nc, in_maps, core_ids=list(range(N_CORES)))
    R = res.results
    y_prompt = np.zeros((4, SEQ, D_MODEL), np.float32)
    kv_prompt = np.zeros((1, 4, SEQ, 4, 2, 64), np.float32)
    win_prompt = np.zeros((1, 4, 512, 2, 2, 64), np.float32)
    sre = np.zeros((1, 4, 32, 64), np.float32)
    sim = np.zeros((1, 4, 32, 64), np.float32)
    y_sample = np.zeros((32, 1, D_MODEL), np.float32)
    kv_sample = np.zeros((1, 32, 1, 4, 2, 64), np.float32)
    win_sample = np.zeros((1, 32, 512, 2, 2, 64), np.float32)
    sre_s = np.zeros((1, 32, 32, 64), np.float32)
    sim_s = np.zeros((1, 32, 32, 64), np.float32)
    for c in range(N_CORES):
        b, h = c // 2, c % 2
        kv_sample[0, 4 * c:4 * c + 4, 0] = R[c]["kvs_o"].reshape(4, 4, 2, 64)
        y_sample[4 * c:4 * c + 4, 0] = R[c]["ys_o"]
        win_sample[0, 4 * c:4 * c + 4] = R[c]["wins_o"].reshape(4, 512, 2, 2, 64)
        sre_s[0, 4 * c:4 * c + 4] = R[c]["sres_o"].reshape(4, 32, 64)
        sim_s[0, 4 * c:4 * c + 4] = R[c]["sims_o"].reshape(4, 32, 64)
        y_prompt[b, h * HALF:(h + 1) * HALF] = R[c]["y_o"]
        kv_prompt[0, b, h * HALF:(h + 1) * HALF] = R[c]["kv_o"].reshape(HALF, 4, 2, 64)
        if h == 1:
            win_prompt[0, b] = R[c]["win_o"].reshape(512, 2, 2, 64)
            sre[0, b] = R[c]["sre_o"].reshape(32, 64)
            sim[0, b] = R[c]["sim_o"].reshape(32, 64)
    return (y_prompt, y_sample, kv_prompt, kv_sample, win_prompt, win_sample, sre, sim, sre_s, sim_s)
```
